# Optimizing a Trainium2 kernel written in Bass

```python
import math, functools
import jax
import jax.numpy as jnp
from jax import lax
import numpy as np

D_MODEL = 2048
BATCH = 1
SEQ = 8192
DEPTH = 2
DEC_BATCH = 32
DEC_SEQ = 1
PAST_LEN = 8192
PAGE_SIZE = 128

D_MIX = D_MODEL
SSM_DIM = D_MIX // 2
SSM_HEAD_DIM = 64
SSM_HEADS = SSM_DIM // SSM_HEAD_DIM
SSM_GROUPS = 4
SSM_STATE = 128
CONV_K = 4
CONV_CH = SSM_DIM + 2 * SSM_GROUPS * SSM_STATE
SSD_CHUNK = 128
NSA_DIM = D_MIX - SSM_DIM
HEAD_DIM = 64
ATT_HEADS = NSA_DIM // HEAD_DIM
KV_HEADS = 4
GQA = ATT_HEADS // KV_HEADS
CMP_STRIDE = 16
L_CMP = 2 * CMP_STRIDE
L_SEL = 64
N_SEL = 16
WINDOW = 512
Q_BLOCK = 128
KV_COLS = 2 * KV_HEADS * HEAD_DIM
IN_COLS = SSM_DIM + CONV_CH + SSM_HEADS + NSA_DIM + 3 * KV_COLS + 3 * ATT_HEADS
D_FF = -(-(8 * D_MODEL) // (3 * 256)) * 256
RMS_EPS = 1e-6
SEL_FORCE = 1e9

kernel_name = 'hymba_ssd_nsa_decode_step'


def rms_norm(x, g):
    xf = x.astype(jnp.float32)
    y = xf * lax.rsqrt(jnp.mean(xf * xf, axis=-1, keepdims=True) + RMS_EPS)
    return (y * g.astype(jnp.float32)).astype(x.dtype)


def masked_softmax(s, mask):
    s = jnp.where(mask, s, -jnp.inf)
    m = jnp.max(s, axis=-1, keepdims=True)
    m = jnp.where(jnp.isfinite(m), m, 0.0)
    e = jnp.where(mask, jnp.exp(s - m), 0.0)
    return e / jnp.maximum(jnp.sum(e, axis=-1, keepdims=True), 1e-30)


def split_proj(u):
    sizes = (SSM_DIM, CONV_CH, SSM_HEADS, NSA_DIM, KV_COLS, KV_COLS, KV_COLS, 3 * ATT_HEADS)
    return jnp.split(u, np.cumsum(sizes)[:-1].tolist(), axis=-1)


def norm_k(kv, g):
    return jnp.stack([rms_norm(kv[:, :, 0], g), kv[:, :, 1]], axis=2)


def ssd_scan(x, dt, a, bm, cm, h0):
    B, T, H, P = x.shape
    G, N = bm.shape[2:]
    Hg = H // G
    Q = SSD_CHUNK if T % SSD_CHUNK == 0 else T
    nc = T // Q
    xc = x.astype(jnp.float32).reshape(B, nc, Q, G, Hg, P)
    dtc = dt.reshape(B, nc, Q, G, Hg)
    bc = bm.astype(jnp.float32).reshape(B, nc, Q, G, N)
    cc = cm.astype(jnp.float32).reshape(B, nc, Q, G, N)
    acum = jnp.cumsum(dtc * a.reshape(G, Hg), axis=2)
    causal = jnp.tril(jnp.ones((Q, Q), dtype=bool))[:, :, None, None]
    seg = acum[:, :, :, None] - acum[:, :, None, :]
    decay = jnp.where(causal, jnp.exp(jnp.where(causal, seg, 0.0)), 0.0)
    cb = jnp.einsum('bcign,bcjgn->bcijg', cc, bc)
    y_diag = jnp.einsum('bcijg,bcijgh,bcjgh,bcjghp->bcighp', cb, decay, dtc, xc)
    decay_end = jnp.exp(acum[:, :, -1:] - acum)
    s_chunk = jnp.einsum('bcjgn,bcjgh,bcjghp->bcghpn', bc, dtc * decay_end, xc)
    chunk_decay = jnp.exp(acum[:, :, -1])

    def step(h, inp):
        s_c, d_c = inp
        return h * d_c[..., None, None] + s_c, h

    h_last, h_in = lax.scan(step, h0.astype(jnp.float32).reshape(B, G, Hg, P, N),
                            (jnp.moveaxis(s_chunk, 1, 0), jnp.moveaxis(chunk_decay, 1, 0)))
    h_in = jnp.moveaxis(h_in, 0, 1)
    y_off = jnp.einsum('bcign,bcghpn,bcigh->bcighp', cc, h_in, jnp.exp(acum))
    y = (y_diag + y_off).reshape(B, T, H, P)
    return y, h_last.reshape(B, H, P, N)


def ssd_mixer(z, xbc, dt_raw, conv_buf, h0, lp):
    B, T = z.shape[:2]
    xbc_full = jnp.concatenate([conv_buf.astype(xbc.dtype), xbc], axis=1)
    new_buf = xbc_full[:, -(CONV_K - 1):]
    conv = lax.conv_general_dilated(xbc_full, lp['conv_w'][:, None, :], window_strides=(1,),
                                    padding='VALID', dimension_numbers=('NWC', 'WIO', 'NWC'),
                                    feature_group_count=CONV_CH)
    xbc_act = jax.nn.silu(conv + lp['conv_b'])
    xs, bm, cm = jnp.split(xbc_act, [SSM_DIM, SSM_DIM + SSM_GROUPS * SSM_STATE], axis=-1)
    dt = jax.nn.softplus(dt_raw.astype(jnp.float32) + lp['dt_bias'].astype(jnp.float32))
    a = -jnp.exp(lp['a_log'].astype(jnp.float32))
    xh = xs.reshape(B, T, SSM_HEADS, SSM_HEAD_DIM)
    y, h = ssd_scan(xh, dt, a, bm.reshape(B, T, SSM_GROUPS, SSM_STATE),
                    cm.reshape(B, T, SSM_GROUPS, SSM_STATE), h0)
    y = y + lp['d_skip'].astype(jnp.float32)[:, None] * xh.astype(jnp.float32)
    y = y.reshape(B, T, SSM_DIM) * jax.nn.silu(z.astype(jnp.float32))
    y = rms_norm(y, lp['ssm_norm_g'])
    return y.astype(z.dtype), new_buf, h


def prepare_nsa_keys(kv_cmp, kv_slc, lp):
    B, T = kv_cmp.shape[:2]
    t_pad = -(-T // L_SEL) * L_SEL
    pad = ((0, 0), (0, t_pad - T), (0, 0), (0, 0), (0, 0))
    kv_cmp = jnp.pad(kv_cmp, pad)
    kv_slc = jnp.pad(kv_slc, pad)
    ch = kv_cmp.reshape(B, t_pad // CMP_STRIDE, CMP_STRIDE, 2, KV_HEADS, HEAD_DIM)
    blocks = jnp.concatenate([ch[:, :-1], ch[:, 1:]], axis=2)
    comp = jnp.einsum('bnlkhd,klde->bnkhe', blocks + lp['cmp_pe'][:, :, None, :], lp['cmp_w'])
    ck = rms_norm(comp[:, :, 0], lp['k_norm_g'][0])
    cv = comp[:, :, 1]
    ns = t_pad // L_SEL
    kb = kv_slc[:, :, 0].reshape(B, ns, L_SEL, KV_HEADS, HEAD_DIM).transpose(0, 3, 1, 2, 4)
    vb = kv_slc[:, :, 1].reshape(B, ns, L_SEL, KV_HEADS, HEAD_DIM).transpose(0, 3, 1, 2, 4)
    nc = ck.shape[1]
    i = jnp.arange(nc)[:, None] * CMP_STRIDE
    j = jnp.arange(ns)[None, :] * L_SEL
    cover = ((i < j + L_SEL) & (i + L_CMP > j)).astype(jnp.float32)
    return ck, cv, kb, vb, cover


def nsa_attend(q, q_pos, ck, cv, kb, vb, wk, wv, w_pos, gates, cover):
    B, Tq = q.shape[:2]
    nc, ns = ck.shape[1], kb.shape[2]
    scale = HEAD_DIM ** -0.5
    s_c = jnp.einsum('bthgd,bnhd->bhgtn', q, ck, preferred_element_type=jnp.float32) * scale
    m_c = (jnp.arange(nc) * CMP_STRIDE + L_CMP - 1)[None, :] <= q_pos[:, None]
    p_c = masked_softmax(s_c, m_c)
    o_c = jnp.einsum('bhgtn,bnhd->bthgd', p_c.astype(cv.dtype), cv)
    imp = jnp.einsum('bhgtn,ns->bhts', p_c, cover)
    blk = jnp.arange(ns)[None, :]
    valid = blk * L_SEL <= q_pos[:, None]
    cur = (q_pos // L_SEL)[:, None]
    forced = (blk == 0) | (blk == cur) | (blk == cur - 1)
    score = jnp.where(valid & forced, SEL_FORCE, jnp.where(valid, imp, -1.0))
    n_sel = min(N_SEL, ns)
    _, idx = lax.top_k(score, n_sel)
    bi = jnp.arange(B)[:, None, None, None]
    hi = jnp.arange(KV_HEADS)[None, :, None, None]
    k_sel = kb[bi, hi, idx].reshape(B, KV_HEADS, Tq, n_sel * L_SEL, HEAD_DIM)
    v_sel = vb[bi, hi, idx].reshape(B, KV_HEADS, Tq, n_sel * L_SEL, HEAD_DIM)
    kpos = (idx[..., None] * L_SEL + jnp.arange(L_SEL)).reshape(B, KV_HEADS, Tq, n_sel * L_SEL)
    m_s = (kpos <= q_pos[:, None])[:, :, None]
    s_s = jnp.einsum('bthgd,bhtkd->bhgtk', q, k_sel, preferred_element_type=jnp.float32) * scale
    p_s = masked_softmax(s_s, m_s)
    o_s = jnp.einsum('bhgtk,bhtkd->bthgd', p_s.astype(v_sel.dtype), v_sel)
    s_w = jnp.einsum('bthgd,bshd->bhgts', q, wk, preferred_element_type=jnp.float32) * scale
    m_w = ((w_pos[None, :] <= q_pos[:, None]) & (w_pos[None, :] >= q_pos[:, None] - WINDOW)
           & (w_pos[None, :] >= 0))
    p_w = masked_softmax(s_w, m_w)
    o_w = jnp.einsum('bhgts,bshd->bthgd', p_w.astype(wv.dtype), wv)
    return gates[..., 0, None] * o_c + gates[..., 1, None] * o_s + gates[..., 2, None] * o_w


def nsa_prompt(q, kvc, kvs, kvw, gates, lp):
    B, T = q.shape[:2]
    ck, cv, kb, vb, cover = prepare_nsa_keys(kvc, kvs, lp)
    win_pad = jnp.pad(kvw, ((0, 0), (WINDOW, 0), (0, 0), (0, 0), (0, 0)))

    def one_block(i):
        q0 = i * Q_BLOCK
        qb = lax.dynamic_slice_in_dim(q, q0, Q_BLOCK, axis=1)
        gb = lax.dynamic_slice_in_dim(gates, q0, Q_BLOCK, axis=1)
        wb = lax.dynamic_slice_in_dim(win_pad, q0, WINDOW + Q_BLOCK, axis=1)
        w_pos = q0 - WINDOW + jnp.arange(WINDOW + Q_BLOCK)
        q_pos = q0 + jnp.arange(Q_BLOCK)
        return nsa_attend(qb, q_pos, ck, cv, kb, vb, wb[:, :, 0], wb[:, :, 1], w_pos, gb, cover)

    o = lax.map(one_block, jnp.arange(T // Q_BLOCK))
    o = jnp.moveaxis(o, 0, 1).reshape(B, T, KV_HEADS, GQA, HEAD_DIM)
    return o, kvw[:, -min(WINDOW, T):]


def nsa_sample(q, kvc, kvs, kvw, gates, lp, cmp_pages, slc_pages, win_buf, page_table):
    B, T = q.shape[:2]
    past = page_table.shape[1] * cmp_pages.shape[1]

    def gather(pages):
        return pages[page_table].reshape(B, past, 2, KV_HEADS, HEAD_DIM)

    full_c = jnp.concatenate([gather(cmp_pages).astype(kvc.dtype), kvc], axis=1)
    full_s = jnp.concatenate([gather(slc_pages).astype(kvs.dtype), kvs], axis=1)
    ck, cv, kb, vb, cover = prepare_nsa_keys(full_c, full_s, lp)
    w_buf = win_buf.shape[1]
    w_all = jnp.concatenate([win_buf.astype(kvw.dtype), kvw], axis=1)
    w_pos = past - w_buf + jnp.arange(w_buf + T)
    q_pos = past + jnp.arange(T)
    o = nsa_attend(q, q_pos, ck, cv, kb, vb, w_all[:, :, 0], w_all[:, :, 1], w_pos, gates, cover)
    return o, w_all[:, -w_buf:]


def trunk_layer(x, conv_buf, h0, nsa_fn, lp):
    B, T, _ = x.shape
    u = rms_norm(x, lp['norm1_g']) @ lp['w_in']
    z, xbc, dt_raw, q, kvc, kvs, kvw, gt = split_proj(u)
    y_ssd, conv_new, h_new = ssd_mixer(z, xbc, dt_raw, conv_buf, h0, lp)
    q = rms_norm(q.reshape(B, T, ATT_HEADS, HEAD_DIM), lp['q_norm_g']).reshape(B, T, KV_HEADS, GQA, HEAD_DIM)
    kvc = kvc.reshape(B, T, 2, KV_HEADS, HEAD_DIM)
    kvs = norm_k(kvs.reshape(B, T, 2, KV_HEADS, HEAD_DIM), lp['k_norm_g'][1])
    kvw = norm_k(kvw.reshape(B, T, 2, KV_HEADS, HEAD_DIM), lp['k_norm_g'][2])
    gates = jax.nn.sigmoid(gt).reshape(B, T, KV_HEADS, GQA, 3)
    o_att, win_new = nsa_fn(q, kvc, kvs, kvw, gates, lp)
    mix = jnp.concatenate([y_ssd, o_att.reshape(B, T, NSA_DIM).astype(y_ssd.dtype)], axis=-1) @ lp['w_out']
    h = x + mix
    g, v = jnp.split(rms_norm(h, lp['norm2_g']) @ lp['w_gu'], 2, axis=-1)
    out = h + (jax.nn.silu(g) * v) @ lp['w_down']
    return out, (kvc, kvs, win_new, h_new, conv_new)


def setup_inputs(seed: int = 0) -> dict:
    key = jax.random.key(seed)
    ks = jax.random.split(key, 32)
    f32 = jnp.float32
    n_pages = PAST_LEN // PAGE_SIZE
    n_used = DEC_BATCH * n_pages
    n_pool = n_used + n_used // 4
    w_buf = min(WINDOW, PAST_LEN)

    def nrm(k, shape, s=1.0):
        return jax.random.normal(k, shape, f32) * s

    dt0 = jnp.exp(jax.random.uniform(ks[12], (DEPTH, SSM_HEADS), f32, math.log(1e-3), math.log(1e-1)))
    return {
        'x_prompt': nrm(ks[0], (BATCH, SEQ, D_MODEL)),
        'x_sample': nrm(ks[1], (DEC_BATCH, DEC_SEQ, D_MODEL)),
        'cache_cmp_kv': nrm(ks[2], (DEPTH, n_pool, PAGE_SIZE, 2, KV_HEADS, HEAD_DIM)),
        'cache_slc_kv': nrm(ks[3], (DEPTH, n_pool, PAGE_SIZE, 2, KV_HEADS, HEAD_DIM)),
        'state_win_kv': nrm(ks[4], (DEPTH, DEC_BATCH, w_buf, 2, KV_HEADS, HEAD_DIM)),
        'state_ssm': nrm(ks[5], (DEPTH, DEC_BATCH, SSM_HEADS, SSM_HEAD_DIM, SSM_STATE), 0.5),
        'state_conv': nrm(ks[6], (DEPTH, DEC_BATCH, CONV_K - 1, CONV_CH)),
        'page_table': jax.random.permutation(ks[7], n_pool)[:n_used].reshape(DEC_BATCH, n_pages).astype(jnp.int32),
        'norm1_g': 1.0 + nrm(ks[8], (DEPTH, D_MODEL), 0.02),
        'w_in': nrm(ks[9], (DEPTH, D_MODEL, IN_COLS), D_MODEL ** -0.5),
        'conv_w': nrm(ks[10], (DEPTH, CONV_K, CONV_CH), CONV_K ** -0.5),
        'conv_b': nrm(ks[11], (DEPTH, CONV_CH), 0.02),
        'dt_bias': dt0 + jnp.log(-jnp.expm1(-dt0)),
        'a_log': jnp.log(jax.random.uniform(ks[13], (DEPTH, SSM_HEADS), f32, 1.0, 16.0)),
        'd_skip': 1.0 + nrm(ks[14], (DEPTH, SSM_HEADS), 0.02),
        'ssm_norm_g': 1.0 + nrm(ks[15], (DEPTH, SSM_DIM), 0.02),
        'q_norm_g': 1.0 + nrm(ks[16], (DEPTH, HEAD_DIM), 0.02),
        'k_norm_g': 1.0 + nrm(ks[17], (DEPTH, 3, HEAD_DIM), 0.02),
        'cmp_pe': nrm(ks[18], (DEPTH, L_CMP, 2, HEAD_DIM), 0.02),
        'cmp_w': nrm(ks[19], (DEPTH, 2, L_CMP, HEAD_DIM, HEAD_DIM), (L_CMP * HEAD_DIM) ** -0.5),
        'w_out': nrm(ks[20], (DEPTH, D_MIX, D_MODEL), D_MIX ** -0.5),
        'norm2_g': 1.0 + nrm(ks[21], (DEPTH, D_MODEL), 0.02),
        'w_gu': nrm(ks[22], (DEPTH, D_MODEL, 2 * D_FF), D_MODEL ** -0.5),
        'w_down': nrm(ks[23], (DEPTH, D_FF, D_MODEL), D_FF ** -0.5),
    }


def reference(x_prompt, x_sample, cache_cmp_kv, cache_slc_kv, state_win_kv, state_ssm, state_conv,
              page_table, norm1_g, w_in, conv_w, conv_b, dt_bias, a_log, d_skip, ssm_norm_g,
              q_norm_g, k_norm_g, cmp_pe, cmp_w, w_out, norm2_g, w_gu, w_down):
    xp, xs = x_prompt, x_sample
    bp = xp.shape[0]
    st_p, st_s = [], []
    for l in range(DEPTH):
        lp = {'norm1_g': norm1_g[l], 'w_in': w_in[l], 'conv_w': conv_w[l], 'conv_b': conv_b[l],
              'dt_bias': dt_bias[l], 'a_log': a_log[l], 'd_skip': d_skip[l], 'ssm_norm_g': ssm_norm_g[l],
              'q_norm_g': q_norm_g[l], 'k_norm_g': k_norm_g[l], 'cmp_pe': cmp_pe[l], 'cmp_w': cmp_w[l],
              'w_out': w_out[l], 'norm2_g': norm2_g[l], 'w_gu': w_gu[l], 'w_down': w_down[l]}
        conv0 = jnp.zeros((bp, CONV_K - 1, CONV_CH), xp.dtype)
        h0 = jnp.zeros((bp, SSM_HEADS, SSM_HEAD_DIM, SSM_STATE), jnp.float32)
        xp, sp = trunk_layer(xp, conv0, h0, nsa_prompt, lp)
        nsa_s = functools.partial(nsa_sample, cmp_pages=cache_cmp_kv[l], slc_pages=cache_slc_kv[l],
                                  win_buf=state_win_kv[l], page_table=page_table)
        xs, ss = trunk_layer(xs, state_conv[l], state_ssm[l], nsa_s, lp)
        st_p.append(sp)
        st_s.append(ss)
    cmp_p = jnp.stack([s[0] for s in st_p])
    cmp_s = jnp.stack([s[0] for s in st_s])
    slc_p = jnp.stack([s[1] for s in st_p])
    slc_s = jnp.stack([s[1] for s in st_s])
    win_p = jnp.stack([s[2] for s in st_p])
    win_s = jnp.stack([s[2] for s in st_s])
    ssm_p = jnp.stack([s[3] for s in st_p])
    ssm_s = jnp.stack([s[3] for s in st_s])
    conv_p = jnp.stack([s[4] for s in st_p])
    conv_s = jnp.stack([s[4] for s in st_s])
    return (xp, xs, cmp_p, cmp_s, slc_p, slc_s, win_p, win_s, ssm_p, ssm_s, conv_p, conv_s)
```

```python
from concourse.bass_utils import run_bass_kernel_spmd
from contextlib import ExitStack
import numpy as np
import concourse.bass as bass
import concourse.mybir as mybir

F32 = mybir.dt.float32
BF16 = mybir.dt.bfloat16
I32 = mybir.dt.int32
U32 = mybir.dt.uint32
AF = mybir.ActivationFunctionType
ALU = mybir.AluOpType
AX = mybir.AxisListType

ENGS = ("pe", "act", "dve", "pool", "sp")


class Buf:
    def __init__(self, t, name=""):
        self.t = t
        self.name = name
        self.w = None
        self.r = {}

    def __getitem__(self, idx):
        return self.t[idx]


class Prog:
    def __init__(self, nc, n_dma_sems=12, self_sync=True):
        self.nc = nc
        self.st = ExitStack()
        self.ops = {e: [] for e in ENGS}
        self.cnt = {e: 0 for e in ENGS}
        self.waited = {e: {} for e in ENGS}
        self.sem = {}
        for e in ENGS:
            if e != "sp":
                self.sem[e] = self.st.enter_context(nc.semaphore("c_" + e))
        self.dq = {}
        for q in ("sp", "act", "pool"):
            sems = [self.st.enter_context(nc.semaphore(f"d_{q}{i}")) for i in range(n_dma_sems)]
            self.dq[q] = {"sems": sems, "m": 0}
        self.self_sync = self_sync
        self.nalloc = 0

    def sb(self, shape, dt, name=None):
        self.nalloc += 1
        name = name or f"sb{self.nalloc}"
        return Buf(self.st.enter_context(self.nc.sbuf_tensor(name, list(shape), dt)), name)

    def ps(self, shape, dt, name=None):
        self.nalloc += 1
        name = name or f"ps{self.nalloc}"
        return Buf(self.st.enter_context(self.nc.psum_tensor(name, list(shape), dt)), name)

    def dram(self, name, shape, dt, kind="Internal"):
        return Buf(self.nc.dram_tensor(name, list(shape), dt, kind=kind).ap(), name)

    def _deps(self, eng, reads, writes):
        waits = []

        def need(ev):
            if ev is None:
                return
            sem, val, key = ev
            if key == eng and (eng == "pe" or not self.self_sync):
                return
            if self.waited[eng].get(key, 0) >= val:
                return
            self.waited[eng][key] = val
            waits.append((sem, val))

        for b in reads:
            need(b.w)
        for b in writes:
            need(b.w)
            for ev in b.r.values():
                need(ev)
        return waits

    def _commit(self, ev, reads, writes):
        for b in reads:
            b.r[ev[2]] = ev
        for b in writes:
            b.w = ev
            b.r = {}

    def op(self, eng, fn, reads=(), writes=()):
        waits = self._deps(eng, reads, writes)
        self.cnt[eng] += 1
        ev = (self.sem[eng], self.cnt[eng], eng)
        self.ops[eng].append((waits, fn, (self.sem[eng], 1)))
        self._commit(ev, reads, writes)
        return ev

    def dma(self, q, out, in_, reads=(), writes=(), indirect=None, **kw):
        D = self.dq[q]
        m = D["m"]
        D["m"] += 1
        ns = len(D["sems"])
        sem = D["sems"][m % ns]
        key = f"d_{q}{m % ns}"
        tgt = 16 * (m // ns + 1)
        D.setdefault("tg", {})[m % ns] = tgt
        waits = self._deps(q, reads, writes)
        if m >= ns and self.waited[q].get(key, 0) < tgt - 16:
            self.waited[q][key] = tgt - 16
            waits.append((sem, tgt - 16))
        if indirect is None:
            fn = lambda e: e.dma_start(out=out, in_=in_, **kw)
        else:
            fn = indirect
        self.ops[q].append((waits, fn, (sem, 16)))
        ev = (sem, tgt, key)
        self._commit(ev, reads, writes)
        return ev

    def barrier(self):
        evs = []
        for e in ENGS:
            if e != "sp" and self.cnt[e] > 0:
                evs.append((self.sem[e], self.cnt[e], e))
        for q, D in self.dq.items():
            for i, tg in D.get("tg", {}).items():
                evs.append((D["sems"][i], tg, f"d_{q}{i}"))
        for e in ENGS:
            waits = []
            for sem, val, key in evs:
                if self.waited[e].get(key, 0) >= val:
                    continue
                self.waited[e][key] = val
                waits.append((sem, val))
            if waits:
                self.ops[e].append((waits, None, None))

    def emit(self):
        nc = self.nc
        self.barrier()
        ops = self.ops

        def run(name, e):
            for waits, fn, inc in ops[name]:
                for sem, val in waits:
                    e.wait_ge(sem, val)
                if fn is not None:
                    ins = fn(e)
                    ins.then_inc(inc[0], inc[1])

        with nc.Block() as block:
            @block.tensor
            def _(e):
                run("pe", e)

            @block.scalar
            def _(e):
                run("act", e)

            @block.vector
            def _(e):
                run("dve", e)

            @block.gpsimd
            def _(e):
                run("pool", e)

            @block.sync
            def _(e):
                run("sp", e)
        self.ops = {e: [] for e in ENGS}

    def close(self):
        self.st.close()


def bcast_ap(ap, dims):
    return bass.AP(ap.tensor, ap.offset, dims)


D = 2048
NTOK = 1028
TILES = [(i * 128, 128) for i in range(8)] + [(1024, 4)]
EPS = 1e-6


def rms_rows(P, xt, rows, width, gbc, out_bf, sq_scr, ssq, rstd):
    P.op("act", lambda e: e.activation(out=sq_scr[:rows, :width], in_=xt[:rows, :width], func=AF.Square,
                                       accum_out=ssq[:rows, :]),
         reads=[xt], writes=[sq_scr, ssq])
    P.op("dve", lambda e: e.tensor_scalar(out=rstd[:rows, :], in0=ssq[:rows, :], scalar1=1.0 / width, scalar2=EPS,
                                          op0=ALU.mult, op1=ALU.add), reads=[ssq], writes=[rstd])
    P.op("act", lambda e: e.activation(out=rstd[:rows, :], in_=rstd[:rows, :], func=AF.Sqrt), reads=[rstd], writes=[rstd])
    P.op("dve", lambda e: e.reciprocal(out=rstd[:rows, :], in_=rstd[:rows, :]), reads=[rstd], writes=[rstd])
    P.op("dve", lambda e: e.scalar_tensor_tensor(out=out_bf[:rows, :width], in0=xt[:rows, :width], scalar=rstd[:rows, :],
                                                 in1=gbc[:rows, :width], op0=ALU.mult, op1=ALU.mult),
         reads=[xt, rstd, gbc], writes=[out_bf])


def build_A(ncols, norm_groups=None):
    norm_groups = norm_groups or {}
    nc = bass.Bass("TRN2", target_bir_lowering=False)
    gains = nc.dram_tensor("gains", [3 * 64], F32, kind="ExternalInput").ap()
    sw_in = nc.dram_tensor("sw_in", [4, 512, 512], F32, kind="ExternalInput").ap()
    sc_in = nc.dram_tensor("sc_in", [4, 3, 2048], F32, kind="ExternalInput").ap()
    sw_out = nc.dram_tensor("sw_out", [4, 511, 512], F32, kind="ExternalOutput").ap()
    sc_out = nc.dram_tensor("sc_out", [4, 2, 2048], F32, kind="ExternalOutput").ap()
    h = nc.dram_tensor("h", [NTOK, D], F32, kind="ExternalInput").ap()
    g = nc.dram_tensor("g", [D], F32, kind="ExternalInput").ap()
    W = nc.dram_tensor("W", [D, ncols], F32, kind="ExternalInput").ap()
    ident = nc.dram_tensor("ident", [128, 128], F32, kind="ExternalInput").ap()
    u = nc.dram_tensor("u", [NTOK, ncols], F32, kind="ExternalOutput").ap()
    P = Prog(nc)
    gbc = P.sb([128, D], F32, "gbc")
    idf = P.sb([128, 128], F32, "idf")
    idb = P.sb([128, 128], BF16, "idb")
    xnT = P.sb([128, 16, NTOK], BF16, "xnT")
    xts = [P.sb([128, D], F32, f"xt{i}") for i in range(2)]
    xnb = [P.sb([128, D], BF16, f"xnb{i}") for i in range(2)]
    sq = P.sb([128, D], BF16, "sq")
    ssq = P.sb([128, 1], F32, "ssq")
    rstd = P.sb([128, 1], F32, "rstd")
    pst = [P.ps([128, 1024], BF16, f"pst{i}") for i in range(2)]
    psm = [P.ps([128, 512], F32, f"psm{i}") for i in range(4)]
    wst = [P.sb([128, 16, 512], F32, f"wst{i}") for i in range(2)]
    wbf = [P.sb([128, 16, 512], BF16, f"wbf{i}") for i in range(2)]
    ot = [P.sb([128, 512], F32, f"ot{i}") for i in range(4)]
    gn = P.sb([128, 3, 64], F32, "gn")
    nsq = P.sb([128, 512], F32, "nsq"); ns8 = P.sb([128, 8], F32, "ns8")
    P.dma("sp", gn[:], gains.partition_broadcast(128), writes=[gn])
    for b in range(4):
        P.dma("act", sw_out[b], sw_in[b, 1:512, :])
        P.dma("act", sc_out[b], sc_in[b, 1:3, :])

    P.dma("sp", gbc[:], g.partition_broadcast(128), writes=[gbc])
    P.dma("sp", idf[:], ident[:, :], writes=[idf])
    P.op("dve", lambda e: e.tensor_copy(out=idb[:], in_=idf[:]), reads=[idf], writes=[idb])
    for ti, (r0, rows) in enumerate(TILES):
        xt = xts[ti % 2]; xb = xnb[ti % 2]
        P.dma("sp", xt[:rows, :], h[r0:r0 + rows, :], writes=[xt])
        rms_rows(P, xt, rows, D, gbc, xb, sq, ssq, rstd)
        for half in range(2):
            pt = pst[half]
            for j in range(8):
                kc = half * 8 + j
                P.op("pe", lambda e, pt=pt, j=j, kc=kc, xb=xb, rows=rows: e.transpose(
                    out=pt[:, j * 128:j * 128 + rows], in_=xb[:rows, kc * 128:(kc + 1) * 128], identity=idb[:rows, :rows]),
                    reads=[xb, idb], writes=[pt])
            eng = "act" if half == 0 else "dve"
            src = pt[:, :].rearrange("p (a b) -> p a b", b=128)[:, :, :rows]
            dst = xnT[:, half * 8:half * 8 + 8, r0:r0 + rows]
            if eng == "act":
                P.op("act", lambda e, src=src, dst=dst: e.copy(out=dst, in_=src), reads=[pt], writes=[xnT])
            else:
                P.op("dve", lambda e, src=src, dst=dst: e.tensor_copy(out=dst, in_=src), reads=[pt], writes=[xnT])
    ngrp = (ncols + 511) // 512
    k = 0
    for gi in range(ngrp):
        c0 = gi * 512
        cw = min(512, ncols - c0)
        ws = wst[gi % 2]; wb = wbf[gi % 2]
        P.dma("sp", ws[:, :, :cw], W[:, c0:c0 + cw].rearrange("(kc p) c -> p kc c", p=128), writes=[ws])
        for q4 in range(4):
            eng = ("pool", "dve", "pool", "act")[q4]
            sl = slice(q4 * 4, q4 * 4 + 4)
            if eng == "act":
                P.op("act", lambda e, ws=ws, wb=wb, sl=sl, cw=cw: e.copy(out=wb[:, sl, :cw], in_=ws[:, sl, :cw]), reads=[ws], writes=[wb])
            else:
                P.op(eng, lambda e, ws=ws, wb=wb, sl=sl, cw=cw: e.tensor_copy(out=wb[:, sl, :cw], in_=ws[:, sl, :cw]), reads=[ws], writes=[wb])
        for ti, (r0, rows) in enumerate(TILES):
            pm = psm[k % 4]; o = ot[k % 4]
            for kc in range(16):
                P.op("pe", lambda e, pm=pm, kc=kc, wb=wb, r0=r0, rows=rows, cw=cw: e.matmul(
                    pm[:rows, :cw], xnT[:, kc, r0:r0 + rows], wb[:, kc, :cw], start=(kc == 0), stop=(kc == 15)),
                    reads=[xnT, wb], writes=[pm])
            if k % 2 == 0:
                P.op("act", lambda e, pm=pm, o=o, rows=rows, cw=cw: e.copy(out=o[:rows, :cw], in_=pm[:rows, :cw]), reads=[pm], writes=[o])
            else:
                P.op("dve", lambda e, pm=pm, o=o, rows=rows, cw=cw: e.tensor_copy(out=o[:rows, :cw], in_=pm[:rows, :cw]), reads=[pm], writes=[o])
            if gi in norm_groups:
                nh, gidx = norm_groups[gi]
                w = nh * 64
                P.op("act", lambda e, o=o, rows=rows, w=w: e.activation(out=nsq[:rows, :w], in_=o[:rows, :w], func=AF.Square), reads=[o], writes=[nsq])
                P.op("dve", lambda e, rows=rows, nh=nh, w=w: e.tensor_reduce(out=ns8[:rows, :nh], in_=nsq[:rows, :w].rearrange("p (h d) -> p h d", d=64),
                                                                           axis=AX.X, op=ALU.add), reads=[nsq], writes=[ns8])
                P.op("dve", lambda e, rows=rows, nh=nh: e.tensor_scalar(out=ns8[:rows, :nh], in0=ns8[:rows, :nh], scalar1=1.0 / 64, scalar2=EPS, op0=ALU.mult, op1=ALU.add),
                     reads=[ns8], writes=[ns8])
                P.op("act", lambda e, rows=rows, nh=nh: e.activation(out=ns8[:rows, :nh], in_=ns8[:rows, :nh], func=AF.Sqrt), reads=[ns8], writes=[ns8])
                P.op("dve", lambda e, rows=rows, nh=nh: e.reciprocal(out=ns8[:rows, :nh], in_=ns8[:rows, :nh]), reads=[ns8], writes=[ns8])
                ov = lambda o=o, rows=rows, w=w: o[:rows, :w].rearrange("p (h d) -> p h d", d=64)
                P.op("dve", lambda e, o=o, rows=rows, nh=nh, w=w: e.tensor_tensor(
                    out=o[:rows, :w].rearrange("p (h d) -> p h d", d=64), in0=o[:rows, :w].rearrange("p (h d) -> p h d", d=64),
                    in1=bcast_ap(ns8[:rows, 0:1], [[8, rows], [1, nh], [0, 64]]), op=ALU.mult), reads=[o, ns8], writes=[o])
                P.op("dve", lambda e, o=o, rows=rows, nh=nh, w=w, gidx=gidx: e.tensor_tensor(
                    out=o[:rows, :w].rearrange("p (h d) -> p h d", d=64), in0=o[:rows, :w].rearrange("p (h d) -> p h d", d=64),
                    in1=bcast_ap(gn[:rows, gidx, 0:1], [[192, rows], [0, nh], [1, 64]]), op=ALU.mult), reads=[o, gn], writes=[o])
            P.dma("sp", u[r0:r0 + rows, c0:c0 + cw], o[:rows, :cw], reads=[o])
            k += 1
    P.emit()
    P.close()
    return nc


PERM = np.concatenate([np.arange(0, 1024), np.arange(1024, 3072), np.arange(3088, 4112), np.arange(4112, 4624),
                       np.arange(4624, 5136), np.arange(5136, 5648), np.arange(3072, 3088), np.arange(5648, 5696)])
NORMG = {6: (8, 0), 7: (8, 0), 9: (4, 1), 10: (4, 2)}


def run_A(h_p, h_s, g, W, qg, kg, st_win, st_conv):
    nc = build_A(5696, NORMG)
    ident = np.eye(128, dtype=np.float32)
    f = lambda a: np.ascontiguousarray(a, dtype=np.float32)
    g = f(g); Wp = f(W[:, PERM])
    gains = f(np.concatenate([qg, kg[1], kg[2]]))
    maps = []
    for c in range(8):
        hc = np.concatenate([h_p[c * 1024:(c + 1) * 1024], h_s[c * 4:(c + 1) * 4]], axis=0)
        maps.append({"h": f(hc), "g": g, "W": Wp, "ident": ident, "gains": gains,
                     "sw_in": f(st_win[4 * c:4 * c + 4].reshape(4, 512, 512)), "sc_in": f(st_conv[4 * c:4 * c + 4])})
    res = run_bass_kernel_spmd(nc, maps, core_ids=list(range(8)))
    up = np.concatenate([r["u"][:1024] for r in res.results], axis=0)
    us = np.concatenate([r["u"][1024:] for r in res.results], axis=0)
    names = (("z", 0, 1024), ("xbc", 1024, 3072), ("q", 3072, 4096), ("kvc", 4096, 4608), ("kvs", 4608, 5120), ("kvw", 5120, 5632),
             ("dt", 5632, 5648), ("gt", 5648, 5696))
    out = {}
    for nm, a, b in names:
        out["p_" + nm] = up[:, a:b]
        out["s_" + nm] = us[:, a:b]
    out["sw_keep"] = np.concatenate([r["sw_out"] for r in res.results], axis=0)
    out["sc_keep"] = np.concatenate([r["sc_out"] for r in res.results], axis=0)
    return out


T = 8192
NCK = 64


def build_B():
    nc = bass.Bass("TRN2", target_bir_lowering=False)
    I = lambda n, s: nc.dram_tensor(n, s, F32, kind="ExternalInput").ap()
    O = lambda n, s: nc.dram_tensor(n, s, F32, kind="ExternalOutput").ap()
    xbcT = I("xbcT", [3, 128, T + 3]); cw = I("cw", [3, 128, 4]); cb = I("cb", [128, 3])
    dtr = I("dtr", [128, NCK, 2]); z = I("z", [128, NCK, 128])
    dtb = I("dtb", [128, 2]); alog = I("alog", [128, 2]); dsk = I("dsk", [128, 2])
    consts = I("consts", [4, 128, 128])
    ident = I("ident", [128, 128])
    sx = I("sx", [128, 32, 4]); sB = I("sB", [4, 128, 32, 128]); sC = I("sC", [4, 128, 32, 128])
    scwx = I("scwx", [128, 4]); scbx = I("scbx", [128, 1])
    scwB = I("scwB", [128, 4, 128]); scbB = I("scbB", [128, 128]); scwC = I("scwC", [128, 4, 128]); scbC = I("scbC", [128, 128])
    sdtr = I("sdtr", [128, 32]); spar = I("spar", [128, 3])
    sz = I("sz", [128, 32]); sH = I("sH", [128, 32, 128])
    yg = O("yg", [T, 128]); hfin = O("hfin", [128, 128]); syg = O("syg", [128, 32]); sHo = O("sHo", [128, 32, 128])

    P = Prog(nc)
    cst = P.sb([128, 4, 128], F32, "cst")
    idf = P.sb([128, 128], F32, "idf"); idb = P.sb([128, 128], BF16, "idb")
    P.dma("sp", cst[:], consts.rearrange("a p f -> p a f"), writes=[cst])
    P.dma("sp", idf[:], ident[:, :], writes=[idf])
    P.op("dve", lambda e: e.tensor_copy(out=idb[:], in_=idf[:]), reads=[idf], writes=[idb])
    US, LL, TRI, ONES = 0, 1, 2, 3

    H = P.sb([128, 32, 128], F32, "sHt")
    tmp = P.sb([128, 32, 128], F32, "stmp")
    inb = P.sb([128, 32, 128], F32, "sinb")
    Bbc = P.sb([128, 32, 128], F32, "sBbc")
    wB = P.sb([128, 4, 128], F32, "swB"); bB = P.sb([128, 128], F32, "sbB")
    sxt = P.sb([128, 32, 4], F32, "sxt"); wx = P.sb([128, 4], F32, "swx"); bx = P.sb([128, 1], F32, "sbx")
    xs = P.sb([128, 32], F32, "sxs"); par = P.sb([128, 3], F32, "spar_t")
    dts = P.sb([128, 32], F32, "sdt"); dec = P.sb([128, 32], F32, "sdec"); dtx = P.sb([128, 32], F32, "sdtx")
    szt = P.sb([128, 32], F32, "szt"); yt = P.sb([128, 32], F32, "syt"); av = P.sb([128, 1], F32, "sav")
    P.dma("sp", H[:], sH[:, :, :], writes=[H])
    P.dma("sp", sxt[:], sx[:, :, :], writes=[sxt])
    P.dma("sp", wx[:], scwx[:, :], writes=[wx]); P.dma("sp", bx[:], scbx[:, :], writes=[bx])
    P.dma("sp", par[:], spar[:, :], writes=[par]); P.dma("sp", dts[:], sdtr[:, :], writes=[dts]); P.dma("sp", szt[:], sz[:, :], writes=[szt])
    P.op("dve", lambda e: e.tensor_scalar(out=xs[:], in0=sxt[:, :, 0], scalar1=wx[:, 0:1], scalar2=None, op0=ALU.mult), reads=[sxt, wx], writes=[xs])
    for k in range(1, 4):
        P.op("dve", lambda e, k=k: e.scalar_tensor_tensor(out=xs[:], in0=sxt[:, :, k], scalar=wx[:, k:k + 1], in1=xs[:], op0=ALU.mult, op1=ALU.add),
             reads=[sxt, wx, xs], writes=[xs])
    P.op("act", lambda e: e.activation(out=xs[:], in_=xs[:], func=AF.Silu, bias=bx[:, 0:1]), reads=[xs, bx], writes=[xs])
    P.op("act", lambda e: e.activation(out=dts[:], in_=dts[:], func=AF.Exp, bias=par[:, 0:1]), reads=[dts, par], writes=[dts])
    P.op("act", lambda e: e.activation(out=dts[:], in_=dts[:], func=AF.Ln, bias=1.0), reads=[dts], writes=[dts])
    P.op("act", lambda e: e.activation(out=av[:], in_=par[:, 1:2], func=AF.Exp), reads=[par], writes=[av])
    P.op("dve", lambda e: e.tensor_scalar(out=av[:], in0=av[:], scalar1=-1.0, scalar2=None, op0=ALU.mult), reads=[av], writes=[av])
    P.op("act", lambda e: e.activation(out=dec[:], in_=dts[:], func=AF.Exp, scale=av[:, 0:1]), reads=[dts, av], writes=[dec])
    P.op("dve", lambda e: e.tensor_tensor(out=dtx[:], in0=dts[:], in1=xs[:], op=ALU.mult), reads=[dts, xs], writes=[dtx])

    def conv_rep(src, wsrc, bsrc, dst):
        P.dma("sp", wB[:], wsrc[:, :, :], writes=[wB]); P.dma("sp", bB[:], bsrc[:, :], writes=[bB])
        for k in range(4):
            P.dma("sp", inb[:], src[k], writes=[inb])
            wk = bcast_ap(wB[:, k, :], [[4 * 128, 128], [0, 32], [1, 128]])
            if k == 0:
                P.op("dve", lambda e, wk=wk: e.tensor_tensor(out=dst[:], in0=inb[:], in1=wk, op=ALU.mult), reads=[inb, wB], writes=[dst])
            else:
                P.op("dve", lambda e, wk=wk: e.tensor_tensor(out=tmp[:], in0=inb[:], in1=wk, op=ALU.mult), reads=[inb, wB], writes=[tmp])
                P.op("pool", lambda e: e.tensor_tensor(out=dst[:], in0=dst[:], in1=tmp[:], op=ALU.add), reads=[dst, tmp], writes=[dst])
        bb = bcast_ap(bB[:, :], [[128, 128], [0, 32], [1, 128]])
        P.op("dve", lambda e: e.tensor_tensor(out=dst[:], in0=dst[:], in1=bb, op=ALU.add), reads=[dst, bB], writes=[dst])
        P.op("act", lambda e: e.activation(out=dst[:], in_=dst[:], func=AF.Silu), reads=[dst], writes=[dst])

    conv_rep(sB, scwB, scbB, Bbc)
    decb = bcast_ap(dec[:, :], [[32, 128], [1, 32], [0, 128]])
    dtxb = bcast_ap(dtx[:, :], [[32, 128], [1, 32], [0, 128]])
    P.op("dve", lambda e: e.tensor_tensor(out=H[:], in0=H[:], in1=decb, op=ALU.mult), reads=[H, dec], writes=[H])
    P.op("dve", lambda e: e.tensor_tensor(out=Bbc[:], in0=Bbc[:], in1=dtxb, op=ALU.mult), reads=[Bbc, dtx], writes=[Bbc])
    P.op("dve", lambda e: e.tensor_tensor(out=H[:], in0=H[:], in1=Bbc[:], op=ALU.add), reads=[H, Bbc], writes=[H])
    P.dma("sp", sHo[:, :, :], H[:], reads=[H])
    conv_rep(sC, scwC, scbC, Bbc)
    P.op("dve", lambda e: e.tensor_tensor(out=Bbc[:], in0=Bbc[:], in1=H[:], op=ALU.mult), reads=[Bbc, H], writes=[Bbc])
    P.op("dve", lambda e: e.tensor_reduce(out=yt[:], in_=Bbc[:], axis=AX.X, op=ALU.add), reads=[Bbc], writes=[yt])
    P.op("dve", lambda e: e.scalar_tensor_tensor(out=yt[:], in0=xs[:], scalar=par[:, 2:3], in1=yt[:], op0=ALU.mult, op1=ALU.add),
         reads=[xs, par, yt], writes=[yt])
    P.op("act", lambda e: e.activation(out=szt[:], in_=szt[:], func=AF.Silu), reads=[szt], writes=[szt])
    P.op("dve", lambda e: e.tensor_tensor(out=yt[:], in0=yt[:], in1=szt[:], op=ALU.mult), reads=[yt, szt], writes=[yt])
    P.dma("sp", syg[:, :], yt[:], reads=[yt])

    act3 = [P.sb([128, T], BF16, f"act3_{i}") for i in range(3)]
    cin = P.sb([128, 2051], F32, "cin"); cacc = P.sb([128, 2048], F32, "cacc")
    cwt = P.sb([128, 3, 4], F32, "cwt"); cbt = P.sb([128, 3], F32, "cbt")
    P.dma("sp", cwt[:], cw.rearrange("a p k -> p a k"), writes=[cwt])
    P.dma("sp", cbt[:], cb[:, :], writes=[cbt])
    for a in range(3):
        for blk in range(4):
            c0 = blk * 2048
            P.dma("sp", cin[:], xbcT[a, :, c0:c0 + 2051], writes=[cin])
            P.op("dve", lambda e, a=a: e.tensor_scalar(out=cacc[:], in0=cin[:, 0:2048], scalar1=cwt[:, a, 0:1], scalar2=None, op0=ALU.mult),
                 reads=[cin, cwt], writes=[cacc])
            for k in range(1, 4):
                P.op("dve", lambda e, a=a, k=k: e.scalar_tensor_tensor(out=cacc[:], in0=cin[:, k:k + 2048], scalar=cwt[:, a, k:k + 1], in1=cacc[:],
                                                                    op0=ALU.mult, op1=ALU.add), reads=[cin, cwt, cacc], writes=[cacc])
            P.op("act", lambda e, a=a, c0=c0: e.activation(out=act3[a][:, c0:c0 + 2048], in_=cacc[:], func=AF.Silu, bias=cbt[:, a:a + 1]),
                 reads=[cacc, cbt], writes=[act3[a]])
    xT, BT, CT = act3
    dt = P.sb([128, NCK, 2], F32, "dt"); dA = P.sb([128, NCK, 2], F32, "dA"); ww = P.sb([128, NCK, 2], F32, "ww")
    ee = P.sb([128, NCK, 2], F32, "ee"); cd = P.sb([128, NCK, 2], F32, "cd")
    p3 = P.sb([128, 3, 2], F32, "p3"); a2 = P.sb([128, 2], F32, "a2")
    P.dma("sp", dt[:], dtr[:, :, :], writes=[dt])
    P.dma("sp", p3[:, 0, :], dtb[:, :], writes=[p3]); P.dma("sp", p3[:, 1, :], alog[:, :], writes=[p3]); P.dma("sp", p3[:, 2, :], dsk[:, :], writes=[p3])
    P.op("dve", lambda e: e.tensor_tensor(out=dt[:], in0=dt[:], in1=bcast_ap(p3[:, 0, :], [[6, 128], [0, NCK], [1, 2]]), op=ALU.add), reads=[dt, p3], writes=[dt])
    P.op("act", lambda e: e.activation(out=dt[:], in_=dt[:], func=AF.Exp), reads=[dt], writes=[dt])
    P.op("act", lambda e: e.activation(out=dt[:], in_=dt[:], func=AF.Ln, bias=1.0), reads=[dt], writes=[dt])
    P.op("act", lambda e: e.activation(out=a2[:], in_=p3[:, 1, :], func=AF.Exp), reads=[p3], writes=[a2])
    P.op("dve", lambda e: e.tensor_scalar(out=a2[:], in0=a2[:], scalar1=-1.0, scalar2=None, op0=ALU.mult), reads=[a2], writes=[a2])
    P.op("dve", lambda e: e.tensor_tensor(out=dA[:], in0=dt[:], in1=bcast_ap(a2[:, :], [[2, 128], [0, NCK], [1, 2]]), op=ALU.mult), reads=[dt, a2], writes=[dA])
    ptx_ = P.ps([128, 1024], BF16, "ptx")
    pbank = [P.ps([128, 512], F32, f"pb{i}") for i in range(6)]
    class _V:
        def __init__(s, b, n): s.b = b; s.n = n
    def view(b, n):
        v = Buf(b.t, b.name); v.__dict__ = b.__dict__; return b
    psA, psB, psC = pbank[0], pbank[1], pbank[2]
    dA2 = dA[:, :, :].rearrange("p a b -> p (a b)")
    P.op("pe", lambda e: e.matmul(psA[:, 0:128], cst[:, LL, :], dA2, start=True, stop=True), reads=[cst, dA], writes=[psA])
    P.op("pe", lambda e: e.matmul(psB[:, 0:128], cst[:, US, :], dA2, start=True, stop=True), reads=[cst, dA], writes=[psB])
    P.op("pe", lambda e: e.matmul(psC[:, 0:128], cst[:, ONES, :], dA2, start=True, stop=True), reads=[cst, dA], writes=[psC])
    f2 = lambda t: t[:, :, :].rearrange("p a b -> p (a b)")
    P.op("act", lambda e: e.activation(out=f2(ee), in_=psA[:, 0:128], func=AF.Exp), reads=[psA], writes=[ee])
    P.op("act", lambda e: e.activation(out=f2(ww), in_=psB[:, 0:128], func=AF.Exp), reads=[psB], writes=[ww])
    P.op("act", lambda e: e.activation(out=f2(cd), in_=psC[:, 0:128], func=AF.Exp), reads=[psC], writes=[cd])
    P.op("dve", lambda e: e.tensor_tensor(out=ww[:], in0=ww[:], in1=dt[:], op=ALU.mult), reads=[ww, dt], writes=[ww])

    Hs = P.sb([128, 128], F32, "Hs"); Hb = P.sb([128, 128], BF16, "Hb")
    P.op("dve", lambda e: e.memset(Hs[:], 0.0), writes=[Hs])
    P.op("dve", lambda e: e.memset(Hb[:], 0.0), writes=[Hb])
    ptx = ptx_
    pG = pbank[0]; pseg = [pbank[1], pbank[2]]; pyd = pbank[3]; pyo = pbank[4]; pS = pbank[5]
    xtok = [P.sb([128, 256], BF16, f"xtok{i}") for i in range(2)]
    Gm = [P.sb([128, 128], F32, f"Gm{i}") for i in range(2)]
    UdA = [P.sb([128, 128], F32, f"UdA{i}") for i in range(2)]
    Eb = [P.sb([128, 128], F32, f"Eb{i}") for i in range(2)]
    MT = [P.sb([128, 128], BF16, f"MT{i}") for i in range(2)]
    xw = [P.sb([128, 128], BF16, f"xw{i}") for i in range(2)]
    yds = [P.sb([128, 128], F32, f"yds{i}") for i in range(2)]
    yo = [P.sb([128, 128], F32, f"yo{i}") for i in range(2)]
    zt = [P.sb([128, 128], F32, f"zt{i}") for i in range(2)]
    for ck in range(NCK):
        s = ck % 2
        cs = slice(ck * 128, (ck + 1) * 128)
        xt_ = xtok[s]
        P.dma("sp", zt[s][:], z[:, ck, :], writes=[zt[s]])
        P.op("act", lambda e, s=s: e.activation(out=zt[s][:], in_=zt[s][:], func=AF.Silu), reads=[zt[s]], writes=[zt[s]])
        P.op("pe", lambda e, cs=cs: e.transpose(out=ptx[:, 0:128], in_=xT[:, cs], identity=idb[:]), reads=[xT, idb], writes=[ptx])
        P.op("pe", lambda e, cs=cs: e.transpose(out=ptx[:, 128:256], in_=BT[:, cs], identity=idb[:]), reads=[BT, idb], writes=[ptx])
        P.op("act", lambda e, xt_=xt_: e.copy(out=xt_[:], in_=ptx[:, 0:256]), reads=[ptx], writes=[xt_])
        P.op("pe", lambda e, cs=cs: e.matmul(pG[:, 0:128], BT[:, cs], CT[:, cs], start=True, stop=True), reads=[BT, CT], writes=[pG])
        P.op("dve", lambda e, s=s: e.tensor_tensor(out=Gm[s][:], in0=pG[:, 0:128], in1=cst[:, TRI, :], op=ALU.mult), reads=[pG, cst], writes=[Gm[s]])
        for hh in range(2):
            hs = slice(hh * 64, hh * 64 + 64)
            P.op("dve", lambda e, hh=hh, ck=ck: e.tensor_scalar(out=UdA[hh][:], in0=cst[:, US, :], scalar1=dA[:, ck, hh:hh + 1], scalar2=None, op0=ALU.mult),
                 reads=[cst, dA], writes=[UdA[hh]])
            P.op("pe", lambda e, hh=hh: e.matmul(pseg[hh][:, 0:128], UdA[hh][:], cst[:, LL, :], start=True, stop=True), reads=[UdA[hh], cst], writes=[pseg[hh]])
            P.op("act", lambda e, hh=hh: e.activation(out=Eb[hh][:], in_=pseg[hh][:, 0:128], func=AF.Exp), reads=[pseg[hh]], writes=[Eb[hh]])
            P.op("dve", lambda e, hh=hh, ck=ck, s=s: e.scalar_tensor_tensor(out=MT[hh][:], in0=Eb[hh][:], scalar=dt[:, ck, hh:hh + 1], in1=Gm[s][:],
                                                                        op0=ALU.mult, op1=ALU.mult), reads=[Eb[hh], dt, Gm[s]], writes=[MT[hh]])
            P.op("pe", lambda e, hh=hh, hs=hs, xt_=xt_: e.matmul(pyd[:, hs], MT[hh][:], xt_[:, hs], start=True, stop=True), reads=[MT[hh], xt_], writes=[pyd])
            P.op("pe", lambda e, hs=hs, cs=cs: e.matmul(pyo[:, hs], CT[:, cs], Hb[:, hs], start=True, stop=True), reads=[CT, Hb], writes=[pyo])
            P.op("dve", lambda e, hh=hh, hs=hs, ck=ck, s=s, xt_=xt_: e.tensor_scalar(out=xw[s][:, hs], in0=xt_[:, hs], scalar1=ww[:, ck, hh:hh + 1], scalar2=None, op0=ALU.mult),
                 reads=[xt_, ww], writes=[xw[s]])
        P.op("pe", lambda e, s=s, xt_=xt_: e.matmul(pS[:, 0:128], xt_[:, 128:256], xw[s][:], start=True, stop=True), reads=[xt_, xw[s]], writes=[pS])
        P.op("act", lambda e, s=s: e.copy(out=yds[s][:], in_=pyd[:, 0:128]), reads=[pyd], writes=[yds[s]])
        for hh in range(2):
            hs = slice(hh * 64, hh * 64 + 64)
            P.op("dve", lambda e, hh=hh, hs=hs, ck=ck, s=s: e.scalar_tensor_tensor(out=yo[s][:, hs], in0=pyo[:, hs], scalar=ee[:, ck, hh:hh + 1], in1=yds[s][:, hs],
                                                                               op0=ALU.mult, op1=ALU.add), reads=[pyo, ee, yds[s]], writes=[yo[s]])
            P.op("dve", lambda e, hh=hh, hs=hs, s=s, xt_=xt_: e.scalar_tensor_tensor(out=yo[s][:, hs], in0=xt_[:, hs], scalar=p3[:, 2, hh:hh + 1], in1=yo[s][:, hs],
                                                                                 op0=ALU.mult, op1=ALU.add), reads=[xt_, p3, yo[s]], writes=[yo[s]])
            P.op("dve", lambda e, hh=hh, hs=hs, ck=ck: e.scalar_tensor_tensor(out=Hs[:, hs], in0=Hs[:, hs], scalar=cd[:, ck, hh:hh + 1], in1=pS[:, hs],
                                                                          op0=ALU.mult, op1=ALU.add), reads=[Hs, cd, pS, pyo], writes=[Hs])
        P.op("act", lambda e: e.copy(out=Hb[:], in_=Hs[:]), reads=[Hs, pyo], writes=[Hb])
        P.op("dve", lambda e, s=s: e.tensor_tensor(out=yo[s][:], in0=yo[s][:], in1=zt[s][:], op=ALU.mult), reads=[yo[s], zt[s]], writes=[yo[s]])
        P.dma("sp", yg[cs, :], yo[s][:], reads=[yo[s]])
    P.dma("sp", hfin[:, :], Hs[:], reads=[Hs])
    P.emit()
    P.close()
    return nc


def run_B(z_p, xbc_p, dt_p, z_s, xbc_s, dt_s, st_conv, st_ssm, conv_w, conv_b, dt_bias, a_log, d_skip):
    nc = build_B()
    f = lambda a: np.ascontiguousarray(a, dtype=np.float32)
    t = np.arange(128)
    consts = np.stack([(t[:, None] > t[None, :]), (t[:, None] <= t[None, :]), (t[:, None] <= t[None, :]), np.ones((128, 128), bool)]).astype(np.float32)
    ident = np.eye(128, dtype=np.float32)
    maps = []
    for c in range(8):
        gi = c // 2
        cols = [np.arange(128 * c, 128 * c + 128), np.arange(1024 + 128 * gi, 1024 + 128 * gi + 128), np.arange(1536 + 128 * gi, 1536 + 128 * gi + 128)]
        xbcT = np.zeros((3, 128, T + 3), np.float32)
        for a in range(3):
            xbcT[a, :, 3:] = xbc_p[:, cols[a]].T
        cw = np.stack([conv_w[:, cols[a]].T for a in range(3)])
        cb = np.stack([conv_b[cols[a]] for a in range(3)], axis=1)
        hsl = slice(2 * c, 2 * c + 2)
        m = {"xbcT": xbcT, "cw": f(cw), "cb": f(cb),
             "dtr": f(dt_p[:, hsl].reshape(64, 128, 2).transpose(1, 0, 2)),
             "z": f(z_p[:, 128 * c:128 * c + 128].reshape(64, 128, 128).transpose(1, 0, 2)),
             "dtb": f(np.broadcast_to(dt_bias[hsl], (128, 2))), "alog": f(np.broadcast_to(a_log[hsl], (128, 2))),
             "dsk": f(np.broadcast_to(d_skip[hsl], (128, 2))), "consts": consts, "ident": ident}
        full = np.concatenate([st_conv, xbc_s[:, None, :]], axis=1)
        m["sx"] = f(full[:, :, cols[0]].transpose(2, 0, 1))
        for nm, a in (("B", 1), ("C", 2)):
            m["s" + nm] = f(np.broadcast_to(full[:, :, cols[a]].transpose(1, 0, 2)[:, None], (4, 128, 32, 128)))
            m["scw" + nm] = f(np.broadcast_to(conv_w[:, cols[a]][None], (128, 4, 128)))
            m["scb" + nm] = f(np.broadcast_to(conv_b[cols[a]][None], (128, 128)))
        m["scwx"] = f(conv_w[:, cols[0]].T); m["scbx"] = f(conv_b[cols[0]][:, None])
        hp = np.repeat(np.arange(2 * c, 2 * c + 2), 64)
        m["sdtr"] = f(dt_s[:, hp].T)
        m["spar"] = f(np.stack([dt_bias[hp], a_log[hp], d_skip[hp]], axis=1))
        m["sz"] = f(z_s[:, 128 * c:128 * c + 128].T)
        m["sH"] = f(st_ssm[:, hsl].reshape(32, 128, 128).transpose(1, 0, 2))
        maps.append(m)
    res = run_bass_kernel_spmd(nc, maps, core_ids=list(range(8)))
    yg_p = np.concatenate([r["yg"] for r in res.results], axis=1)
    ssm_p = np.concatenate([r["hfin"].T.reshape(2, 64, 128) for r in res.results], axis=0)
    yg_s = np.concatenate([r["syg"].T for r in res.results], axis=1)
    ssm_s = np.concatenate([r["sHo"].transpose(1, 0, 2).reshape(32, 2, 64, 128) for r in res.results], axis=1)
    return yg_p, ssm_p, yg_s, ssm_s


NPOOL = 2560
NB = 16


def build_G(ntab):
    nc = bass.Bass("TRN2", target_bir_lowering=False)
    pools = [nc.dram_tensor(f"pool{t}", [NPOOL * 128, 128], F32, kind="ExternalInput").ap() for t in range(ntab)]
    pt = nc.dram_tensor("pt", [1024], I32, kind="ExternalInput").ap()
    iota = nc.dram_tensor("iota", [128, 1], I32, kind="ExternalInput").ap()
    outs = [nc.dram_tensor(f"g{t}", [16, 8192, 128], F32, kind="ExternalOutput").ap() for t in range(ntab)]
    P = Prog(nc)
    ptb = P.sb([128, 1024], I32, "ptb")
    io = P.sb([128, 1], I32, "io")
    idx = P.sb([128, 1024], I32, "idx")
    bufs = [P.sb([128, NB, 128], F32, f"gb{i}") for i in range(4)]
    P.dma("sp", ptb[:], pt.partition_broadcast(128), writes=[ptb])
    P.dma("sp", io[:], iota[:, :], writes=[io])
    P.op("dve", lambda e: e.tensor_scalar(out=idx[:], in0=ptb[:], scalar1=128.0, scalar2=io[:, 0:1], op0=ALU.mult, op1=ALU.add),
         reads=[ptb, io], writes=[idx])
    k = 0
    for t in range(ntab):
        for s in range(16):
            for j0 in range(0, 64, NB):
                b = bufs[k % 4]; k += 1
                for jj in range(NB):
                    col = s * 64 + j0 + jj
                    P.dma("pool", None, None, reads=[idx], writes=[b],
                          indirect=lambda e, b=b, jj=jj, col=col, t=t: e.indirect_dma_start(
                              out=b[:, jj, :], out_offset=None, in_=pools[t][:, :],
                              in_offset=bass.IndirectOffsetOnAxis(ap=idx[:, col:col + 1], axis=0)))
                P.dma("sp", outs[t][s, j0 * 128:(j0 + NB) * 128, :].rearrange("(j p) c -> p j c", p=128), b[:], reads=[b])
    P.emit()
    P.close()
    return nc


def run_G(pool_list, page_table):
    nt = len(pool_list)
    nc = build_G(nt)
    iota = np.arange(128, dtype=np.int32)[:, None]
    maps = []
    byhead = [[np.ascontiguousarray(p[:, :, :, h, :], dtype=np.float32).reshape(NPOOL * 128, 128) for h in range(4)] for p in pool_list]
    for c in range(8):
        h = c % 4; half = c // 4
        m = {"pt": np.ascontiguousarray(page_table[16 * half:16 * half + 16].reshape(-1), dtype=np.int32), "iota": iota}
        for t in range(nt):
            m[f"pool{t}"] = byhead[t][h]
        maps.append(m)
    res = run_bass_kernel_spmd(nc, maps, core_ids=list(range(8)))
    outs = []
    for t in range(nt):
        g = np.zeros((32, 8192, 2, 4, 64), np.float32)
        for c, r in enumerate(res.results):
            h = c % 4; half = c // 4
            g[16 * half:16 * half + 16, :, :, h, :] = r[f"g{t}"].reshape(16, 8192, 2, 64)
        outs.append(g)
    return outs


TP = 8256
TPK = 8320
NJ = 32
NSEQ = 16
SCALE = 0.125
NEG = -1.0e30


def ntmax(j):
    return (16 * j + 14) // 128


CM_IDX = {}
for _j in range(NJ):
    for _nt in range(ntmax(_j) + 1):
        CM_IDX[(_j, _nt)] = len(CM_IDX)
NCM = len(CM_IDX)


def build_C(do_prompt=True, do_sample=True):
    nc = bass.Bass("TRN2", target_bir_lowering=False)
    I = lambda n, s: nc.dram_tensor(n, s, F32, kind="ExternalInput").ap()
    O = lambda n, s: nc.dram_tensor(n, s, F32, kind="ExternalOutput").ap()
    qT = I("qT", [NJ, 128, 512]); gts = I("gts", [128, NJ, 12])
    kcv = I("kcv", [128, TP]); ksw = I("ksw", [128, 8192]); vs = I("vs", [128, 64, 64]); vw = I("vw", [128, 64, 64])
    wkv = I("wkv", [128, 32, 64]); peT = I("peT", [128, 32]); gk0 = I("gk0", [64, 1])
    cover = I("cover", [128, 5, 129]); t3 = I("t3", [128, 32, 128]); ident = I("ident", [128, 128])
    LM = I("LM", [128, 2, 128]); WM = I("WM", [128, 6, 128]); CM = I("CM", [128, NCM, 128]); ADD = I("ADD", [128, NJ, 128])
    s_kcv = I("s_kcv", [NSEQ, 128, TP]); s_ks = I("s_ks", [NSEQ, 64, TPK]); s_vs = I("s_vs", [NSEQ, 128, 65, 64])
    s_kw = I("s_kw", [NSEQ, 64, 640]); s_vw = I("s_vw", [NSEQ, 128, 5, 64])
    s_q = I("s_q", [NSEQ, 128, 4]); s_gt = I("s_gt", [NSEQ, 4, 3])
    s_msk = I("s_msk", [128, 75])
    s_add = I("s_add", [1, 129])
    o_p = O("o_p", [128, NJ, 256]); o_s = O("o_s", [NSEQ, 4, 64])

    P = Prog(nc)
    stg = [P.sb([128, 2048], F32, f"stg{i}") for i in range(2)]
    scnt = [0]

    def load_cast(dst, dst_ap, src_ap, parts, fshape, p0=0):
        if isinstance(fshape, int):
            fshape = (fshape,)
        n = int(np.prod(fshape))
        st = stg[scnt[0] % 2]
        eng = ("dve", "pool")[scnt[0] % 2]
        scnt[0] += 1
        sview = st[p0:p0 + parts, 0:n]
        if len(fshape) == 2:
            sview = sview.rearrange("p (a b) -> p a b", b=fshape[1])
        P.dma("sp", sview, src_ap, writes=[st])
        P.op(eng, lambda e: e.tensor_copy(out=dst_ap, in_=sview), reads=[st], writes=[dst])

    idf = P.sb([128, 128], F32, "idf"); idb = P.sb([128, 128], BF16, "idb")
    P.dma("sp", idf[:], ident[:, :], writes=[idf])
    P.op("dve", lambda e: e.tensor_copy(out=idb[:], in_=idf[:]), reads=[idf], writes=[idb])
    ones64 = P.sb([64, 64], F32, "ones64"); P.op("dve", lambda e: e.memset(ones64[:], 1.0), writes=[ones64])
    one_bf = P.sb([1, 2], BF16, "one_bf"); P.op("dve", lambda e: e.memset(one_bf[:], 1.0), writes=[one_bf])
    wkvb = P.sb([128, 32, 64], BF16, "wkvb")
    load_cast(wkvb, wkvb[:, :, :], wkv[:, :, :], 128, (32, 64))
    pe = P.sb([128, 32], F32, "pe"); P.dma("sp", pe[:], peT[:, :], writes=[pe])
    gk = P.sb([64, 1], F32, "gk"); P.dma("sp", gk[:], gk0[:, :], writes=[gk])
    covb = P.sb([128, 5, 129], BF16, "covb")
    load_cast(covb, covb[:, :, :], cover[:, :, :], 128, (5, 129))
    t3b = P.sb([128, 32, 128], BF16, "t3b")
    for a in range(2):
        load_cast(t3b, t3b[:, a * 16:(a + 1) * 16, :], t3[:, a * 16:(a + 1) * 16, :], 128, (16, 128))

    KVlo = P.sb([128, TP], BF16, "KVlo"); KVhi = P.sb([128, TP], BF16, "KVhi")
    KSW = P.sb([128, TPK], BF16, "KSW")
    vs_a = P.sb([128, 65, 65], BF16, "vs_a"); vw_a = P.sb([128, 64, 65], BF16, "vw_a"); cv_a = P.sb([128, 5, 65], BF16, "cv_a")
    for t_ in (vs_a, vw_a, cv_a):
        P.op("pool", lambda e, t_=t_: e.memset(t_[:], 1.0), writes=[t_])
    ckf = P.sb([64, 516], F32, "ckf"); cks = P.sb([64, 516], F32, "cks"); ckT = P.sb([64, 640], BF16, "ckT")
    P.op("pool", lambda e: e.memset(ckT[:], 0.0), writes=[ckT])
    pS = [P.ps([128, 512], F32, f"pS{i}") for i in range(2)]
    pM = P.ps([128, 512], F32, "pM"); pOc = P.ps([128, 512], F32, "pOc"); pOs = P.ps([128, 512], F32, "pOs")
    pOw = P.ps([128, 512], F32, "pOw"); pU = P.ps([128, 512], F32, "pU"); pT = P.ps([128, 1024], BF16, "pT")
    pOT, pUT, pX = pOc, pOs, pOw

    def compress(src):
        for c0 in range(0, TP, 2048):
            w = min(2048, TP - c0)
            st = stg[scnt[0] % 2]; scnt[0] += 1
            P.dma("sp", st[:, :w], src[:, c0:c0 + w], writes=[st])
            for dst, po, eng in ((KVlo, 0, "dve"), (KVhi, 16, "pool")):
                P.op(eng, lambda e, dst=dst, po=po, st=st, c0=c0, w=w: e.tensor_tensor(
                    out=dst[:, c0:c0 + w].rearrange("p (n l) -> p n l", l=16), in0=st[:, :w].rearrange("p (n l) -> p n l", l=16),
                    in1=bcast_ap(pe[:, po:po + 1], [[32, 128], [0, w // 16], [1, 16]]), op=ALU.add), reads=[st, pe], writes=[dst])
        for (n0, nn, pb) in ((0, 512, pS[0]), (512, 3, pS[1])):
            for l in range(32):
                srcb = KVlo if l < 16 else KVhi
                P.op("pe", lambda e, srcb=srcb, l=l, n0=n0, nn=nn, pb=pb: e.matmul(
                    pb[0:64, 0:nn], wkvb[0:64, l, :], bcast_ap(srcb[0:64, 16 * n0 + l:16 * n0 + l + 1], [[TP, 64], [16, nn]]),
                    start=(l == 0), stop=(l == 31)), reads=[wkvb, srcb], writes=[pb])
        P.op("act", lambda e: e.copy(out=ckf[:, 0:512], in_=pS[0][0:64, 0:512]), reads=[pS[0]], writes=[ckf])
        P.op("act", lambda e: e.copy(out=ckf[:, 512:515], in_=pS[1][0:64, 0:3]), reads=[pS[1]], writes=[ckf])
        P.op("act", lambda e: e.activation(out=cks[:, 0:515], in_=ckf[:, 0:515], func=AF.Square), reads=[ckf], writes=[cks])
        P.op("pe", lambda e: e.matmul(pS[0][0:64, 0:512], ones64[:], cks[:, 0:512], start=True, stop=True), reads=[ones64, cks], writes=[pS[0]])
        P.op("pe", lambda e: e.matmul(pS[1][0:64, 0:3], ones64[:], cks[:, 512:515], start=True, stop=True), reads=[ones64, cks], writes=[pS[1]])
        P.op("dve", lambda e: e.tensor_scalar(out=cks[:, 0:512], in0=pS[0][0:64, 0:512], scalar1=1.0 / 64, scalar2=1e-6, op0=ALU.mult, op1=ALU.add),
             reads=[pS[0]], writes=[cks])
        P.op("dve", lambda e: e.tensor_scalar(out=cks[:, 512:515], in0=pS[1][0:64, 0:3], scalar1=1.0 / 64, scalar2=1e-6, op0=ALU.mult, op1=ALU.add),
             reads=[pS[1]], writes=[cks])
        P.op("act", lambda e: e.activation(out=cks[:, 0:515], in_=cks[:, 0:515], func=AF.Sqrt), reads=[cks], writes=[cks])
        P.op("dve", lambda e: e.reciprocal(out=cks[:, 0:515], in_=cks[:, 0:515]), reads=[cks], writes=[cks])
        P.op("dve", lambda e: e.tensor_tensor(out=ckf[:, 0:515], in0=ckf[:, 0:515], in1=cks[:, 0:515], op=ALU.mult), reads=[ckf, cks], writes=[ckf])
        P.op("dve", lambda e: e.tensor_scalar(out=ckT[:, 0:515], in0=ckf[:, 0:515], scalar1=gk[:, 0:1], scalar2=None, op0=ALU.mult),
             reads=[ckf, gk], writes=[ckT])
        for nt in range(5):
            nn = 128 if nt < 4 else 3
            for l in range(32):
                srcb = KVlo if l < 16 else KVhi
                col = 16 * 128 * nt + l
                P.op("pe", lambda e, srcb=srcb, l=l, nn=nn, col=col: e.matmul(
                    pU[0:nn, 0:64], bcast_ap(srcb[64:128, col:col + 1], [[TP, 64], [16, nn]]), wkvb[64:128, l, :],
                    start=(l == 0), stop=(l == 31)), reads=[wkvb, srcb], writes=[pU])
            P.op("act", lambda e, nt=nt, nn=nn: e.copy(out=cv_a[0:nn, nt, 0:64], in_=pU[0:nn, 0:64]), reads=[pU], writes=[cv_a])

    def bc_g(buf, ap1):
        return bcast_ap(ap1, [[ap1.ap[0][0], 128], [0, 4], [1, 128]])

    if do_prompt:
        LMb = P.sb([128, 2, 128], BF16, "LMb"); WMb = P.sb([128, 6, 128], BF16, "WMb"); CMb = P.sb([128, NCM, 128], BF16, "CMb")
        load_cast(LMb, LMb[:, :, :], LM[:, :, :], 128, (2, 128))
        load_cast(WMb, WMb[:, :, :], WM[:, :, :], 128, (6, 128))
        for a in range(0, NCM, 16):
            load_cast(CMb, CMb[:, a:a + 16, :], CM[:, a:a + 16, :], 128, (16, 128))
        addt = P.sb([128, NJ, 128], F32, "addt"); P.dma("sp", addt[:], ADD[:, :, :], writes=[addt])
        gtt = P.sb([128, NJ, 12], F32, "gtt"); P.dma("sp", gtt[:], gts[:, :, :], writes=[gtt])
        P.op("act", lambda e: e.activation(out=gtt[:], in_=gtt[:], func=AF.Sigmoid), reads=[gtt], writes=[gtt])
        compress(kcv)
        for c0 in range(0, 8192, 2048):
            load_cast(KSW, KSW[:, c0:c0 + 2048], ksw[:, c0:c0 + 2048], 128, 2048)
        for a in range(0, 64, 32):
            load_cast(vs_a, vs_a[:, a:a + 32, 0:64], vs[:, a:a + 32, :], 128, (32, 64))
            load_cast(vw_a, vw_a[:, a:a + 32, 0:64], vw[:, a:a + 32, :], 128, (32, 64))
        qb = [P.sb([128, 512], BF16, f"qb{i}") for i in range(2)]
        Pt = [P.sb([128, 512], BF16, f"Pt{i}") for i in range(3)]
        imp = P.sb([128, 128], F32, "imp"); sc2 = P.sb([128, 128], F32, "sc2"); m8 = P.sb([128, 16], F32, "m8")
        selb = P.sb([128, 128], BF16, "selb"); selT = P.sb([128, 128], BF16, "selT")
        rs = P.sb([128, 12], F32, "rs"); cf = P.sb([128, 12], F32, "cf")
        ot = [P.sb([128, 256], F32, f"ot{i}") for i in range(2)]; otmp = P.sb([128, 256], F32, "otmp")
        pk = [0]

        def qk_exp(lhs_ap, q_ap, reads):
            ps_ = pS[pk[0] % 2]; pt_ = Pt[pk[0] % 3]; pk[0] += 1
            P.op("pe", lambda e: e.matmul(ps_[:, :], lhs_ap, q_ap, start=True, stop=True), reads=reads, writes=[ps_])
            P.op("act", lambda e: e.activation(out=pt_[:], in_=ps_[:, :], func=AF.Exp, scale=SCALE), reads=[ps_], writes=[pt_])
            return pt_

        def maskmul(pt_, mask_ap, rd):
            P.op("dve", lambda e: e.tensor_tensor(out=pt_[:, :].rearrange("p (g q) -> p g q", g=4), in0=pt_[:, :].rearrange("p (g q) -> p g q", g=4),
                                                  in1=mask_ap, op=ALU.mult), reads=[pt_] + rd, writes=[pt_])

        OTs = P.sb([65, 512], F32, "OTs"); UTs = P.sb([128, 512], F32, "UTs")
        Otok = [P.sb([128, 260], F32, f"Otok{i}") for i in range(3)]

        def pv(pt_, vbuf, kt, first, last):
            P.op("pe", lambda e: e.matmul(pOT[0:65, :], vbuf[:, kt, :], pt_[:, :], start=first, stop=last), reads=[pt_, vbuf], writes=[pOT])

        def finish_branch(br):
            P.op("act", lambda e: e.copy(out=OTs[:, :], in_=pOT[0:65, :]), reads=[pOT], writes=[OTs])
            for g in range(4):
                P.op("pe", lambda e, g=g: e.transpose(out=pX[:, g * 65:(g + 1) * 65], in_=OTs[0:65, g * 128:(g + 1) * 128], identity=idf[0:65, 0:65]),
                     reads=[OTs, idf], writes=[pX])
            P.op("act", lambda e: e.copy(out=Otok[br][:, :], in_=pX[:, 0:260]), reads=[pX], writes=[Otok[br]])

        for j in range(NJ):
            q_ = qb[j % 2]
            load_cast(q_, q_[:, :], qT[j], 128, 512)
            nts = list(range(ntmax(j) + 1))
            for nt in nts:
                pt_ = qk_exp(ckT[0:64, nt * 128:(nt + 1) * 128], q_[0:64, :], [ckT, q_])
                maskmul(pt_, bc_g(CMb, CMb[:, CM_IDX[(j, nt)], :]), [CMb])
                pv(pt_, cv_a, nt, nt == 0, nt == nts[-1])
                P.op("pe", lambda e, nt=nt, pt_=pt_: e.matmul(pUT[:, :], covb[:, nt, 0:128], pt_[:, :], start=(nt == 0), stop=(nt == nts[-1])),
                     reads=[pt_, covb], writes=[pUT])
            finish_branch(0)
            P.op("act", lambda e: e.copy(out=UTs[:, :], in_=pUT[:, :]), reads=[pUT], writes=[UTs])
            for g in range(4):
                P.op("pe", lambda e, g=g: e.transpose(out=pX[:, g * 128:(g + 1) * 128], in_=UTs[:, g * 128:(g + 1) * 128], identity=idf[:, :]),
                     reads=[UTs, idf], writes=[pX])
            P.op("dve", lambda e: e.tensor_scalar(out=rs[:, 0:4], in0=bcast_ap(Otok[0][:, 64:65], [[260, 128], [65, 4]]), scalar1=1e-30, scalar2=None, op0=ALU.max),
                 reads=[Otok[0]], writes=[rs])
            P.op("dve", lambda e: e.reciprocal(out=rs[:, 0:4], in_=rs[:, 0:4]), reads=[rs], writes=[rs])
            P.op("dve", lambda e: e.tensor_scalar(out=imp[:], in0=pX[:, 0:128], scalar1=rs[:, 0:1], scalar2=None, op0=ALU.mult), reads=[pX, rs], writes=[imp])
            for g in range(1, 4):
                P.op("dve", lambda e, g=g: e.scalar_tensor_tensor(out=imp[:], in0=pX[:, g * 128:(g + 1) * 128], scalar=rs[:, g:g + 1], in1=imp[:],
                                                                op0=ALU.mult, op1=ALU.add), reads=[pX, rs, imp], writes=[imp])
            P.op("dve", lambda e, j=j: e.tensor_tensor(out=imp[:], in0=imp[:], in1=addt[:, j, :], op=ALU.add), reads=[imp, addt], writes=[imp])
            P.op("dve", lambda e: e.max(out=m8[:, 0:8], in_=imp[:]), reads=[imp], writes=[m8])
            P.op("dve", lambda e: e.match_replace(out=sc2[:], in_to_replace=m8[:, 0:8], in_values=imp[:], imm_value=NEG), reads=[imp, m8], writes=[sc2])
            P.op("dve", lambda e: e.max(out=m8[:, 8:16], in_=sc2[:]), reads=[sc2], writes=[m8])
            P.op("dve", lambda e: e.tensor_scalar(out=selb[:], in0=imp[:], scalar1=m8[:, 15:16], scalar2=None, op0=ALU.is_ge), reads=[imp, m8], writes=[selb])
            P.op("pe", lambda e: e.transpose(out=pT[:, 0:128], in_=selb[:], identity=idb[:]), reads=[selb, idb], writes=[pT])
            P.op("act", lambda e: e.copy(out=selT[:], in_=pT[:, 0:128]), reads=[pT], writes=[selT])
            kts = list(range(2 * j + 2))
            for kt in kts:
                pt_ = qk_exp(KSW[0:64, kt * 128:(kt + 1) * 128], q_[0:64, :], [KSW, q_])
                w0 = 64 * (kt // 32)
                P.op("pe", lambda e, kt=kt, w0=w0: e.matmul(pM[:, 0:128], t3b[w0:w0 + 64, kt % 32, :], selT[w0:w0 + 64, :], start=True, stop=True),
                     reads=[t3b, selT], writes=[pM])
                maskmul(pt_, bcast_ap(pM[:, 0:1], [[pM[:, :].ap[0][0], 128], [0, 4], [1, 128]]), [pM])
                if kt >= 2 * j:
                    maskmul(pt_, bc_g(LMb, LMb[:, kt - 2 * j, :]), [LMb])
                pv(pt_, vs_a, kt, kt == 0, kt == kts[-1])
            finish_branch(1)
            wk = [(i, 2 * j - 4 + i) for i in range(6) if 2 * j - 4 + i >= 0]
            for i, kt in wk:
                pt_ = qk_exp(KSW[64:128, kt * 128:(kt + 1) * 128], q_[64:128, :], [KSW, q_])
                maskmul(pt_, bc_g(WMb, WMb[:, i, :]), [WMb])
                pv(pt_, vw_a, kt, (i, kt) == wk[0], (i, kt) == wk[-1])
            finish_branch(2)
            o_ = ot[j % 2]
            for br, po in enumerate(Otok):
                P.op("dve", lambda e, br=br, po=po: e.tensor_scalar(out=rs[:, br * 4:br * 4 + 4], in0=bcast_ap(po[:, 64:65], [[260, 128], [65, 4]]),
                                                                     scalar1=1e-30, scalar2=None, op0=ALU.max), reads=[po], writes=[rs])
                P.op("dve", lambda e, br=br: e.reciprocal(out=rs[:, br * 4:br * 4 + 4], in_=rs[:, br * 4:br * 4 + 4]), reads=[rs], writes=[rs])
                P.op("dve", lambda e, br=br, j=j: e.tensor_tensor(out=cf[:, br * 4:br * 4 + 4], in0=rs[:, br * 4:br * 4 + 4],
                                                                in1=bcast_ap(gtt[:, j, br:br + 1], [[NJ * 12, 128], [3, 4]]), op=ALU.mult), reads=[rs, gtt], writes=[cf])
                dst = o_ if br == 0 else otmp
                P.op("dve", lambda e, br=br, po=po, dst=dst: e.tensor_tensor(
                    out=dst[:, :].rearrange("p (g d) -> p g d", g=4), in0=bcast_ap(po[:, 0:1], [[260, 128], [65, 4], [1, 64]]),
                    in1=bcast_ap(cf[:, br * 4:br * 4 + 1], [[12, 128], [1, 4], [0, 64]]), op=ALU.mult), reads=[po, cf], writes=[dst])
                if br > 0:
                    P.op("pool", lambda e, o_=o_: e.tensor_tensor(out=o_[:], in0=o_[:], in1=otmp[:], op=ALU.add), reads=[o_, otmp], writes=[o_])
            P.dma("sp", o_p[:, j, :], o_[:], reads=[o_])

    if do_sample:
        smk = P.sb([128, 75], F32, "smk"); P.dma("sp", smk[:], s_msk[:, :], writes=[smk])
        sad = P.sb([1, 129], F32, "sad"); P.dma("sp", sad[:], s_add[:, :], writes=[sad])
        sqb = P.sb([128, 4], BF16, "sqb"); sqf = P.sb([128, 4], F32, "sqf")
        Pc = P.sb([128, 20], BF16, "Pc"); Pss = P.sb([128, 260], BF16, "Pss"); Pw = P.sb([128, 20], BF16, "Pw")
        Usb = P.sb([4, 132], F32, "Usb"); srs = P.sb([4, 4], F32, "srs"); simp = P.sb([1, 132], F32, "simp"); ssc2 = P.sb([1, 132], F32, "ssc2")
        sm8 = P.sb([1, 16], F32, "sm8"); ssel = P.sb([1, 132], BF16, "ssel"); selx = P.sb([1, TPK], BF16, "selx")
        P.op("dve", lambda e: e.memset(selx[:], 0.0), writes=[selx])
        mcol = P.sb([128, 65], F32, "mcol"); sgt = P.sb([4, 3], F32, "sgt"); scf = P.sb([4, 4], F32, "scf")
        so = P.sb([4, 64], F32, "so"); so2 = P.sb([4, 64], F32, "so2")
        for b in range(NSEQ):
            compress(s_kcv[b])
            P.dma("sp", sqf[:], s_q[b], writes=[sqf])
            P.op("dve", lambda e: e.tensor_copy(out=sqb[:], in_=sqf[:]), reads=[sqf], writes=[sqb])
            P.dma("sp", sgt[:], s_gt[b], writes=[sgt])
            P.op("act", lambda e: e.activation(out=sgt[:], in_=sgt[:], func=AF.Sigmoid), reads=[sgt], writes=[sgt])
            for nt in range(5):
                P.op("pe", lambda e, nt=nt: e.matmul(pS[0][:, nt * 4:nt * 4 + 4], ckT[0:64, nt * 128:(nt + 1) * 128], sqb[0:64, :], start=True, stop=True),
                     reads=[ckT, sqb], writes=[pS[0]])
            P.op("act", lambda e: e.activation(out=Pc[:], in_=pS[0][:, 0:20], func=AF.Exp, scale=SCALE), reads=[pS[0]], writes=[Pc])
            P.op("dve", lambda e: e.tensor_tensor(out=Pc[:, :].rearrange("p (t g) -> p t g", g=4), in0=Pc[:, :].rearrange("p (t g) -> p t g", g=4),
                                                  in1=bcast_ap(smk[:, 0:1], [[75, 128], [1, 5], [0, 4]]), op=ALU.mult), reads=[Pc, smk], writes=[Pc])
            for nt in range(5):
                P.op("pe", lambda e, nt=nt: e.matmul(pOc[0:4, 0:65], Pc[:, nt * 4:nt * 4 + 4], cv_a[:, nt, :], start=(nt == 0), stop=(nt == 4)),
                     reads=[Pc, cv_a], writes=[pOc])
                P.op("pe", lambda e, nt=nt: e.matmul(pU[0:4, 0:129], Pc[:, nt * 4:nt * 4 + 4], covb[:, nt, :], start=(nt == 0), stop=(nt == 4)),
                     reads=[Pc, covb], writes=[pU])
            P.op("dve", lambda e: e.reciprocal(out=srs[:, 0:1], in_=pOc[0:4, 64:65]), reads=[pOc], writes=[srs])
            P.op("act", lambda e: e.copy(out=Usb[:, 0:129], in_=pU[0:4, 0:129]), reads=[pU], writes=[Usb])
            P.op("pe", lambda e: e.matmul(pM[0:1, 0:129], srs[:, 0:1], Usb[:, 0:129], start=True, stop=True), reads=[srs, Usb], writes=[pM])
            P.op("dve", lambda e: e.tensor_tensor(out=simp[:, 0:129], in0=pM[0:1, 0:129], in1=sad[:, :], op=ALU.add), reads=[pM, sad], writes=[simp])
            P.op("dve", lambda e: e.max(out=sm8[:, 0:8], in_=simp[:, 0:129]), reads=[simp], writes=[sm8])
            P.op("dve", lambda e: e.match_replace(out=ssc2[:, 0:129], in_to_replace=sm8[:, 0:8], in_values=simp[:, 0:129], imm_value=NEG),
                 reads=[simp, sm8], writes=[ssc2])
            P.op("dve", lambda e: e.max(out=sm8[:, 8:16], in_=ssc2[:, 0:129]), reads=[ssc2], writes=[sm8])
            P.op("dve", lambda e: e.tensor_scalar(out=ssel[:, 0:129], in0=simp[:, 0:129], scalar1=sm8[:, 15:16], scalar2=None, op0=ALU.is_ge),
                 reads=[simp, sm8], writes=[ssel])
            P.op("dve", lambda e: e.tensor_copy(out=selx[:, 0:TP].rearrange("p (s k) -> p s k", k=64), in_=bcast_ap(ssel[:, 0:1], [[132, 1], [1, 129], [0, 64]])),
                 reads=[ssel], writes=[selx])
            for kt in range(65):
                P.op("pe", lambda e, kt=kt: e.matmul(pS[1][:, kt:kt + 1], selx[0:1, kt * 128:(kt + 1) * 128], one_bf[0:1, 0:1], start=True, stop=True),
                     reads=[selx, one_bf], writes=[pS[1]])
            P.op("dve", lambda e: e.tensor_tensor(out=mcol[:], in0=pS[1][:, 0:65], in1=smk[:, 5:70], op=ALU.mult), reads=[pS[1], smk], writes=[mcol])
            for c0 in range(0, TPK, 2048):
                w = min(2048, TPK - c0)
                load_cast(KSW, KSW[0:64, c0:c0 + w], s_ks[b, :, c0:c0 + w], 64, w)
            for a, na in ((0, 32), (32, 32), (64, 1)):
                load_cast(vs_a, vs_a[:, a:a + na, 0:64], s_vs[b, :, a:a + na, :], 128, (na, 64))
            for kt in range(65):
                P.op("pe", lambda e, kt=kt: e.matmul(pS[0][:, kt * 4:kt * 4 + 4], KSW[0:64, kt * 128:(kt + 1) * 128], sqb[0:64, :], start=True, stop=True),
                     reads=[KSW, sqb], writes=[pS[0]])
            P.op("act", lambda e: e.activation(out=Pss[:], in_=pS[0][:, 0:260], func=AF.Exp, scale=SCALE), reads=[pS[0]], writes=[Pss])
            P.op("dve", lambda e: e.tensor_tensor(out=Pss[:, :].rearrange("p (t g) -> p t g", g=4), in0=Pss[:, :].rearrange("p (t g) -> p t g", g=4),
                                                  in1=bcast_ap(mcol[:, 0:1], [[65, 128], [1, 65], [0, 4]]), op=ALU.mult), reads=[Pss, mcol], writes=[Pss])
            for kt in range(65):
                P.op("pe", lambda e, kt=kt: e.matmul(pOs[0:4, 0:65], Pss[:, kt * 4:kt * 4 + 4], vs_a[:, kt, :], start=(kt == 0), stop=(kt == 64)),
                     reads=[Pss, vs_a], writes=[pOs])
            load_cast(KSW, KSW[64:128, 0:640], s_kw[b], 64, 640, p0=64)
            load_cast(vw_a, vw_a[:, 0:5, 0:64], s_vw[b], 128, (5, 64))
            for kt in range(5):
                P.op("pe", lambda e, kt=kt: e.matmul(pS[1][:, 128 + kt * 4:128 + kt * 4 + 4], KSW[64:128, kt * 128:(kt + 1) * 128], sqb[64:128, :], start=True, stop=True),
                     reads=[KSW, sqb], writes=[pS[1]])
            P.op("act", lambda e: e.activation(out=Pw[:], in_=pS[1][:, 128:148], func=AF.Exp, scale=SCALE), reads=[pS[1]], writes=[Pw])
            P.op("dve", lambda e: e.tensor_tensor(out=Pw[:, :].rearrange("p (t g) -> p t g", g=4), in0=Pw[:, :].rearrange("p (t g) -> p t g", g=4),
                                                  in1=bcast_ap(smk[:, 70:71], [[75, 128], [1, 5], [0, 4]]), op=ALU.mult), reads=[Pw, smk], writes=[Pw])
            for kt in range(5):
                P.op("pe", lambda e, kt=kt: e.matmul(pOw[0:4, 0:65], Pw[:, kt * 4:kt * 4 + 4], vw_a[:, kt, :], start=(kt == 0), stop=(kt == 4)),
                     reads=[Pw, vw_a], writes=[pOw])
            for br, po in enumerate((pOc, pOs, pOw)):
                P.op("dve", lambda e, br=br, po=po: e.reciprocal(out=srs[:, br + 1:br + 2], in_=po[0:4, 64:65]), reads=[po], writes=[srs])
                P.op("dve", lambda e, br=br: e.tensor_tensor(out=scf[:, br:br + 1], in0=srs[:, br + 1:br + 2], in1=sgt[:, br:br + 1], op=ALU.mult),
                     reads=[srs, sgt], writes=[scf])
                if br == 0:
                    P.op("dve", lambda e, po=po: e.tensor_scalar(out=so[:], in0=po[0:4, 0:64], scalar1=scf[:, 0:1], scalar2=None, op0=ALU.mult),
                         reads=[po, scf], writes=[so])
                else:
                    P.op("dve", lambda e, br=br, po=po: e.scalar_tensor_tensor(out=so[:], in0=po[0:4, 0:64], scalar=scf[:, br:br + 1], in1=so[:],
                                                                             op0=ALU.mult, op1=ALU.add), reads=[po, scf, so], writes=[so])
            P.op("act", lambda e: e.copy(out=so2[:], in_=so[:]), reads=[so], writes=[so2])
            P.dma("sp", o_s[b], so2[:], reads=[so2])
    P.emit()
    P.close()
    return nc


def core_consts(c):
    half = c // 4
    kl = np.arange(128)[:, None]; ql = np.arange(128)[None, :]
    tri = (kl <= ql).astype(np.float32); tri2 = (kl >= ql).astype(np.float32)
    one = np.ones((128, 128), np.float32); zero = np.zeros((128, 128), np.float32)
    LM = [tri, zero] if half == 0 else [one, tri]
    WM = [tri2, one, one, one, tri, zero] if half == 0 else [zero, tri2, one, one, one, tri]
    CM = np.zeros((128, NCM, 128), np.float32)
    ADD = np.zeros((128, NJ, 128), np.float32)
    blk = np.arange(128)[None, :]; qq = np.arange(128)[:, None]
    for j in range(NJ):
        t = 2 * j + half
        for nt in range(ntmax(j) + 1):
            CM[:, CM_IDX[(j, nt)], :] = (16 * (128 * nt + kl) + 31 <= 128 * t + ql)
        a = np.zeros((128, 128), np.float32)
        a = np.where(blk > 2 * t + 1, -1.0, a)
        a = np.where(blk == 2 * t + 1, np.where(qq >= 64, 1e9, -1.0), a)
        a = np.where(blk == 2 * t, 1e9, a)
        a = np.where((blk == 2 * t - 1) & (qq < 64), 1e9, a)
        a = np.where(blk == 0, 1e9, a)
        ADD[:, j, :] = a
    return (np.stack(LM, 1).astype(np.float32), np.stack(WM, 1).astype(np.float32), CM, ADD)


def shared_consts():
    n = np.arange(640)[:, None]; s_ = np.arange(129)[None, :]
    cov = ((16 * n < 64 * s_ + 64) & (16 * n + 32 > 64 * s_)).astype(np.float32)
    cover = cov.reshape(5, 128, 129).transpose(1, 0, 2)
    b = np.arange(128)[:, None, None]; m = np.arange(32)[None, :, None]; k = np.arange(128)[None, None, :]
    t3 = ((b % 64) == 2 * m + k // 64).astype(np.float32)
    pl = np.arange(128)[:, None]
    msk = np.concatenate([(128 * np.arange(5)[None, :] + pl <= 510), (128 * np.arange(65)[None, :] + pl <= 8192),
                          (128 * np.arange(5)[None, :] + pl <= 512)], axis=1).astype(np.float32)
    sadd = np.zeros((1, 129), np.float32); sadd[0, [0, 127, 128]] = 1e9
    return cover, t3, msk, sadd


def run_C(o, gcmp, gslc, st_win, cmp_w, cmp_pe, kg0, do_prompt=True, do_sample=True):
    nc = build_C(do_prompt, do_sample)
    f = lambda a: np.ascontiguousarray(a, dtype=np.float32)
    cover, t3, msk, sadd = shared_consts()
    ident = np.eye(128, dtype=np.float32)
    pq = o["p_q"].reshape(64, 128, 16, 64); pgt = o["p_gt"].reshape(64, 128, 4, 4, 3)
    pkc = o["p_kvc"].reshape(8192, 2, 4, 64); pks = o["p_kvs"].reshape(8192, 2, 4, 64); pkw = o["p_kvw"].reshape(8192, 2, 4, 64)
    sq = o["s_q"].reshape(32, 16, 64); sgt = o["s_gt"].reshape(32, 4, 4, 3)
    skc = o["s_kvc"].reshape(32, 2, 4, 64); sks = o["s_kvs"].reshape(32, 2, 4, 64); skw = o["s_kvw"].reshape(32, 2, 4, 64)
    stw = st_win.reshape(32, 512, 2, 4, 64)
    maps = []
    for c in range(8):
        h = c % 4; half = c // 4
        tj = 2 * np.arange(NJ) + half
        LM, WM, CM, ADD = core_consts(c)
        m = {"cover": f(cover), "t3": f(t3), "ident": ident, "LM": LM, "WM": WM, "CM": CM, "ADD": ADD, "s_msk": msk, "s_add": sadd,
             "gk0": f(kg0[:, None])}
        q4 = pq[tj][:, :, 4 * h:4 * h + 4]
        qT = q4.transpose(0, 3, 2, 1).reshape(NJ, 64, 512)
        m["qT"] = f(np.concatenate([qT, qT], axis=1))
        m["gts"] = f(pgt[tj][:, :, h].transpose(1, 0, 2, 3).reshape(128, NJ, 12))
        kcv = np.zeros((128, TP), np.float32)
        kcv[0:64, :8192] = pkc[:, 0, h].T; kcv[64:128, :8192] = pkc[:, 1, h].T
        m["kcv"] = kcv
        m["ksw"] = f(np.concatenate([pks[:, 0, h].T, pkw[:, 0, h].T], axis=0))
        m["vs"] = f(pks[:, 1, h].reshape(64, 128, 64).transpose(1, 0, 2))
        m["vw"] = f(pkw[:, 1, h].reshape(64, 128, 64).transpose(1, 0, 2))
        m["wkv"] = f(np.concatenate([cmp_w[0].transpose(1, 0, 2), cmp_w[1].transpose(1, 0, 2)], axis=0))
        m["peT"] = f(np.concatenate([cmp_pe[:, 0, :].T, cmp_pe[:, 1, :].T], axis=0))
        bs = np.arange(16 * half, 16 * half + 16)
        s_kcv = np.zeros((NSEQ, 128, TP), np.float32); s_ks = np.zeros((NSEQ, 64, TPK), np.float32)
        s_vs = np.zeros((NSEQ, TPK, 64), np.float32); s_kw = np.zeros((NSEQ, 64, 640), np.float32); s_vw = np.zeros((NSEQ, 640, 64), np.float32)
        for i, b in enumerate(bs):
            s_kcv[i, 0:64, :8192] = gcmp[b, :, 0, h].T; s_kcv[i, 0:64, 8192] = skc[b, 0, h]
            s_kcv[i, 64:128, :8192] = gcmp[b, :, 1, h].T; s_kcv[i, 64:128, 8192] = skc[b, 1, h]
            s_ks[i, :, :8192] = gslc[b, :, 0, h].T; s_ks[i, :, 8192] = sks[b, 0, h]
            s_vs[i, :8192] = gslc[b, :, 1, h]; s_vs[i, 8192] = sks[b, 1, h]
            s_kw[i, :, :512] = stw[b, :, 0, h].T; s_kw[i, :, 512] = skw[b, 0, h]
            s_vw[i, :512] = stw[b, :, 1, h]; s_vw[i, 512] = skw[b, 1, h]
        m["s_kcv"] = s_kcv; m["s_ks"] = s_ks
        m["s_vs"] = f(s_vs.reshape(NSEQ, 65, 128, 64).transpose(0, 2, 1, 3))
        m["s_kw"] = s_kw; m["s_vw"] = f(s_vw.reshape(NSEQ, 5, 128, 64).transpose(0, 2, 1, 3))
        sqT = sq[bs][:, 4 * h:4 * h + 4].transpose(0, 2, 1)
        m["s_q"] = f(np.concatenate([sqT, sqT], axis=1))
        m["s_gt"] = f(sgt[bs][:, h])
        maps.append(m)
    res = run_bass_kernel_spmd(nc, maps, core_ids=list(range(8)))
    op = np.zeros((64, 128, 16, 64), np.float32); os_ = np.zeros((32, 16, 64), np.float32)
    for c, r in enumerate(res.results):
        h = c % 4; half = c // 4
        tj = 2 * np.arange(NJ) + half
        oc = r["o_p"].reshape(128, NJ, 4, 64).transpose(1, 0, 2, 3)
        op[tj, :, 4 * h:4 * h + 4] = oc
        os_[16 * half:16 * half + 16, 4 * h:4 * h + 4] = r["o_s"]
    return op.reshape(8192, 1024), os_.reshape(32, 1024)


DFF = 5632
NQ = 11
FCH = 44 // NQ


def tiles_to_T(P, src, width, col0, dstT, kc0, gbc, bufs, idb, pst, norm):
    xts, xnb, sq, ssq, rstd = bufs
    nkc = width // 128
    for ti, (r0, rows) in enumerate(TILES):
        xt = xts[ti % 2]; xb = xnb[ti % 2]
        P.dma("sp", xt[:rows, :width], src[r0:r0 + rows, col0:col0 + width], writes=[xt])
        if norm:
            rms_rows(P, xt, rows, width, gbc, xb, sq, ssq, rstd)
        else:
            P.op("dve", lambda e, xt=xt, xb=xb, rows=rows: e.tensor_copy(out=xb[:rows, :width], in_=xt[:rows, :width]),
                 reads=[xt], writes=[xb])
        for b0 in range(0, nkc, 8):
            pt = pst[(b0 // 8) % 2]
            nb = min(8, nkc - b0)
            for j in range(nb):
                kc = b0 + j
                P.op("pe", lambda e, pt=pt, j=j, kc=kc, xb=xb, rows=rows: e.transpose(
                    out=pt[:, j * 128:j * 128 + rows], in_=xb[:rows, kc * 128:(kc + 1) * 128], identity=idb[:rows, :rows]),
                    reads=[xb, idb], writes=[pt])
            src_ap = pt[:, :].rearrange("p (a b) -> p a b", b=128)[:, :nb, :rows]
            dst = dstT[:, kc0 + b0:kc0 + b0 + nb, r0:r0 + rows]
            if (b0 // 8) % 2 == 0:
                P.op("act", lambda e, s=src_ap, d=dst: e.copy(out=d, in_=s), reads=[pt], writes=[dstT])
            else:
                P.op("dve", lambda e, s=src_ap, d=dst: e.tensor_copy(out=d, in_=s), reads=[pt], writes=[dstT])


def build_D():
    nc = bass.Bass("TRN2", target_bir_lowering=False)
    yg = nc.dram_tensor("yg", [NTOK, 1024], F32, kind="ExternalInput").ap()
    oa = nc.dram_tensor("oa", [NTOK, 1024], F32, kind="ExternalInput").ap()
    h = nc.dram_tensor("h", [NTOK, D], F32, kind="ExternalInput").ap()
    gs = nc.dram_tensor("gs", [1024], F32, kind="ExternalInput").ap()
    g2 = nc.dram_tensor("g2", [D], F32, kind="ExternalInput").ap()
    Wo = nc.dram_tensor("Wo", [D, D], F32, kind="ExternalInput").ap()
    Wgu = nc.dram_tensor("Wgu", [D, 2 * DFF], F32, kind="ExternalInput").ap()
    Wd = nc.dram_tensor("Wd", [DFF, D], F32, kind="ExternalInput").ap()
    ident = nc.dram_tensor("ident", [128, 128], F32, kind="ExternalInput").ap()
    hout = nc.dram_tensor("hout", [NTOK, D], F32, kind="ExternalOutput").ap()
    P = Prog(nc)
    gsbc = P.sb([128, 1024], F32, "gsbc")
    g2bc = P.sb([128, D], F32, "g2bc")
    idf = P.sb([128, 128], F32, "idf")
    idb = P.sb([128, 128], BF16, "idb")
    xT = P.sb([128, 16, NTOK], BF16, "xT")
    h1 = P.sb([128, 9, D], F32, "h1")
    xt0 = P.sb([128, D], F32, "xt0"); xts = [xt0, xt0]
    xnb = [P.sb([128, D], BF16, f"xnb{i}") for i in range(2)]
    sq = P.sb([128, D], BF16, "sq")
    ssq = P.sb([128, 1], F32, "ssq")
    rstd = P.sb([128, 1], F32, "rstd")
    bufs = (xts, xnb, sq, ssq, rstd)
    pst = [P.ps([128, 1024], BF16, f"pst{i}") for i in range(2)]
    psm = [P.ps([128, 512], F32, f"psm{i}") for i in range(6)]
    wst = [P.sb([128, 4, 512], F32, f"wst{i}") for i in range(2)]
    wo_bf = P.sb([128, 16, 512], BF16, "wo_bf")
    wg_bf = [P.sb([128, 16, 128], BF16, f"wg_bf{i}") for i in range(2)]
    wv_bf = [P.sb([128, 16, 128], BF16, f"wv_bf{i}") for i in range(2)]
    actT = P.sb([128, FCH, NTOK], BF16, "actT")
    sg = [P.sb([128, 512], F32, f"sg{i}") for i in range(2)]

    P.dma("sp", gsbc[:], gs.partition_broadcast(128), writes=[gsbc])
    P.dma("sp", g2bc[:], g2.partition_broadcast(128), writes=[g2bc])
    P.dma("sp", idf[:], ident[:, :], writes=[idf])
    P.op("dve", lambda e: e.tensor_copy(out=idb[:], in_=idf[:]), reads=[idf], writes=[idb])
    cnt = [0]

    def load_w(dst, src_rows_view, nk, cw):
        for k0 in range(0, nk, 4):
            nkk = min(4, nk - k0)
            ws = wst[cnt[0] % 2]
            P.dma("sp", ws[:, :nkk, :cw], src_rows_view[:, k0:k0 + nkk, :], writes=[ws])
            eng = ("pool", "dve", "pool", "act")[cnt[0] % 4]
            if eng == "act":
                P.op("act", lambda e, ws=ws, k0=k0, nkk=nkk: e.copy(out=dst[:, k0:k0 + nkk, :cw], in_=ws[:, :nkk, :cw]), reads=[ws], writes=[dst])
            else:
                P.op(eng, lambda e, ws=ws, k0=k0, nkk=nkk: e.tensor_copy(out=dst[:, k0:k0 + nkk, :cw], in_=ws[:, :nkk, :cw]), reads=[ws], writes=[dst])
            cnt[0] += 1

    tiles_to_T(P, yg, 1024, 0, xT, 0, gsbc, bufs, idb, pst, True)
    tiles_to_T(P, oa, 1024, 0, xT, 8, None, bufs, idb, pst, False)
    k = 0
    for cg in range(4):
        load_w(wo_bf, Wo[:, cg * 512:(cg + 1) * 512].rearrange("(kc p) c -> p kc c", p=128), 16, 512)
        for ti, (r0, rows) in enumerate(TILES):
            pm = psm[k % 6]; k += 1
            for kc in range(16):
                P.op("pe", lambda e, pm=pm, kc=kc, r0=r0, rows=rows: e.matmul(
                    pm[:rows, :], xT[:, kc, r0:r0 + rows], wo_bf[:, kc, :], start=(kc == 0), stop=(kc == 15)),
                    reads=[xT, wo_bf], writes=[pm])
            xt = xts[k % 2]
            P.dma("sp", xt[:rows, :512], h[r0:r0 + rows, cg * 512:(cg + 1) * 512], writes=[xt])
            P.op("dve", lambda e, pm=pm, xt=xt, ti=ti, rows=rows, cg=cg: e.tensor_tensor(
                out=h1[:rows, ti, cg * 512:(cg + 1) * 512], in0=pm[:rows, :], in1=xt[:rows, :512], op=ALU.add),
                reads=[pm, xt], writes=[h1])
    for ti, (r0, rows) in enumerate(TILES):
        xb = xnb[ti % 2]
        h1t = Buf(h1.t, "h1v")
        P.op("act", lambda e, ti=ti, rows=rows: e.activation(out=sq[:rows, :], in_=h1[:rows, ti, :], func=AF.Square, accum_out=ssq[:rows, :]),
             reads=[h1], writes=[sq, ssq])
        P.op("dve", lambda e, rows=rows: e.tensor_scalar(out=rstd[:rows, :], in0=ssq[:rows, :], scalar1=1.0 / D, scalar2=1e-6, op0=ALU.mult, op1=ALU.add),
             reads=[ssq], writes=[rstd])
        P.op("act", lambda e, rows=rows: e.activation(out=rstd[:rows, :], in_=rstd[:rows, :], func=AF.Sqrt), reads=[rstd], writes=[rstd])
        P.op("dve", lambda e, rows=rows: e.reciprocal(out=rstd[:rows, :], in_=rstd[:rows, :]), reads=[rstd], writes=[rstd])
        P.op("dve", lambda e, ti=ti, rows=rows, xb=xb: e.scalar_tensor_tensor(out=xb[:rows, :], in0=h1[:rows, ti, :], scalar=rstd[:rows, :],
                                                                      in1=g2bc[:rows, :], op0=ALU.mult, op1=ALU.mult),
             reads=[h1, rstd, g2bc], writes=[xb])
        for half in range(2):
            pt = pst[half]
            for j in range(8):
                kc = half * 8 + j
                P.op("pe", lambda e, pt=pt, j=j, kc=kc, xb=xb, rows=rows: e.transpose(
                    out=pt[:, j * 128:j * 128 + rows], in_=xb[:rows, kc * 128:(kc + 1) * 128], identity=idb[:rows, :rows]),
                    reads=[xb, idb], writes=[pt])
            src_ap = pt[:, :].rearrange("p (a b) -> p a b", b=128)[:, :, :rows]
            dst = xT[:, half * 8:half * 8 + 8, r0:r0 + rows]
            if half == 0:
                P.op("act", lambda e, s=src_ap, d=dst: e.copy(out=d, in_=s), reads=[pt], writes=[xT])
            else:
                P.op("dve", lambda e, s=src_ap, d=dst: e.tensor_copy(out=d, in_=s), reads=[pt], writes=[xT])
    TG = [(0, 512), (512, 512), (1024, 4)]
    for qi in range(NQ):
        for fi in range(FCH):
            fc = qi * FCH + fi
            wg = wg_bf[fc % 2]; wv = wv_bf[fc % 2]
            load_w(wg, Wgu[:, fc * 128:(fc + 1) * 128].rearrange("(kc p) c -> p kc c", p=128), 16, 128)
            load_w(wv, Wgu[:, DFF + fc * 128:DFF + (fc + 1) * 128].rearrange("(kc p) c -> p kc c", p=128), 16, 128)
            for gi, (t0, tw) in enumerate(TG):
                pg = psm[(2 * gi) % 6]; pv = psm[(2 * gi + 1) % 6]
                for kc in range(16):
                    P.op("pe", lambda e, pg=pg, kc=kc, wg=wg, t0=t0, tw=tw: e.matmul(
                        pg[:, :tw], wg[:, kc, :], xT[:, kc, t0:t0 + tw], start=(kc == 0), stop=(kc == 15)),
                        reads=[xT, wg], writes=[pg])
                for kc in range(16):
                    P.op("pe", lambda e, pv=pv, kc=kc, wv=wv, t0=t0, tw=tw: e.matmul(
                        pv[:, :tw], wv[:, kc, :], xT[:, kc, t0:t0 + tw], start=(kc == 0), stop=(kc == 15)),
                        reads=[xT, wv], writes=[pv])
                s = sg[gi % 2]
                P.op("act", lambda e, pg=pg, s=s, tw=tw: e.activation(out=s[:, :tw], in_=pg[:, :tw], func=AF.Silu), reads=[pg], writes=[s])
                P.op("dve", lambda e, pv=pv, s=s, fi=fi, t0=t0, tw=tw: e.tensor_tensor(
                    out=actT[:, fi, t0:t0 + tw], in0=s[:, :tw], in1=pv[:, :tw], op=ALU.mult), reads=[s, pv], writes=[actT])
        for cg in range(4):
            load_w(wo_bf, Wd[qi * FCH * 128:(qi + 1) * FCH * 128, cg * 512:(cg + 1) * 512].rearrange("(kc p) c -> p kc c", p=128), FCH, 512)
            for ti, (r0, rows) in enumerate(TILES):
                pm = psm[k % 6]; k += 1
                for fi in range(FCH):
                    P.op("pe", lambda e, pm=pm, fi=fi, r0=r0, rows=rows: e.matmul(
                        pm[:rows, :], actT[:, fi, r0:r0 + rows], wo_bf[:, fi, :], start=(fi == 0), stop=(fi == FCH - 1)),
                        reads=[actT, wo_bf], writes=[pm])
                P.op("dve", lambda e, pm=pm, ti=ti, rows=rows, cg=cg: e.tensor_tensor(
                    out=h1[:rows, ti, cg * 512:(cg + 1) * 512], in0=pm[:rows, :], in1=h1[:rows, ti, cg * 512:(cg + 1) * 512], op=ALU.add),
                    reads=[pm, h1], writes=[h1])
    for ti, (r0, rows) in enumerate(TILES):
        P.dma("sp", hout[r0:r0 + rows, :], h1[:rows, ti, :], reads=[h1])
    P.emit()
    P.close()
    return nc


def run_D(yg_p, yg_s, oa_p, oa_s, h_p, h_s, gs, g2, Wo, Wgu, Wd):
    nc = build_D()
    ident = np.eye(128, dtype=np.float32)
    f = lambda a: np.ascontiguousarray(a, dtype=np.float32)
    maps = []
    for c in range(8):
        cat = lambda p, s: f(np.concatenate([p[c * 1024:(c + 1) * 1024], s[c * 4:(c + 1) * 4]], axis=0))
        maps.append({"yg": cat(yg_p, yg_s), "oa": cat(oa_p, oa_s), "h": cat(h_p, h_s), "gs": f(gs), "g2": f(g2),
                     "Wo": f(Wo), "Wgu": f(Wgu), "Wd": f(Wd), "ident": ident})
    res = run_bass_kernel_spmd(nc, maps, core_ids=list(range(8)))
    hp = np.concatenate([r["hout"][:1024] for r in res.results], axis=0)
    hs = np.concatenate([r["hout"][1024:] for r in res.results], axis=0)
    return hp, hs


def kernel(x_prompt, x_sample, cache_cmp_kv, cache_slc_kv, state_win_kv, state_ssm, state_conv, page_table,
           norm1_g, w_in, conv_w, conv_b, dt_bias, a_log, d_skip, ssm_norm_g, q_norm_g, k_norm_g, cmp_pe, cmp_w,
           w_out, norm2_g, w_gu, w_down):
    A = lambda a: np.asarray(a)
    hp = A(x_prompt)[0].astype(np.float32, copy=False); hs = A(x_sample)[:, 0].astype(np.float32, copy=False)
    pt = A(page_table)
    gath = run_G([A(cache_cmp_kv)[0], A(cache_slc_kv)[0], A(cache_cmp_kv)[1], A(cache_slc_kv)[1]], pt)
    depth = 2
    cmp_p = np.zeros((depth, 1, 8192, 2, 4, 64), np.float32); cmp_s = np.zeros((depth, 32, 1, 2, 4, 64), np.float32)
    slc_p = np.zeros_like(cmp_p); slc_s = np.zeros_like(cmp_s)
    win_p = np.zeros((depth, 1, 512, 2, 4, 64), np.float32); win_s = np.zeros((depth, 32, 512, 2, 4, 64), np.float32)
    ssm_p = np.zeros((depth, 1, 16, 64, 128), np.float32); ssm_s = np.zeros((depth, 32, 16, 64, 128), np.float32)
    conv_p = np.zeros((depth, 1, 3, 2048), np.float32); conv_s = np.zeros((depth, 32, 3, 2048), np.float32)
    for l in range(depth):
        o = run_A(hp, hs, A(norm1_g)[l], A(w_in)[l], A(q_norm_g)[l], A(k_norm_g)[l], A(state_win_kv)[l], A(state_conv)[l])
        ygp, sp, ygs, ss = run_B(o["p_z"], o["p_xbc"], o["p_dt"], o["s_z"], o["s_xbc"], o["s_dt"], A(state_conv)[l], A(state_ssm)[l],
                                 A(conv_w)[l], A(conv_b)[l], A(dt_bias)[l], A(a_log)[l], A(d_skip)[l])
        oap, oas = run_C(o, gath[2 * l], gath[2 * l + 1], A(state_win_kv)[l], A(cmp_w)[l], A(cmp_pe)[l], A(k_norm_g)[l][0])
        cmp_p[l, 0] = o["p_kvc"].reshape(8192, 2, 4, 64); cmp_s[l, :, 0] = o["s_kvc"].reshape(32, 2, 4, 64)
        slc_p[l, 0] = o["p_kvs"].reshape(8192, 2, 4, 64); slc_s[l, :, 0] = o["s_kvs"].reshape(32, 2, 4, 64)
        win_p[l, 0] = o["p_kvw"][-512:].reshape(512, 2, 4, 64)
        win_s[l] = np.concatenate([o["sw_keep"], o["s_kvw"][:, None]], axis=1).reshape(32, 512, 2, 4, 64)
        ssm_p[l, 0] = sp; ssm_s[l] = ss
        conv_p[l, 0] = o["p_xbc"][-3:]
        conv_s[l] = np.concatenate([o["sc_keep"], o["s_xbc"][:, None]], axis=1)
        hp, hs = run_D(ygp, ygs, oap, oas, hp, hs, A(ssm_norm_g)[l], A(norm2_g)[l], A(w_out)[l], A(w_gu)[l], A(w_down)[l])
    return (hp[None].astype(np.float32), hs[:, None].astype(np.float32), cmp_p, cmp_s, slc_p, slc_s, win_p, win_s, ssm_p, ssm_s, conv_p, conv_s)
```

```python
from concourse.bass_utils import run_bass_kernel_spmd
from contextlib import ExitStack
import numpy as np
import concourse.bass as bass
import concourse.mybir as mybir

F32 = mybir.dt.float32
BF16 = mybir.dt.bfloat16
I32 = mybir.dt.int32
U32 = mybir.dt.uint32
AF = mybir.ActivationFunctionType
ALU = mybir.AluOpType
AX = mybir.AxisListType

ENGS = ("pe", "act", "dve", "pool", "sp")


class Buf:
    def __init__(self, t, name=""):
        self.t = t
        self.name = name
        self.w = None
        self.r = {}

    def __getitem__(self, idx):
        return self.t[idx]


class Prog:
    def __init__(self, nc, n_dma_sems=12, self_sync=True):
        self.nc = nc
        self.st = ExitStack()
        self.ops = {e: [] for e in ENGS}
        self.cnt = {e: 0 for e in ENGS}
        self.waited = {e: {} for e in ENGS}
        self.sem = {}
        for e in ENGS:
            if e != "sp":
                self.sem[e] = self.st.enter_context(nc.semaphore("c_" + e))
        self.dq = {}
        for q in ("sp", "act", "pool"):
            sems = [self.st.enter_context(nc.semaphore(f"d_{q}{i}")) for i in range(n_dma_sems)]
            self.dq[q] = {"sems": sems, "m": 0}
        self.self_sync = self_sync
        self.nalloc = 0

    def sb(self, shape, dt, name=None):
        self.nalloc += 1
        name = name or f"sb{self.nalloc}"
        return Buf(self.st.enter_context(self.nc.sbuf_tensor(name, list(shape), dt)), name)

    def ps(self, shape, dt, name=None):
        self.nalloc += 1
        name = name or f"ps{self.nalloc}"
        return Buf(self.st.enter_context(self.nc.psum_tensor(name, list(shape), dt)), name)

    def dram(self, name, shape, dt, kind="Internal"):
        return Buf(self.nc.dram_tensor(name, list(shape), dt, kind=kind).ap(), name)

    def _deps(self, eng, reads, writes):
        waits = []

        def need(ev):
            if ev is None:
                return
            sem, val, key = ev
            if key == eng and (eng == "pe" or not self.self_sync):
                return
            if self.waited[eng].get(key, 0) >= val:
                return
            self.waited[eng][key] = val
            waits.append((sem, val))

        for b in reads:
            need(b.w)
        for b in writes:
            need(b.w)
            for ev in b.r.values():
                need(ev)
        return waits

    def _commit(self, ev, reads, writes):
        for b in reads:
            b.r[ev[2]] = ev
        for b in writes:
            b.w = ev
            b.r = {}

    def op(self, eng, fn, reads=(), writes=()):
        waits = self._deps(eng, reads, writes)
        self.cnt[eng] += 1
        ev = (self.sem[eng], self.cnt[eng], eng)
        self.ops[eng].append((waits, fn, (self.sem[eng], 1)))
        self._commit(ev, reads, writes)
        return ev

    def dma(self, q, out, in_, reads=(), writes=(), indirect=None, **kw):
        D = self.dq[q]
        m = D["m"]
        D["m"] += 1
        ns = len(D["sems"])
        sem = D["sems"][m % ns]
        key = f"d_{q}{m % ns}"
        tgt = 16 * (m // ns + 1)
        D.setdefault("tg", {})[m % ns] = tgt
        waits = self._deps(q, reads, writes)
        if m >= ns and self.waited[q].get(key, 0) < tgt - 16:
            self.waited[q][key] = tgt - 16
            waits.append((sem, tgt - 16))
        if indirect is None:
            fn = lambda e: e.dma_start(out=out, in_=in_, **kw)
        else:
            fn = indirect
        self.ops[q].append((waits, fn, (sem, 16)))
        ev = (sem, tgt, key)
        self._commit(ev, reads, writes)
        return ev

    def barrier(self):
        evs = []
        for e in ENGS:
            if e != "sp" and self.cnt[e] > 0:
                evs.append((self.sem[e], self.cnt[e], e))
        for q, D in self.dq.items():
            for i, tg in D.get("tg", {}).items():
                evs.append((D["sems"][i], tg, f"d_{q}{i}"))
        for e in ENGS:
            waits = []
            for sem, val, key in evs:
                if self.waited[e].get(key, 0) >= val:
                    continue
                self.waited[e][key] = val
                waits.append((sem, val))
            if waits:
                self.ops[e].append((waits, None, None))

    def emit(self):
        nc = self.nc
        self.barrier()
        ops = self.ops

        def run(name, e):
            for waits, fn, inc in ops[name]:
                for sem, val in waits:
                    e.wait_ge(sem, val)
                if fn is not None:
                    ins = fn(e)
                    ins.then_inc(inc[0], inc[1])

        with nc.Block() as block:
            @block.tensor
            def _(e):
                run("pe", e)

            @block.scalar
            def _(e):
                run("act", e)

            @block.vector
            def _(e):
                run("dve", e)

            @block.gpsimd
            def _(e):
                run("pool", e)

            @block.sync
            def _(e):
                run("sp", e)
        self.ops = {e: [] for e in ENGS}

    def close(self):
        self.st.close()


def bcast_ap(ap, dims):
    return bass.AP(ap.tensor, ap.offset, dims)


D = 2048
NTOK = 1028
TILES = [(i * 128, 128) for i in range(8)] + [(1024, 4)]
EPS = 1e-6


def rms_rows(P, xt, rows, width, gbc, out_bf, sq_scr, ssq, rstd):
    P.op("act", lambda e: e.activation(out=sq_scr[:rows, :width], in_=xt[:rows, :width], func=AF.Square,
                                       accum_out=ssq[:rows, :]),
         reads=[xt], writes=[sq_scr, ssq])
    P.op("dve", lambda e: e.tensor_scalar(out=rstd[:rows, :], in0=ssq[:rows, :], scalar1=1.0 / width, scalar2=EPS,
                                          op0=ALU.mult, op1=ALU.add), reads=[ssq], writes=[rstd])
    P.op("act", lambda e: e.activation(out=rstd[:rows, :], in_=rstd[:rows, :], func=AF.Sqrt), reads=[rstd], writes=[rstd])
    P.op("dve", lambda e: e.reciprocal(out=rstd[:rows, :], in_=rstd[:rows, :]), reads=[rstd], writes=[rstd])
    P.op("dve", lambda e: e.scalar_tensor_tensor(out=out_bf[:rows, :width], in0=xt[:rows, :width], scalar=rstd[:rows, :],
                                                 in1=gbc[:rows, :width], op0=ALU.mult, op1=ALU.mult),
         reads=[xt, rstd, gbc], writes=[out_bf])


def build_A(ncols, norm_groups=None):
    norm_groups = norm_groups or {}
    nc = bass.Bass("TRN2", target_bir_lowering=False)
    gains = nc.dram_tensor("gains", [3 * 64], F32, kind="ExternalInput").ap()
    sw_in = nc.dram_tensor("sw_in", [4, 512, 512], F32, kind="ExternalInput").ap()
    sc_in = nc.dram_tensor("sc_in", [4, 3, 2048], F32, kind="ExternalInput").ap()
    sw_out = nc.dram_tensor("sw_out", [4, 511, 512], F32, kind="ExternalOutput").ap()
    sc_out = nc.dram_tensor("sc_out", [4, 2, 2048], F32, kind="ExternalOutput").ap()
    h = nc.dram_tensor("h", [NTOK, D], F32, kind="ExternalInput").ap()
    g = nc.dram_tensor("g", [D], F32, kind="ExternalInput").ap()
    W = nc.dram_tensor("W", [D, ncols], F32, kind="ExternalInput").ap()
    ident = nc.dram_tensor("ident", [128, 128], F32, kind="ExternalInput").ap()
    u = nc.dram_tensor("u", [NTOK, ncols], F32, kind="ExternalOutput").ap()
    P = Prog(nc)
    gbc = P.sb([128, D], F32, "gbc")
    idf = P.sb([128, 128], F32, "idf")
    idb = P.sb([128, 128], BF16, "idb")
    xnT = P.sb([128, 16, NTOK], BF16, "xnT")
    xts = [P.sb([128, D], F32, f"xt{i}") for i in range(2)]
    xnb = [P.sb([128, D], BF16, f"xnb{i}") for i in range(2)]
    sq = P.sb([128, D], BF16, "sq")
    ssq = P.sb([128, 1], F32, "ssq")
    rstd = P.sb([128, 1], F32, "rstd")
    pst = [P.ps([128, 1024], BF16, f"pst{i}") for i in range(2)]
    psm = [P.ps([128, 512], F32, f"psm{i}") for i in range(4)]
    wst = [P.sb([128, 16, 512], F32, f"wst{i}") for i in range(2)]
    wbf = [P.sb([128, 16, 512], BF16, f"wbf{i}") for i in range(2)]
    ot = [P.sb([128, 512], F32, f"ot{i}") for i in range(4)]
    gn = P.sb([128, 3, 64], F32, "gn")
    nsq = P.sb([128, 512], F32, "nsq"); ns8 = P.sb([128, 8], F32, "ns8")
    P.dma("sp", gn[:], gains.partition_broadcast(128), writes=[gn])
    for b in range(4):
        P.dma("act", sw_out[b], sw_in[b, 1:512, :])
        P.dma("act", sc_out[b], sc_in[b, 1:3, :])

    P.dma("sp", gbc[:], g.partition_broadcast(128), writes=[gbc])
    P.dma("sp", idf[:], ident[:, :], writes=[idf])
    P.op("dve", lambda e: e.tensor_copy(out=idb[:], in_=idf[:]), reads=[idf], writes=[idb])
    for ti, (r0, rows) in enumerate(TILES):
        xt = xts[ti % 2]; xb = xnb[ti % 2]
        P.dma("sp", xt[:rows, :], h[r0:r0 + rows, :], writes=[xt])
        rms_rows(P, xt, rows, D, gbc, xb, sq, ssq, rstd)
        for half in range(2):
            pt = pst[half]
            for j in range(8):
                kc = half * 8 + j
                P.op("pe", lambda e, pt=pt, j=j, kc=kc, xb=xb, rows=rows: e.transpose(
                    out=pt[:, j * 128:j * 128 + rows], in_=xb[:rows, kc * 128:(kc + 1) * 128], identity=idb[:rows, :rows]),
                    reads=[xb, idb], writes=[pt])
            eng = "act" if half == 0 else "dve"
            src = pt[:, :].rearrange("p (a b) -> p a b", b=128)[:, :, :rows]
            dst = xnT[:, half * 8:half * 8 + 8, r0:r0 + rows]
            if eng == "act":
                P.op("act", lambda e, src=src, dst=dst: e.copy(out=dst, in_=src), reads=[pt], writes=[xnT])
            else:
                P.op("dve", lambda e, src=src, dst=dst: e.tensor_copy(out=dst, in_=src), reads=[pt], writes=[xnT])
    ngrp = (ncols + 511) // 512
    k = 0
    for gi in range(ngrp):
        c0 = gi * 512
        cw = min(512, ncols - c0)
        ws = wst[gi % 2]; wb = wbf[gi % 2]
        P.dma("sp", ws[:, :, :cw], W[:, c0:c0 + cw].rearrange("(kc p) c -> p kc c", p=128), writes=[ws])
        for q4 in range(4):
            eng = ("pool", "dve", "pool", "act")[q4]
            sl = slice(q4 * 4, q4 * 4 + 4)
            if eng == "act":
                P.op("act", lambda e, ws=ws, wb=wb, sl=sl, cw=cw: e.copy(out=wb[:, sl, :cw], in_=ws[:, sl, :cw]), reads=[ws], writes=[wb])
            else:
                P.op(eng, lambda e, ws=ws, wb=wb, sl=sl, cw=cw: e.tensor_copy(out=wb[:, sl, :cw], in_=ws[:, sl, :cw]), reads=[ws], writes=[wb])
        for ti, (r0, rows) in enumerate(TILES):
            pm = psm[k % 4]; o = ot[k % 4]
            for kc in range(16):
                P.op("pe", lambda e, pm=pm, kc=kc, wb=wb, r0=r0, rows=rows, cw=cw: e.matmul(
                    pm[:rows, :cw], xnT[:, kc, r0:r0 + rows], wb[:, kc, :cw], start=(kc == 0), stop=(kc == 15)),
                    reads=[xnT, wb], writes=[pm])
            if k % 2 == 0:
                P.op("act", lambda e, pm=pm, o=o, rows=rows, cw=cw: e.copy(out=o[:rows, :cw], in_=pm[:rows, :cw]), reads=[pm], writes=[o])
            else:
                P.op("dve", lambda e, pm=pm, o=o, rows=rows, cw=cw: e.tensor_copy(out=o[:rows, :cw], in_=pm[:rows, :cw]), reads=[pm], writes=[o])
            if gi in norm_groups:
                nh, gidx = norm_groups[gi]
                w = nh * 64
                P.op("act", lambda e, o=o, rows=rows, w=w: e.activation(out=nsq[:rows, :w], in_=o[:rows, :w], func=AF.Square), reads=[o], writes=[nsq])
                P.op("dve", lambda e, rows=rows, nh=nh, w=w: e.tensor_reduce(out=ns8[:rows, :nh], in_=nsq[:rows, :w].rearrange("p (h d) -> p h d", d=64),
                                                                           axis=AX.X, op=ALU.add), reads=[nsq], writes=[ns8])
                P.op("dve", lambda e, rows=rows, nh=nh: e.tensor_scalar(out=ns8[:rows, :nh], in0=ns8[:rows, :nh], scalar1=1.0 / 64, scalar2=EPS, op0=ALU.mult, op1=ALU.add),
                     reads=[ns8], writes=[ns8])
                P.op("act", lambda e, rows=rows, nh=nh: e.activation(out=ns8[:rows, :nh], in_=ns8[:rows, :nh], func=AF.Sqrt), reads=[ns8], writes=[ns8])
                P.op("dve", lambda e, rows=rows, nh=nh: e.reciprocal(out=ns8[:rows, :nh], in_=ns8[:rows, :nh]), reads=[ns8], writes=[ns8])
                ov = lambda o=o, rows=rows, w=w: o[:rows, :w].rearrange("p (h d) -> p h d", d=64)
                P.op("dve", lambda e, o=o, rows=rows, nh=nh, w=w: e.tensor_tensor(
                    out=o[:rows, :w].rearrange("p (h d) -> p h d", d=64), in0=o[:rows, :w].rearrange("p (h d) -> p h d", d=64),
                    in1=bcast_ap(ns8[:rows, 0:1], [[8, rows], [1, nh], [0, 64]]), op=ALU.mult), reads=[o, ns8], writes=[o])
                P.op("dve", lambda e, o=o, rows=rows, nh=nh, w=w, gidx=gidx: e.tensor_tensor(
                    out=o[:rows, :w].rearrange("p (h d) -> p h d", d=64), in0=o[:rows, :w].rearrange("p (h d) -> p h d", d=64),
                    in1=bcast_ap(gn[:rows, gidx, 0:1], [[192, rows], [0, nh], [1, 64]]), op=ALU.mult), reads=[o, gn], writes=[o])
            P.dma("sp", u[r0:r0 + rows, c0:c0 + cw], o[:rows, :cw], reads=[o])
            k += 1
    P.emit()
    P.close()
    return nc


PERM = np.concatenate([np.arange(0, 1024), np.arange(1024, 3072), np.arange(3088, 4112), np.arange(4112, 4624),
                       np.arange(4624, 5136), np.arange(5136, 5648), np.arange(3072, 3088), np.arange(5648, 5696)])
NORMG = {6: (8, 0), 7: (8, 0), 9: (4, 1), 10: (4, 2)}


def run_A(h_p, h_s, g, W, qg, kg, st_win, st_conv):
    nc = build_A(5696, NORMG)
    ident = np.eye(128, dtype=np.float32)
    f = lambda a: np.ascontiguousarray(a, dtype=np.float32)
    g = f(g); Wp = f(W[:, PERM])
    gains = f(np.concatenate([qg, kg[1], kg[2]]))
    maps = []
    for c in range(8):
        hc = np.concatenate([h_p[c * 1024:(c + 1) * 1024], h_s[c * 4:(c + 1) * 4]], axis=0)
        maps.append({"h": f(hc), "g": g, "W": Wp, "ident": ident, "gains": gains,
                     "sw_in": f(st_win[4 * c:4 * c + 4].reshape(4, 512, 512)), "sc_in": f(st_conv[4 * c:4 * c + 4])})
    res = run_bass_kernel_spmd(nc, maps, core_ids=list(range(8)))
    up = np.concatenate([r["u"][:1024] for r in res.results], axis=0)
    us = np.concatenate([r["u"][1024:] for r in res.results], axis=0)
    names = (("z", 0, 1024), ("xbc", 1024, 3072), ("q", 3072, 4096), ("kvc", 4096, 4608), ("kvs", 4608, 5120), ("kvw", 5120, 5632),
             ("dt", 5632, 5648), ("gt", 5648, 5696))
    out = {}
    for nm, a, b in names:
        out["p_" + nm] = up[:, a:b]
        out["s_" + nm] = us[:, a:b]
    out["sw_keep"] = np.concatenate([r["sw_out"] for r in res.results], axis=0)
    out["sc_keep"] = np.concatenate([r["sc_out"] for r in res.results], axis=0)
    return out


T = 8192
NCK = 64


def build_B():
    nc = bass.Bass("TRN2", target_bir_lowering=False)
    I = lambda n, s: nc.dram_tensor(n, s, F32, kind="ExternalInput").ap()
    O = lambda n, s: nc.dram_tensor(n, s, F32, kind="ExternalOutput").ap()
    xbcT = I("xbcT", [3, 128, T + 3]); cw = I("cw", [3, 128, 4]); cb = I("cb", [128, 3])
    dtr = I("dtr", [128, NCK, 2]); z = I("z", [128, NCK, 128])
    dtb = I("dtb", [128, 2]); alog = I("alog", [128, 2]); dsk = I("dsk", [128, 2])
    consts = I("consts", [4, 128, 128])
    ident = I("ident", [128, 128])
    sx = I("sx", [128, 32, 4]); sB = I("sB", [4, 128, 32, 128]); sC = I("sC", [4, 128, 32, 128])
    scwx = I("scwx", [128, 4]); scbx = I("scbx", [128, 1])
    scwB = I("scwB", [128, 4, 128]); scbB = I("scbB", [128, 128]); scwC = I("scwC", [128, 4, 128]); scbC = I("scbC", [128, 128])
    sdtr = I("sdtr", [128, 32]); spar = I("spar", [128, 3])
    sz = I("sz", [128, 32]); sH = I("sH", [128, 32, 128])
    yg = O("yg", [T, 128]); hfin = O("hfin", [128, 128]); syg = O("syg", [128, 32]); sHo = O("sHo", [128, 32, 128])

    P = Prog(nc)
    cst = P.sb([128, 4, 128], F32, "cst")
    idf = P.sb([128, 128], F32, "idf"); idb = P.sb([128, 128], BF16, "idb")
    P.dma("sp", cst[:], consts.rearrange("a p f -> p a f"), writes=[cst])
    P.dma("sp", idf[:], ident[:, :], writes=[idf])
    P.op("dve", lambda e: e.tensor_copy(out=idb[:], in_=idf[:]), reads=[idf], writes=[idb])
    US, LL, TRI, ONES = 0, 1, 2, 3

    H = P.sb([128, 32, 128], F32, "sHt")
    tmp = P.sb([128, 32, 128], F32, "stmp")
    inb = P.sb([128, 32, 128], F32, "sinb")
    Bbc = P.sb([128, 32, 128], F32, "sBbc")
    wB = P.sb([128, 4, 128], F32, "swB"); bB = P.sb([128, 128], F32, "sbB")
    sxt = P.sb([128, 32, 4], F32, "sxt"); wx = P.sb([128, 4], F32, "swx"); bx = P.sb([128, 1], F32, "sbx")
    xs = P.sb([128, 32], F32, "sxs"); par = P.sb([128, 3], F32, "spar_t")
    dts = P.sb([128, 32], F32, "sdt"); dec = P.sb([128, 32], F32, "sdec"); dtx = P.sb([128, 32], F32, "sdtx")
    szt = P.sb([128, 32], F32, "szt"); yt = P.sb([128, 32], F32, "syt"); av = P.sb([128, 1], F32, "sav")
    P.dma("sp", H[:], sH[:, :, :], writes=[H])
    P.dma("sp", sxt[:], sx[:, :, :], writes=[sxt])
    P.dma("sp", wx[:], scwx[:, :], writes=[wx]); P.dma("sp", bx[:], scbx[:, :], writes=[bx])
    P.dma("sp", par[:], spar[:, :], writes=[par]); P.dma("sp", dts[:], sdtr[:, :], writes=[dts]); P.dma("sp", szt[:], sz[:, :], writes=[szt])
    P.op("dve", lambda e: e.tensor_scalar(out=xs[:], in0=sxt[:, :, 0], scalar1=wx[:, 0:1], scalar2=None, op0=ALU.mult), reads=[sxt, wx], writes=[xs])
    for k in range(1, 4):
        P.op("dve", lambda e, k=k: e.scalar_tensor_tensor(out=xs[:], in0=sxt[:, :, k], scalar=wx[:, k:k + 1], in1=xs[:], op0=ALU.mult, op1=ALU.add),
             reads=[sxt, wx, xs], writes=[xs])
    P.op("act", lambda e: e.activation(out=xs[:], in_=xs[:], func=AF.Silu, bias=bx[:, 0:1]), reads=[xs, bx], writes=[xs])
    P.op("act", lambda e: e.activation(out=dts[:], in_=dts[:], func=AF.Exp, bias=par[:, 0:1]), reads=[dts, par], writes=[dts])
    P.op("act", lambda e: e.activation(out=dts[:], in_=dts[:], func=AF.Ln, bias=1.0), reads=[dts], writes=[dts])
    P.op("act", lambda e: e.activation(out=av[:], in_=par[:, 1:2], func=AF.Exp), reads=[par], writes=[av])
    P.op("dve", lambda e: e.tensor_scalar(out=av[:], in0=av[:], scalar1=-1.0, scalar2=None, op0=ALU.mult), reads=[av], writes=[av])
    P.op("act", lambda e: e.activation(out=dec[:], in_=dts[:], func=AF.Exp, scale=av[:, 0:1]), reads=[dts, av], writes=[dec])
    P.op("dve", lambda e: e.tensor_tensor(out=dtx[:], in0=dts[:], in1=xs[:], op=ALU.mult), reads=[dts, xs], writes=[dtx])

    def conv_rep(src, wsrc, bsrc, dst):
        P.dma("sp", wB[:], wsrc[:, :, :], writes=[wB]); P.dma("sp", bB[:], bsrc[:, :], writes=[bB])
        for k in range(4):
            P.dma("sp", inb[:], src[k], writes=[inb])
            wk = bcast_ap(wB[:, k, :], [[4 * 128, 128], [0, 32], [1, 128]])
            if k == 0:
                P.op("dve", lambda e, wk=wk: e.tensor_tensor(out=dst[:], in0=inb[:], in1=wk, op=ALU.mult), reads=[inb, wB], writes=[dst])
            else:
                P.op("dve", lambda e, wk=wk: e.tensor_tensor(out=tmp[:], in0=inb[:], in1=wk, op=ALU.mult), reads=[inb, wB], writes=[tmp])
                P.op("pool", lambda e: e.tensor_tensor(out=dst[:], in0=dst[:], in1=tmp[:], op=ALU.add), reads=[dst, tmp], writes=[dst])
        bb = bcast_ap(bB[:, :], [[128, 128], [0, 32], [1, 128]])
        P.op("dve", lambda e: e.tensor_tensor(out=dst[:], in0=dst[:], in1=bb, op=ALU.add), reads=[dst, bB], writes=[dst])
        P.op("act", lambda e: e.activation(out=dst[:], in_=dst[:], func=AF.Silu), reads=[dst], writes=[dst])

    conv_rep(sB, scwB, scbB, Bbc)
    decb = bcast_ap(dec[:, :], [[32, 128], [1, 32], [0, 128]])
    dtxb = bcast_ap(dtx[:, :], [[32, 128], [1, 32], [0, 128]])
    P.op("dve", lambda e: e.tensor_tensor(out=H[:], in0=H[:], in1=decb, op=ALU.mult), reads=[H, dec], writes=[H])
    P.op("dve", lambda e: e.tensor_tensor(out=Bbc[:], in0=Bbc[:], in1=dtxb, op=ALU.mult), reads=[Bbc, dtx], writes=[Bbc])
    P.op("dve", lambda e: e.tensor_tensor(out=H[:], in0=H[:], in1=Bbc[:], op=ALU.add), reads=[H, Bbc], writes=[H])
    P.dma("sp", sHo[:, :, :], H[:], reads=[H])
    conv_rep(sC, scwC, scbC, Bbc)
    P.op("dve", lambda e: e.tensor_tensor(out=Bbc[:], in0=Bbc[:], in1=H[:], op=ALU.mult), reads=[Bbc, H], writes=[Bbc])
    P.op("dve", lambda e: e.tensor_reduce(out=yt[:], in_=Bbc[:], axis=AX.X, op=ALU.add), reads=[Bbc], writes=[yt])
    P.op("dve", lambda e: e.scalar_tensor_tensor(out=yt[:], in0=xs[:], scalar=par[:, 2:3], in1=yt[:], op0=ALU.mult, op1=ALU.add),
         reads=[xs, par, yt], writes=[yt])
    P.op("act", lambda e: e.activation(out=szt[:], in_=szt[:], func=AF.Silu), reads=[szt], writes=[szt])
    P.op("dve", lambda e: e.tensor_tensor(out=yt[:], in0=yt[:], in1=szt[:], op=ALU.mult), reads=[yt, szt], writes=[yt])
    P.dma("sp", syg[:, :], yt[:], reads=[yt])

    act3 = [P.sb([128, T], BF16, f"act3_{i}") for i in range(3)]
    cin = P.sb([128, 2051], F32, "cin"); cacc = P.sb([128, 2048], F32, "cacc")
    cwt = P.sb([128, 3, 4], F32, "cwt"); cbt = P.sb([128, 3], F32, "cbt")
    P.dma("sp", cwt[:], cw.rearrange("a p k -> p a k"), writes=[cwt])
    P.dma("sp", cbt[:], cb[:, :], writes=[cbt])
    for a in range(3):
        for blk in range(4):
            c0 = blk * 2048
            P.dma("sp", cin[:], xbcT[a, :, c0:c0 + 2051], writes=[cin])
            P.op("dve", lambda e, a=a: e.tensor_scalar(out=cacc[:], in0=cin[:, 0:2048], scalar1=cwt[:, a, 0:1], scalar2=None, op0=ALU.mult),
                 reads=[cin, cwt], writes=[cacc])
            for k in range(1, 4):
                P.op("dve", lambda e, a=a, k=k: e.scalar_tensor_tensor(out=cacc[:], in0=cin[:, k:k + 2048], scalar=cwt[:, a, k:k + 1], in1=cacc[:],
                                                                    op0=ALU.mult, op1=ALU.add), reads=[cin, cwt, cacc], writes=[cacc])
            P.op("act", lambda e, a=a, c0=c0: e.activation(out=act3[a][:, c0:c0 + 2048], in_=cacc[:], func=AF.Silu, bias=cbt[:, a:a + 1]),
                 reads=[cacc, cbt], writes=[act3[a]])
    xT, BT, CT = act3
    dt = P.sb([128, NCK, 2], F32, "dt"); dA = P.sb([128, NCK, 2], F32, "dA"); ww = P.sb([128, NCK, 2], F32, "ww")
    ee = P.sb([128, NCK, 2], F32, "ee"); cd = P.sb([128, NCK, 2], F32, "cd")
    p3 = P.sb([128, 3, 2], F32, "p3"); a2 = P.sb([128, 2], F32, "a2")
    P.dma("sp", dt[:], dtr[:, :, :], writes=[dt])
    P.dma("sp", p3[:, 0, :], dtb[:, :], writes=[p3]); P.dma("sp", p3[:, 1, :], alog[:, :], writes=[p3]); P.dma("sp", p3[:, 2, :], dsk[:, :], writes=[p3])
    P.op("dve", lambda e: e.tensor_tensor(out=dt[:], in0=dt[:], in1=bcast_ap(p3[:, 0, :], [[6, 128], [0, NCK], [1, 2]]), op=ALU.add), reads=[dt, p3], writes=[dt])
    P.op("act", lambda e: e.activation(out=dt[:], in_=dt[:], func=AF.Exp), reads=[dt], writes=[dt])
    P.op("act", lambda e: e.activation(out=dt[:], in_=dt[:], func=AF.Ln, bias=1.0), reads=[dt], writes=[dt])
    P.op("act", lambda e: e.activation(out=a2[:], in_=p3[:, 1, :], func=AF.Exp), reads=[p3], writes=[a2])
    P.op("dve", lambda e: e.tensor_scalar(out=a2[:], in0=a2[:], scalar1=-1.0, scalar2=None, op0=ALU.mult), reads=[a2], writes=[a2])
    P.op("dve", lambda e: e.tensor_tensor(out=dA[:], in0=dt[:], in1=bcast_ap(a2[:, :], [[2, 128], [0, NCK], [1, 2]]), op=ALU.mult), reads=[dt, a2], writes=[dA])
    ptx_ = P.ps([128, 1024], BF16, "ptx")
    pbank = [P.ps([128, 512], F32, f"pb{i}") for i in range(6)]
    class _V:
        def __init__(s, b, n): s.b = b; s.n = n
    def view(b, n):
        v = Buf(b.t, b.name); v.__dict__ = b.__dict__; return b
    psA, psB, psC = pbank[0], pbank[1], pbank[2]
    dA2 = dA[:, :, :].rearrange("p a b -> p (a b)")
    P.op("pe", lambda e: e.matmul(psA[:, 0:128], cst[:, LL, :], dA2, start=True, stop=True), reads=[cst, dA], writes=[psA])
    P.op("pe", lambda e: e.matmul(psB[:, 0:128], cst[:, US, :], dA2, start=True, stop=True), reads=[cst, dA], writes=[psB])
    P.op("pe", lambda e: e.matmul(psC[:, 0:128], cst[:, ONES, :], dA2, start=True, stop=True), reads=[cst, dA], writes=[psC])
    f2 = lambda t: t[:, :, :].rearrange("p a b -> p (a b)")
    P.op("act", lambda e: e.activation(out=f2(ee), in_=psA[:, 0:128], func=AF.Exp), reads=[psA], writes=[ee])
    P.op("act", lambda e: e.activation(out=f2(ww), in_=psB[:, 0:128], func=AF.Exp), reads=[psB], writes=[ww])
    P.op("act", lambda e: e.activation(out=f2(cd), in_=psC[:, 0:128], func=AF.Exp), reads=[psC], writes=[cd])
    P.op("dve", lambda e: e.tensor_tensor(out=ww[:], in0=ww[:], in1=dt[:], op=ALU.mult), reads=[ww, dt], writes=[ww])

    Hs = P.sb([128, 128], F32, "Hs"); Hb = P.sb([128, 128], BF16, "Hb")
    P.op("dve", lambda e: e.memset(Hs[:], 0.0), writes=[Hs])
    P.op("dve", lambda e: e.memset(Hb[:], 0.0), writes=[Hb])
    ptx = ptx_
    pG = pbank[0]; pseg = [pbank[1], pbank[2]]; pyd = pbank[3]; pyo = pbank[4]; pS = pbank[5]
    xtok = [P.sb([128, 256], BF16, f"xtok{i}") for i in range(2)]
    Gm = [P.sb([128, 128], F32, f"Gm{i}") for i in range(2)]
    UdA = [P.sb([128, 128], F32, f"UdA{i}") for i in range(2)]
    Eb = [P.sb([128, 128], F32, f"Eb{i}") for i in range(2)]
    MT = [P.sb([128, 128], BF16, f"MT{i}") for i in range(2)]
    xw = [P.sb([128, 128], BF16, f"xw{i}") for i in range(2)]
    yds = [P.sb([128, 128], F32, f"yds{i}") for i in range(2)]
    yo = [P.sb([128, 128], F32, f"yo{i}") for i in range(2)]
    zt = [P.sb([128, 128], F32, f"zt{i}") for i in range(2)]
    for ck in range(NCK):
        s = ck % 2
        cs = slice(ck * 128, (ck + 1) * 128)
        xt_ = xtok[s]
        P.dma("sp", zt[s][:], z[:, ck, :], writes=[zt[s]])
        P.op("act", lambda e, s=s: e.activation(out=zt[s][:], in_=zt[s][:], func=AF.Silu), reads=[zt[s]], writes=[zt[s]])
        P.op("pe", lambda e, cs=cs: e.transpose(out=ptx[:, 0:128], in_=xT[:, cs], identity=idb[:]), reads=[xT, idb], writes=[ptx])
        P.op("pe", lambda e, cs=cs: e.transpose(out=ptx[:, 128:256], in_=BT[:, cs], identity=idb[:]), reads=[BT, idb], writes=[ptx])
        P.op("act", lambda e, xt_=xt_: e.copy(out=xt_[:], in_=ptx[:, 0:256]), reads=[ptx], writes=[xt_])
        P.op("pe", lambda e, cs=cs: e.matmul(pG[:, 0:128], BT[:, cs], CT[:, cs], start=True, stop=True), reads=[BT, CT], writes=[pG])
        P.op("dve", lambda e, s=s: e.tensor_tensor(out=Gm[s][:], in0=pG[:, 0:128], in1=cst[:, TRI, :], op=ALU.mult), reads=[pG, cst], writes=[Gm[s]])
        for hh in range(2):
            hs = slice(hh * 64, hh * 64 + 64)
            P.op("dve", lambda e, hh=hh, ck=ck: e.tensor_scalar(out=UdA[hh][:], in0=cst[:, US, :], scalar1=dA[:, ck, hh:hh + 1], scalar2=None, op0=ALU.mult),
                 reads=[cst, dA], writes=[UdA[hh]])
            P.op("pe", lambda e, hh=hh: e.matmul(pseg[hh][:, 0:128], UdA[hh][:], cst[:, LL, :], start=True, stop=True), reads=[UdA[hh], cst], writes=[pseg[hh]])
            P.op("act", lambda e, hh=hh: e.activation(out=Eb[hh][:], in_=pseg[hh][:, 0:128], func=AF.Exp), reads=[pseg[hh]], writes=[Eb[hh]])
            P.op("dve", lambda e, hh=hh, ck=ck, s=s: e.scalar_tensor_tensor(out=MT[hh][:], in0=Eb[hh][:], scalar=dt[:, ck, hh:hh + 1], in1=Gm[s][:],
                                                                        op0=ALU.mult, op1=ALU.mult), reads=[Eb[hh], dt, Gm[s]], writes=[MT[hh]])
            P.op("pe", lambda e, hh=hh, hs=hs, xt_=xt_: e.matmul(pyd[:, hs], MT[hh][:], xt_[:, hs], start=True, stop=True), reads=[MT[hh], xt_], writes=[pyd])
            P.op("pe", lambda e, hs=hs, cs=cs: e.matmul(pyo[:, hs], CT[:, cs], Hb[:, hs], start=True, stop=True), reads=[CT, Hb], writes=[pyo])
            P.op("dve", lambda e, hh=hh, hs=hs, ck=ck, s=s, xt_=xt_: e.tensor_scalar(out=xw[s][:, hs], in0=xt_[:, hs], scalar1=ww[:, ck, hh:hh + 1], scalar2=None, op0=ALU.mult),
                 reads=[xt_, ww], writes=[xw[s]])
        P.op("pe", lambda e, s=s, xt_=xt_: e.matmul(pS[:, 0:128], xt_[:, 128:256], xw[s][:], start=True, stop=True), reads=[xt_, xw[s]], writes=[pS])
        P.op("act", lambda e, s=s: e.copy(out=yds[s][:], in_=pyd[:, 0:128]), reads=[pyd], writes=[yds[s]])
        for hh in range(2):
            hs = slice(hh * 64, hh * 64 + 64)
            P.op("dve", lambda e, hh=hh, hs=hs, ck=ck, s=s: e.scalar_tensor_tensor(out=yo[s][:, hs], in0=pyo[:, hs], scalar=ee[:, ck, hh:hh + 1], in1=yds[s][:, hs],
                                                                               op0=ALU.mult, op1=ALU.add), reads=[pyo, ee, yds[s]], writes=[yo[s]])
            P.op("dve", lambda e, hh=hh, hs=hs, s=s, xt_=xt_: e.scalar_tensor_tensor(out=yo[s][:, hs], in0=xt_[:, hs], scalar=p3[:, 2, hh:hh + 1], in1=yo[s][:, hs],
                                                                                 op0=ALU.mult, op1=ALU.add), reads=[xt_, p3, yo[s]], writes=[yo[s]])
            P.op("dve", lambda e, hh=hh, hs=hs, ck=ck: e.scalar_tensor_tensor(out=Hs[:, hs], in0=Hs[:, hs], scalar=cd[:, ck, hh:hh + 1], in1=pS[:, hs],
                                                                          op0=ALU.mult, op1=ALU.add), reads=[Hs, cd, pS, pyo], writes=[Hs])
        P.op("act", lambda e: e.copy(out=Hb[:], in_=Hs[:]), reads=[Hs, pyo], writes=[Hb])
        P.op("dve", lambda e, s=s: e.tensor_tensor(out=yo[s][:], in0=yo[s][:], in1=zt[s][:], op=ALU.mult), reads=[yo[s], zt[s]], writes=[yo[s]])
        P.dma("sp", yg[cs, :], yo[s][:], reads=[yo[s]])
    P.dma("sp", hfin[:, :], Hs[:], reads=[Hs])
    P.emit()
    P.close()
    return nc


def run_B(z_p, xbc_p, dt_p, z_s, xbc_s, dt_s, st_conv, st_ssm, conv_w, conv_b, dt_bias, a_log, d_skip):
    nc = build_B()
    f = lambda a: np.ascontiguousarray(a, dtype=np.float32)
    t = np.arange(128)
    consts = np.stack([(t[:, None] > t[None, :]), (t[:, None] <= t[None, :]), (t[:, None] <= t[None, :]), np.ones((128, 128), bool)]).astype(np.float32)
    ident = np.eye(128, dtype=np.float32)
    maps = []
    for c in range(8):
        gi = c // 2
        cols = [np.arange(128 * c, 128 * c + 128), np.arange(1024 + 128 * gi, 1024 + 128 * gi + 128), np.arange(1536 + 128 * gi, 1536 + 128 * gi + 128)]
        xbcT = np.zeros((3, 128, T + 3), np.float32)
        for a in range(3):
            xbcT[a, :, 3:] = xbc_p[:, cols[a]].T
        cw = np.stack([conv_w[:, cols[a]].T for a in range(3)])
        cb = np.stack([conv_b[cols[a]] for a in range(3)], axis=1)
        hsl = slice(2 * c, 2 * c + 2)
        m = {"xbcT": xbcT, "cw": f(cw), "cb": f(cb),
             "dtr": f(dt_p[:, hsl].reshape(64, 128, 2).transpose(1, 0, 2)),
             "z": f(z_p[:, 128 * c:128 * c + 128].reshape(64, 128, 128).transpose(1, 0, 2)),
             "dtb": f(np.broadcast_to(dt_bias[hsl], (128, 2))), "alog": f(np.broadcast_to(a_log[hsl], (128, 2))),
             "dsk": f(np.broadcast_to(d_skip[hsl], (128, 2))), "consts": consts, "ident": ident}
        full = np.concatenate([st_conv, xbc_s[:, None, :]], axis=1)
        m["sx"] = f(full[:, :, cols[0]].transpose(2, 0, 1))
        for nm, a in (("B", 1), ("C", 2)):
            m["s" + nm] = f(np.broadcast_to(full[:, :, cols[a]].transpose(1, 0, 2)[:, None], (4, 128, 32, 128)))
            m["scw" + nm] = f(np.broadcast_to(conv_w[:, cols[a]][None], (128, 4, 128)))
            m["scb" + nm] = f(np.broadcast_to(conv_b[cols[a]][None], (128, 128)))
        m["scwx"] = f(conv_w[:, cols[0]].T); m["scbx"] = f(conv_b[cols[0]][:, None])
        hp = np.repeat(np.arange(2 * c, 2 * c + 2), 64)
        m["sdtr"] = f(dt_s[:, hp].T)
        m["spar"] = f(np.stack([dt_bias[hp], a_log[hp], d_skip[hp]], axis=1))
        m["sz"] = f(z_s[:, 128 * c:128 * c + 128].T)
        m["sH"] = f(st_ssm[:, hsl].reshape(32, 128, 128).transpose(1, 0, 2))
        maps.append(m)
    res = run_bass_kernel_spmd(nc, maps, core_ids=list(range(8)))
    yg_p = np.concatenate([r["yg"] for r in res.results], axis=1)
    ssm_p = np.concatenate([r["hfin"].T.reshape(2, 64, 128) for r in res.results], axis=0)
    yg_s = np.concatenate([r["syg"].T for r in res.results], axis=1)
    ssm_s = np.concatenate([r["sHo"].transpose(1, 0, 2).reshape(32, 2, 64, 128) for r in res.results], axis=1)
    return yg_p, ssm_p, yg_s, ssm_s


NPOOL = 2560


def build_G(ntab):
    nc = bass.Bass("TRN2", target_bir_lowering=False)
    pools = [nc.dram_tensor(f"pool{t}", [NPOOL * 2, 8192], F32, kind="ExternalInput").ap() for t in range(ntab)]
    ptT = nc.dram_tensor("ptT", [64, 16], I32, kind="ExternalInput").ap()
    outs = [nc.dram_tensor(f"g{t}", [16, 8192, 128], F32, kind="ExternalOutput").ap() for t in range(ntab)]
    P = Prog(nc)
    ptt = P.sb([64, 16], I32, "ptt")
    idx = P.sb([64, 2, 16], I32, "idx")
    bufs = [P.sb([64, 8192], F32, f"gb{i}") for i in range(4)]
    P.dma("sp", ptt[:], ptT[:, :], writes=[ptt])
    for hp in range(2):
        P.op("dve", lambda e, hp=hp: e.tensor_scalar(out=idx[:, hp, :], in0=ptt[:], scalar1=2.0, scalar2=float(hp), op0=ALU.mult, op1=ALU.add),
             reads=[ptt], writes=[idx])
    k = 0
    for t in range(ntab):
        for s in range(16):
            dst = outs[t][s].rearrange("(j hp tt) c -> j hp (tt c)", hp=2, tt=64)
            for hp in range(2):
                b = bufs[k % 4]; k += 1
                P.dma("pool", None, None, reads=[idx], writes=[b],
                      indirect=lambda e, b=b, hp=hp, s=s, t=t: e.indirect_dma_start(
                          out=b[:, :], out_offset=None, in_=pools[t][:, :],
                          in_offset=bass.IndirectOffsetOnAxis(ap=idx[:, hp, s:s + 1], axis=0)))
                P.dma("sp" if k % 2 else "act", dst[:, hp, :], b[:, :], reads=[b])
    P.emit()
    P.close()
    return nc


def run_G(pool_list, page_table):
    nt = len(pool_list)
    nc = build_G(nt)
    maps = []
    byhead = [[np.ascontiguousarray(p[:, :, :, h, :], dtype=np.float32).reshape(NPOOL * 2, 8192) for h in range(4)] for p in pool_list]
    for c in range(8):
        h = c % 4; half = c // 4
        m = {"ptT": np.ascontiguousarray(page_table[16 * half:16 * half + 16].T, dtype=np.int32)}
        for t in range(nt):
            m[f"pool{t}"] = byhead[t][h]
        maps.append(m)
    res = run_bass_kernel_spmd(nc, maps, core_ids=list(range(8)))
    outs = []
    for t in range(nt):
        g = np.zeros((32, 8192, 2, 4, 64), np.float32)
        for c, r in enumerate(res.results):
            h = c % 4; half = c // 4
            g[16 * half:16 * half + 16, :, :, h, :] = r[f"g{t}"].reshape(16, 8192, 2, 64)
        outs.append(g)
    return outs


TP = 8256
TPK = 8320
NJ = 32
NSEQ = 16
SCALE = 0.125
NEG = -1.0e30
BIG = 32768.0


def ntmax(j):
    return (16 * j + 14) // 128


CM_IDX = {}
for _j in range(NJ):
    for _nt in range(ntmax(_j) + 1):
        CM_IDX[(_j, _nt)] = len(CM_IDX)
NCM = len(CM_IDX)


def build_C(do_prompt=True, do_sample=True):
    nc = bass.Bass("TRN2", target_bir_lowering=False)
    I = lambda n, s: nc.dram_tensor(n, s, F32, kind="ExternalInput").ap()
    O = lambda n, s: nc.dram_tensor(n, s, F32, kind="ExternalOutput").ap()
    qT = I("qT", [NJ, 128, 512]); gts = I("gts", [128, NJ, 12])
    kcv = I("kcv", [128, TP]); ksw = I("ksw", [128, 8192]); vs = I("vs", [128, 64, 64]); vw = I("vw", [128, 64, 64])
    wkv = I("wkv", [128, 32, 64]); peT = I("peT", [128, 32]); gk0 = I("gk0", [64, 1])
    cover = I("cover", [128, 5, 129]); t3 = I("t3", [128, 64, 128]); ident = I("ident", [128, 128])
    LM = I("LM", [128, 2, 128]); WM = I("WM", [128, 6, 128]); CM = I("CM", [128, NCM, 128]); ADD = I("ADD", [128, NJ, 128])
    s_kcv = I("s_kcv", [NSEQ, 128, TP]); s_ks = I("s_ks", [NSEQ, 64, TPK]); s_vs = I("s_vs", [NSEQ, 128, 65, 64])
    s_kw = I("s_kw", [NSEQ, 64, 640]); s_vw = I("s_vw", [NSEQ, 128, 5, 64])
    s_q = I("s_q", [NSEQ, 128, 4]); s_gt = I("s_gt", [NSEQ, 4, 3])
    s_msk = I("s_msk", [128, 75])
    s_add = I("s_add", [1, 129])
    o_p = O("o_p", [128, NJ, 256]); o_s = O("o_s", [NSEQ, 4, 64])

    P = Prog(nc)
    stg = [P.sb([128, 2048], F32, f"stg{i}") for i in range(2)]
    scnt = [0]

    def load_cast(dst, dst_ap, src_ap, parts, fshape, p0=0):
        if isinstance(fshape, int):
            fshape = (fshape,)
        n = int(np.prod(fshape))
        st = stg[scnt[0] % 2]
        eng = ("dve", "pool")[scnt[0] % 2]
        scnt[0] += 1
        sview = st[p0:p0 + parts, 0:n]
        if len(fshape) == 2:
            sview = sview.rearrange("p (a b) -> p a b", b=fshape[1])
        P.dma("sp", sview, src_ap, writes=[st])
        P.op(eng, lambda e: e.tensor_copy(out=dst_ap, in_=sview), reads=[st], writes=[dst])

    idf = P.sb([128, 128], F32, "idf"); idb = P.sb([128, 128], BF16, "idb")
    P.dma("sp", idf[:], ident[:, :], writes=[idf])
    P.op("dve", lambda e: e.tensor_copy(out=idb[:], in_=idf[:]), reads=[idf], writes=[idb])
    ones64 = P.sb([64, 64], F32, "ones64"); P.op("dve", lambda e: e.memset(ones64[:], 1.0), writes=[ones64])
    one_bf = P.sb([1, 2], BF16, "one_bf"); P.op("dve", lambda e: e.memset(one_bf[:], 1.0), writes=[one_bf])
    wkvb = P.sb([128, 32, 64], BF16, "wkvb")
    load_cast(wkvb, wkvb[:, :, :], wkv[:, :, :], 128, (32, 64))
    pe = P.sb([128, 32], F32, "pe"); P.dma("sp", pe[:], peT[:, :], writes=[pe])
    gk = P.sb([64, 1], F32, "gk"); P.dma("sp", gk[:], gk0[:, :], writes=[gk])
    covb = P.sb([128, 5, 129], BF16, "covb")
    load_cast(covb, covb[:, :, :], cover[:, :, :], 128, (5, 129))
    t3b = P.sb([128, 64, 128], BF16, "t3b")
    for a in range(4):
        load_cast(t3b, t3b[:, a * 16:(a + 1) * 16, :], t3[:, a * 16:(a + 1) * 16, :], 128, (16, 128))

    KVlo = P.sb([128, TP], BF16, "KVlo"); KVhi = P.sb([128, TP], BF16, "KVhi")
    KSW = P.sb([128, TPK], BF16, "KSW")
    vs_a = P.sb([128, 65, 65], BF16, "vs_a"); vw_a = P.sb([128, 64, 65], BF16, "vw_a"); cv_a = P.sb([128, 5, 65], BF16, "cv_a")
    for t_ in (vs_a, vw_a, cv_a):
        P.op("pool", lambda e, t_=t_: e.memset(t_[:], 1.0), writes=[t_])
    ckf = P.sb([64, 516], F32, "ckf"); cks = P.sb([64, 516], F32, "cks"); ckT = P.sb([128, 640], BF16, "ckT")
    P.op("pool", lambda e: e.memset(ckT[:], 0.0), writes=[ckT])
    pS = [P.ps([128, 512], F32, f"pS{i}") for i in range(2)]
    pM = P.ps([128, 512], F32, "pM"); pOc = P.ps([128, 512], F32, "pOc"); pOs = P.ps([128, 512], F32, "pOs")
    pOw = P.ps([128, 512], F32, "pOw"); pU = P.ps([128, 512], F32, "pU"); pT = P.ps([128, 1024], BF16, "pT")
    pOT, pUT, pX = pOc, pOs, pOw

    def compress(src):
        for c0 in range(0, TP, 2048):
            w = min(2048, TP - c0)
            st = stg[scnt[0] % 2]; scnt[0] += 1
            P.dma("sp", st[:, :w], src[:, c0:c0 + w], writes=[st])
            for dst, po, eng in ((KVlo, 0, "dve"), (KVhi, 16, "pool")):
                P.op(eng, lambda e, dst=dst, po=po, st=st, c0=c0, w=w: e.tensor_tensor(
                    out=dst[:, c0:c0 + w].rearrange("p (n l) -> p n l", l=16), in0=st[:, :w].rearrange("p (n l) -> p n l", l=16),
                    in1=bcast_ap(pe[:, po:po + 1], [[32, 128], [0, w // 16], [1, 16]]), op=ALU.add), reads=[st, pe], writes=[dst])
        for (n0, nn, pb) in ((0, 512, pS[0]), (512, 3, pS[1])):
            for l in range(32):
                srcb = KVlo if l < 16 else KVhi
                P.op("pe", lambda e, srcb=srcb, l=l, n0=n0, nn=nn, pb=pb: e.matmul(
                    pb[0:64, 0:nn], wkvb[0:64, l, :], bcast_ap(srcb[0:64, 16 * n0 + l:16 * n0 + l + 1], [[TP, 64], [16, nn]]),
                    start=(l == 0), stop=(l == 31)), reads=[wkvb, srcb], writes=[pb])
        P.op("act", lambda e: e.copy(out=ckf[:, 0:512], in_=pS[0][0:64, 0:512]), reads=[pS[0]], writes=[ckf])
        P.op("act", lambda e: e.copy(out=ckf[:, 512:515], in_=pS[1][0:64, 0:3]), reads=[pS[1]], writes=[ckf])
        P.op("act", lambda e: e.activation(out=cks[:, 0:515], in_=ckf[:, 0:515], func=AF.Square), reads=[ckf], writes=[cks])
        P.op("pe", lambda e: e.matmul(pS[0][0:64, 0:512], ones64[:], cks[:, 0:512], start=True, stop=True), reads=[ones64, cks], writes=[pS[0]])
        P.op("pe", lambda e: e.matmul(pS[1][0:64, 0:3], ones64[:], cks[:, 512:515], start=True, stop=True), reads=[ones64, cks], writes=[pS[1]])
        P.op("dve", lambda e: e.tensor_scalar(out=cks[:, 0:512], in0=pS[0][0:64, 0:512], scalar1=1.0 / 64, scalar2=1e-6, op0=ALU.mult, op1=ALU.add),
             reads=[pS[0]], writes=[cks])
        P.op("dve", lambda e: e.tensor_scalar(out=cks[:, 512:515], in0=pS[1][0:64, 0:3], scalar1=1.0 / 64, scalar2=1e-6, op0=ALU.mult, op1=ALU.add),
             reads=[pS[1]], writes=[cks])
        P.op("act", lambda e: e.activation(out=cks[:, 0:515], in_=cks[:, 0:515], func=AF.Sqrt), reads=[cks], writes=[cks])
        P.op("dve", lambda e: e.reciprocal(out=cks[:, 0:515], in_=cks[:, 0:515]), reads=[cks], writes=[cks])
        P.op("dve", lambda e: e.tensor_tensor(out=ckf[:, 0:515], in0=ckf[:, 0:515], in1=cks[:, 0:515], op=ALU.mult), reads=[ckf, cks], writes=[ckf])
        P.op("dve", lambda e: e.tensor_scalar(out=ckT[0:64, 0:515], in0=ckf[:, 0:515], scalar1=gk[:, 0:1], scalar2=None, op0=ALU.mult),
             reads=[ckf, gk], writes=[ckT])
        for nt in range(5):
            nn = 128 if nt < 4 else 3
            for l in range(32):
                srcb = KVlo if l < 16 else KVhi
                col = 16 * 128 * nt + l
                P.op("pe", lambda e, srcb=srcb, l=l, nn=nn, col=col: e.matmul(
                    pU[0:nn, 0:64], bcast_ap(srcb[64:128, col:col + 1], [[TP, 64], [16, nn]]), wkvb[64:128, l, :],
                    start=(l == 0), stop=(l == 31)), reads=[wkvb, srcb], writes=[pU])
            P.op("act", lambda e, nt=nt, nn=nn: e.copy(out=cv_a[0:nn, nt, 0:64], in_=pU[0:nn, 0:64]), reads=[pU], writes=[cv_a])

    def bc_g(buf, ap1):
        return bcast_ap(ap1, [[ap1.ap[0][0], 128], [0, 4], [1, 128]])

    if do_prompt:
        LMb = P.sb([128, 2, 128], BF16, "LMb"); WMb = P.sb([128, 6, 128], BF16, "WMb"); CMb = P.sb([128, NCM, 128], BF16, "CMb")
        load_cast(LMb, LMb[:, :, :], LM[:, :, :], 128, (2, 128))
        load_cast(WMb, WMb[:, :, :], WM[:, :, :], 128, (6, 128))
        for a in range(0, NCM, 16):
            load_cast(CMb, CMb[:, a:a + 16, :], CM[:, a:a + 16, :], 128, (16, 128))
        addt = P.sb([128, NJ, 128], F32, "addt"); P.dma("sp", addt[:], ADD[:, :, :], writes=[addt])
        gtt = P.sb([128, NJ, 12], F32, "gtt"); P.dma("sp", gtt[:], gts[:, :, :], writes=[gtt])
        P.op("act", lambda e: e.activation(out=gtt[:], in_=gtt[:], func=AF.Sigmoid), reads=[gtt], writes=[gtt])
        compress(kcv)
        for c0 in range(0, 8192, 2048):
            load_cast(KSW, KSW[:, c0:c0 + 2048], ksw[:, c0:c0 + 2048], 128, 2048)
        for a in range(0, 64, 32):
            load_cast(vs_a, vs_a[:, a:a + 32, 0:64], vs[:, a:a + 32, :], 128, (32, 64))
            load_cast(vw_a, vw_a[:, a:a + 32, 0:64], vw[:, a:a + 32, :], 128, (32, 64))
        qsb = [P.sb([128, 512], BF16, f"qsb{i}") for i in range(2)]; qwb = [P.sb([128, 512], BF16, f"qwb{i}") for i in range(2)]
        for t_ in qsb + qwb:
            P.op("pool", lambda e, t_=t_: e.memset(t_[:], 0.0), writes=[t_])
        Pt = [P.sb([128, 512], BF16, f"Pt{i}") for i in range(3)]
        imp = P.sb([128, 128], F32, "imp"); sc2 = P.sb([128, 128], F32, "sc2"); m8 = P.sb([128, 16], F32, "m8")
        selb = P.sb([128, 128], BF16, "selb"); selT = P.sb([128, 512], BF16, "selT")
        i4f = P.sb([128, 512], F32, "i4f"); I4b = P.sb([128, 512], BF16, "I4b")
        for g in range(4):
            P.op("pool", lambda e, g=g: e.tensor_copy(out=I4b[:, g * 128:(g + 1) * 128], in_=idf[:, :]), reads=[idf], writes=[I4b])
        rs = P.sb([128, 12], F32, "rs"); cf = P.sb([128, 12], F32, "cf")
        ot = [P.sb([128, 256], F32, f"ot{i}") for i in range(2)]; otmp = P.sb([128, 256], F32, "otmp")
        pk = [0]

        def qk_exp(lhs_ap, q_ap, reads, extra=()):
            ps_ = pS[pk[0] % 2]; pt_ = Pt[pk[0] % 3]; pk[0] += 1
            import os
            if os.environ.get('NOEXTRA'): extra = ()
            n = len(extra)
            P.op("pe", lambda e: e.matmul(ps_[:, :], lhs_ap, q_ap, start=True, stop=(n == 0)), reads=reads, writes=[ps_])
            for i, (l_ap, r_ap, rd) in enumerate(extra):
                P.op("pe", lambda e, l_ap=l_ap, r_ap=r_ap, i=i: e.matmul(ps_[:, :], l_ap, r_ap, start=False, stop=(i == n - 1)), reads=rd, writes=[ps_])
            P.op("act", lambda e: e.activation(out=pt_[:], in_=ps_[:, :], func=AF.Exp, scale=SCALE), reads=[ps_], writes=[pt_])
            return pt_

        OTs = P.sb([65, 512], F32, "OTs"); UTs = P.sb([128, 512], F32, "UTs")
        Otok = [P.sb([128, 260], F32, f"Otok{i}") for i in range(3)]

        def pv(pt_, vbuf, kt, first, last):
            P.op("pe", lambda e: e.matmul(pOT[0:65, :], vbuf[:, kt, :], pt_[:, :], start=first, stop=last), reads=[pt_, vbuf], writes=[pOT])

        def finish_branch(br):
            P.op("act", lambda e: e.copy(out=OTs[:, :], in_=pOT[0:65, :]), reads=[pOT], writes=[OTs])
            for g in range(4):
                P.op("pe", lambda e, g=g: e.transpose(out=pX[:, g * 65:(g + 1) * 65], in_=OTs[0:65, g * 128:(g + 1) * 128], identity=idf[0:65, 0:65]),
                     reads=[OTs, idf], writes=[pX])
            P.op("act", lambda e: e.copy(out=Otok[br][:, :], in_=pX[:, 0:260]), reads=[pX], writes=[Otok[br]])

        def run_branch(tiles, vbuf, with_u=False):
            n = len(tiles)
            pts = [None] * n
            pts[0] = qk_exp(*tiles[0][:4])
            for i in range(n):
                if i + 1 < n:
                    pts[i + 1] = qk_exp(*tiles[i + 1][:4])
                pv(pts[i], vbuf, tiles[i][4], i == 0, i == n - 1)
                if with_u:
                    P.op("pe", lambda e, i=i, pt_=pts[i], nt=tiles[i][4]: e.matmul(pUT[:, :], covb[:, nt, 0:128], pt_[:, :], start=(i == 0), stop=(i == n - 1)),
                         reads=[pts[i], covb], writes=[pUT])

        for j in range(NJ):
            qs_ = qsb[j % 2]; qw_ = qwb[j % 2]
            st = stg[scnt[0] % 2]; scnt[0] += 1
            P.dma("sp", st[:, 0:512], qT[j], writes=[st])
            P.op("dve", lambda e, st=st, qs_=qs_: e.tensor_copy(out=qs_[0:64, :], in_=st[0:64, 0:512]), reads=[st], writes=[qs_])
            P.op("pool", lambda e, st=st, qw_=qw_: e.tensor_copy(out=qw_[64:128, :], in_=st[64:128, 0:512]), reads=[st], writes=[qw_])
            nts = list(range(ntmax(j) + 1))
            run_branch([(ckT[:, nt * 128:(nt + 1) * 128], qs_[:, :], [ckT, qs_], [(CMb[:, CM_IDX[(j, nt)], :], I4b[:, :], [CMb, I4b])], nt) for nt in nts],
                       cv_a, with_u=True)
            finish_branch(0)
            P.op("act", lambda e: e.copy(out=UTs[:, :], in_=pUT[:, :]), reads=[pUT], writes=[UTs])
            for g in range(4):
                P.op("pe", lambda e, g=g: e.transpose(out=pX[:, g * 128:(g + 1) * 128], in_=UTs[:, g * 128:(g + 1) * 128], identity=idf[:, :]),
                     reads=[UTs, idf], writes=[pX])
            P.op("dve", lambda e: e.tensor_scalar(out=rs[:, 0:4], in0=bcast_ap(Otok[0][:, 64:65], [[260, 128], [65, 4]]), scalar1=1e-30, scalar2=None, op0=ALU.max),
                 reads=[Otok[0]], writes=[rs])
            P.op("dve", lambda e: e.reciprocal(out=rs[:, 0:4], in_=rs[:, 0:4]), reads=[rs], writes=[rs])
            P.op("dve", lambda e: e.tensor_scalar(out=imp[:], in0=pX[:, 0:128], scalar1=rs[:, 0:1], scalar2=None, op0=ALU.mult), reads=[pX, rs], writes=[imp])
            for g in range(1, 4):
                P.op("dve", lambda e, g=g: e.scalar_tensor_tensor(out=imp[:], in0=pX[:, g * 128:(g + 1) * 128], scalar=rs[:, g:g + 1], in1=imp[:],
                                                                op0=ALU.mult, op1=ALU.add), reads=[pX, rs, imp], writes=[imp])
            P.op("dve", lambda e, j=j: e.tensor_tensor(out=imp[:], in0=imp[:], in1=addt[:, j, :], op=ALU.add), reads=[imp, addt], writes=[imp])
            P.op("dve", lambda e: e.max(out=m8[:, 0:8], in_=imp[:]), reads=[imp], writes=[m8])
            P.op("dve", lambda e: e.match_replace(out=sc2[:], in_to_replace=m8[:, 0:8], in_values=imp[:], imm_value=NEG), reads=[imp, m8], writes=[sc2])
            P.op("dve", lambda e: e.max(out=m8[:, 8:16], in_=sc2[:]), reads=[sc2], writes=[m8])
            P.op("dve", lambda e: e.tensor_scalar(out=selb[:], in0=imp[:], scalar1=m8[:, 15:16], scalar2=None, op0=ALU.is_ge), reads=[imp, m8], writes=[selb])
            P.op("dve", lambda e: e.tensor_scalar(out=selb[:], in0=selb[:], scalar1=1.0, scalar2=BIG, op0=ALU.subtract, op1=ALU.mult), reads=[selb], writes=[selb])
            P.op("pe", lambda e: e.transpose(out=pT[:, 0:128], in_=selb[:], identity=idb[:]), reads=[selb, idb], writes=[pT])
            P.op("act", lambda e: e.copy(out=selT[:, :].rearrange("p (g q) -> p g q", g=4),
                                         in_=bcast_ap(pT[:, 0:1], [[pT[:, :].ap[0][0], 128], [0, 4], [1, 128]])), reads=[pT], writes=[selT])
            kts = list(range(2 * j + 2))
            tl = []
            for kt in kts:
                ex = [(t3b[:, kt, :], selT[:, :], [t3b, selT])]
                if kt >= 2 * j:
                    ex.append((LMb[:, kt - 2 * j, :], I4b[:, :], [LMb, I4b]))
                tl.append((KSW[:, kt * 128:(kt + 1) * 128], qs_[:, :], [KSW, qs_], ex, kt))
            run_branch(tl, vs_a)
            finish_branch(1)
            wk = [(i, 2 * j - 4 + i) for i in range(6) if 2 * j - 4 + i >= 0]
            run_branch([(KSW[:, kt * 128:(kt + 1) * 128], qw_[:, :], [KSW, qw_], [(WMb[:, i, :], I4b[:, :], [WMb, I4b])], kt) for i, kt in wk], vw_a)
            finish_branch(2)
            o_ = ot[j % 2]
            for br, po in enumerate(Otok):
                P.op("dve", lambda e, br=br, po=po: e.tensor_scalar(out=rs[:, br * 4:br * 4 + 4], in0=bcast_ap(po[:, 64:65], [[260, 128], [65, 4]]),
                                                                     scalar1=1e-30, scalar2=None, op0=ALU.max), reads=[po], writes=[rs])
                P.op("dve", lambda e, br=br: e.reciprocal(out=rs[:, br * 4:br * 4 + 4], in_=rs[:, br * 4:br * 4 + 4]), reads=[rs], writes=[rs])
                P.op("dve", lambda e, br=br, j=j: e.tensor_tensor(out=cf[:, br * 4:br * 4 + 4], in0=rs[:, br * 4:br * 4 + 4],
                                                                in1=bcast_ap(gtt[:, j, br:br + 1], [[NJ * 12, 128], [3, 4]]), op=ALU.mult), reads=[rs, gtt], writes=[cf])
                dst = o_ if br == 0 else otmp
                P.op("dve", lambda e, br=br, po=po, dst=dst: e.tensor_tensor(
                    out=dst[:, :].rearrange("p (g d) -> p g d", g=4), in0=bcast_ap(po[:, 0:1], [[260, 128], [65, 4], [1, 64]]),
                    in1=bcast_ap(cf[:, br * 4:br * 4 + 1], [[12, 128], [1, 4], [0, 64]]), op=ALU.mult), reads=[po, cf], writes=[dst])
                if br > 0:
                    P.op("pool", lambda e, o_=o_: e.tensor_tensor(out=o_[:], in0=o_[:], in1=otmp[:], op=ALU.add), reads=[o_, otmp], writes=[o_])
            P.dma("sp", o_p[:, j, :], o_[:], reads=[o_])

    if do_sample:
        smk = P.sb([128, 75], F32, "smk"); P.dma("sp", smk[:], s_msk[:, :], writes=[smk])
        sad = P.sb([1, 129], F32, "sad"); P.dma("sp", sad[:], s_add[:, :], writes=[sad])
        sqb = P.sb([128, 4], BF16, "sqb"); sqf = P.sb([128, 4], F32, "sqf")
        Pc = P.sb([128, 20], BF16, "Pc"); Pss = P.sb([128, 260], BF16, "Pss"); Pw = P.sb([128, 20], BF16, "Pw")
        Usb = P.sb([4, 132], F32, "Usb"); srs = P.sb([4, 4], F32, "srs"); simp = P.sb([1, 132], F32, "simp"); ssc2 = P.sb([1, 132], F32, "ssc2")
        sm8 = P.sb([1, 16], F32, "sm8"); ssel = P.sb([1, 132], BF16, "ssel"); selx = P.sb([1, TPK], BF16, "selx")
        P.op("dve", lambda e: e.memset(selx[:], 0.0), writes=[selx])
        mcol = P.sb([128, 65], F32, "mcol"); sgt = P.sb([4, 3], F32, "sgt"); scf = P.sb([4, 4], F32, "scf")
        so = P.sb([4, 64], F32, "so"); so2 = P.sb([4, 64], F32, "so2")
        for b in range(NSEQ):
            compress(s_kcv[b])
            P.dma("sp", sqf[:], s_q[b], writes=[sqf])
            P.op("dve", lambda e: e.tensor_copy(out=sqb[:], in_=sqf[:]), reads=[sqf], writes=[sqb])
            P.dma("sp", sgt[:], s_gt[b], writes=[sgt])
            P.op("act", lambda e: e.activation(out=sgt[:], in_=sgt[:], func=AF.Sigmoid), reads=[sgt], writes=[sgt])
            for nt in range(5):
                P.op("pe", lambda e, nt=nt: e.matmul(pS[0][:, nt * 4:nt * 4 + 4], ckT[0:64, nt * 128:(nt + 1) * 128], sqb[0:64, :], start=True, stop=True),
                     reads=[ckT, sqb], writes=[pS[0]])
            P.op("act", lambda e: e.activation(out=Pc[:], in_=pS[0][:, 0:20], func=AF.Exp, scale=SCALE), reads=[pS[0]], writes=[Pc])
            P.op("dve", lambda e: e.tensor_tensor(out=Pc[:, :].rearrange("p (t g) -> p t g", g=4), in0=Pc[:, :].rearrange("p (t g) -> p t g", g=4),
                                                  in1=bcast_ap(smk[:, 0:1], [[75, 128], [1, 5], [0, 4]]), op=ALU.mult), reads=[Pc, smk], writes=[Pc])
            for nt in range(5):
                P.op("pe", lambda e, nt=nt: e.matmul(pOc[0:4, 0:65], Pc[:, nt * 4:nt * 4 + 4], cv_a[:, nt, :], start=(nt == 0), stop=(nt == 4)),
                     reads=[Pc, cv_a], writes=[pOc])
                P.op("pe", lambda e, nt=nt: e.matmul(pU[0:4, 0:129], Pc[:, nt * 4:nt * 4 + 4], covb[:, nt, :], start=(nt == 0), stop=(nt == 4)),
                     reads=[Pc, covb], writes=[pU])
            P.op("dve", lambda e: e.reciprocal(out=srs[:, 0:1], in_=pOc[0:4, 64:65]), reads=[pOc], writes=[srs])
            P.op("act", lambda e: e.copy(out=Usb[:, 0:129], in_=pU[0:4, 0:129]), reads=[pU], writes=[Usb])
            P.op("pe", lambda e: e.matmul(pM[0:1, 0:129], srs[:, 0:1], Usb[:, 0:129], start=True, stop=True), reads=[srs, Usb], writes=[pM])
            P.op("dve", lambda e: e.tensor_tensor(out=simp[:, 0:129], in0=pM[0:1, 0:129], in1=sad[:, :], op=ALU.add), reads=[pM, sad], writes=[simp])
            P.op("dve", lambda e: e.max(out=sm8[:, 0:8], in_=simp[:, 0:129]), reads=[simp], writes=[sm8])
            P.op("dve", lambda e: e.match_replace(out=ssc2[:, 0:129], in_to_replace=sm8[:, 0:8], in_values=simp[:, 0:129], imm_value=NEG),
                 reads=[simp, sm8], writes=[ssc2])
            P.op("dve", lambda e: e.max(out=sm8[:, 8:16], in_=ssc2[:, 0:129]), reads=[ssc2], writes=[sm8])
            P.op("dve", lambda e: e.tensor_scalar(out=ssel[:, 0:129], in0=simp[:, 0:129], scalar1=sm8[:, 15:16], scalar2=None, op0=ALU.is_ge),
                 reads=[simp, sm8], writes=[ssel])
            P.op("dve", lambda e: e.tensor_copy(out=selx[:, 0:TP].rearrange("p (s k) -> p s k", k=64), in_=bcast_ap(ssel[:, 0:1], [[132, 1], [1, 129], [0, 64]])),
                 reads=[ssel], writes=[selx])
            for kt in range(65):
                P.op("pe", lambda e, kt=kt: e.matmul(pS[1][:, kt:kt + 1], selx[0:1, kt * 128:(kt + 1) * 128], one_bf[0:1, 0:1], start=True, stop=True),
                     reads=[selx, one_bf], writes=[pS[1]])
            P.op("dve", lambda e: e.tensor_tensor(out=mcol[:], in0=pS[1][:, 0:65], in1=smk[:, 5:70], op=ALU.mult), reads=[pS[1], smk], writes=[mcol])
            for c0 in range(0, TPK, 2048):
                w = min(2048, TPK - c0)
                load_cast(KSW, KSW[0:64, c0:c0 + w], s_ks[b, :, c0:c0 + w], 64, w)
            for a, na in ((0, 32), (32, 32), (64, 1)):
                load_cast(vs_a, vs_a[:, a:a + na, 0:64], s_vs[b, :, a:a + na, :], 128, (na, 64))
            for kt in range(65):
                P.op("pe", lambda e, kt=kt: e.matmul(pS[0][:, kt * 4:kt * 4 + 4], KSW[0:64, kt * 128:(kt + 1) * 128], sqb[0:64, :], start=True, stop=True),
                     reads=[KSW, sqb], writes=[pS[0]])
            P.op("act", lambda e: e.activation(out=Pss[:], in_=pS[0][:, 0:260], func=AF.Exp, scale=SCALE), reads=[pS[0]], writes=[Pss])
            P.op("dve", lambda e: e.tensor_tensor(out=Pss[:, :].rearrange("p (t g) -> p t g", g=4), in0=Pss[:, :].rearrange("p (t g) -> p t g", g=4),
                                                  in1=bcast_ap(mcol[:, 0:1], [[65, 128], [1, 65], [0, 4]]), op=ALU.mult), reads=[Pss, mcol], writes=[Pss])
            for kt in range(65):
                P.op("pe", lambda e, kt=kt: e.matmul(pOs[0:4, 0:65], Pss[:, kt * 4:kt * 4 + 4], vs_a[:, kt, :], start=(kt == 0), stop=(kt == 64)),
                     reads=[Pss, vs_a], writes=[pOs])
            load_cast(KSW, KSW[64:128, 0:640], s_kw[b], 64, 640, p0=64)
            load_cast(vw_a, vw_a[:, 0:5, 0:64], s_vw[b], 128, (5, 64))
            for kt in range(5):
                P.op("pe", lambda e, kt=kt: e.matmul(pS[1][:, 128 + kt * 4:128 + kt * 4 + 4], KSW[64:128, kt * 128:(kt + 1) * 128], sqb[64:128, :], start=True, stop=True),
                     reads=[KSW, sqb], writes=[pS[1]])
            P.op("act", lambda e: e.activation(out=Pw[:], in_=pS[1][:, 128:148], func=AF.Exp, scale=SCALE), reads=[pS[1]], writes=[Pw])
            P.op("dve", lambda e: e.tensor_tensor(out=Pw[:, :].rearrange("p (t g) -> p t g", g=4), in0=Pw[:, :].rearrange("p (t g) -> p t g", g=4),
                                                  in1=bcast_ap(smk[:, 70:71], [[75, 128], [1, 5], [0, 4]]), op=ALU.mult), reads=[Pw, smk], writes=[Pw])
            for kt in range(5):
                P.op("pe", lambda e, kt=kt: e.matmul(pOw[0:4, 0:65], Pw[:, kt * 4:kt * 4 + 4], vw_a[:, kt, :], start=(kt == 0), stop=(kt == 4)),
                     reads=[Pw, vw_a], writes=[pOw])
            for br, po in enumerate((pOc, pOs, pOw)):
                P.op("dve", lambda e, br=br, po=po: e.reciprocal(out=srs[:, br + 1:br + 2], in_=po[0:4, 64:65]), reads=[po], writes=[srs])
                P.op("dve", lambda e, br=br: e.tensor_tensor(out=scf[:, br:br + 1], in0=srs[:, br + 1:br + 2], in1=sgt[:, br:br + 1], op=ALU.mult),
                     reads=[srs, sgt], writes=[scf])
                if br == 0:
                    P.op("dve", lambda e, po=po: e.tensor_scalar(out=so[:], in0=po[0:4, 0:64], scalar1=scf[:, 0:1], scalar2=None, op0=ALU.mult),
                         reads=[po, scf], writes=[so])
                else:
                    P.op("dve", lambda e, br=br, po=po: e.scalar_tensor_tensor(out=so[:], in0=po[0:4, 0:64], scalar=scf[:, br:br + 1], in1=so[:],
                                                                             op0=ALU.mult, op1=ALU.add), reads=[po, scf, so], writes=[so])
            P.op("act", lambda e: e.copy(out=so2[:], in_=so[:]), reads=[so], writes=[so2])
            P.dma("sp", o_s[b], so2[:], reads=[so2])
    P.emit()
    P.close()
    return nc


def core_consts(c):
    half = c // 4
    kl = np.arange(128)[:, None]; ql = np.arange(128)[None, :]
    tri = (kl <= ql).astype(np.float32); tri2 = (kl >= ql).astype(np.float32)
    one = np.ones((128, 128), np.float32); zero = np.zeros((128, 128), np.float32)
    LM = [tri, zero] if half == 0 else [one, tri]
    WM = [tri2, one, one, one, tri, zero] if half == 0 else [zero, tri2, one, one, one, tri]
    CM = np.zeros((128, NCM, 128), np.float32)
    ADD = np.zeros((128, NJ, 128), np.float32)
    blk = np.arange(128)[None, :]; qq = np.arange(128)[:, None]
    for j in range(NJ):
        t = 2 * j + half
        for nt in range(ntmax(j) + 1):
            CM[:, CM_IDX[(j, nt)], :] = (16 * (128 * nt + kl) + 31 <= 128 * t + ql)
        a = np.zeros((128, 128), np.float32)
        a = np.where(blk > 2 * t + 1, -1.0, a)
        a = np.where(blk == 2 * t + 1, np.where(qq >= 64, 1e9, -1.0), a)
        a = np.where(blk == 2 * t, 1e9, a)
        a = np.where((blk == 2 * t - 1) & (qq < 64), 1e9, a)
        a = np.where(blk == 0, 1e9, a)
        ADD[:, j, :] = a
    neg = lambda m: (-BIG * (1.0 - m)).astype(np.float32)
    LMn = np.stack([neg(m.T) for m in LM], 1)
    WMn = np.stack([neg(m.T) for m in WM], 1)
    CMn = neg(CM.transpose(2, 1, 0))
    return (LMn, WMn, CMn, ADD)


def shared_consts():
    n = np.arange(640)[:, None]; s_ = np.arange(129)[None, :]
    cov = ((16 * n < 64 * s_ + 64) & (16 * n + 32 > 64 * s_)).astype(np.float32)
    cover = cov.reshape(5, 128, 129).transpose(1, 0, 2)
    b = np.arange(128)[:, None, None]; m = np.arange(64)[None, :, None]; k = np.arange(128)[None, None, :]
    t3 = (b == 2 * m + k // 64).astype(np.float32)
    pl = np.arange(128)[:, None]
    msk = np.concatenate([(128 * np.arange(5)[None, :] + pl <= 510), (128 * np.arange(65)[None, :] + pl <= 8192),
                          (128 * np.arange(5)[None, :] + pl <= 512)], axis=1).astype(np.float32)
    sadd = np.zeros((1, 129), np.float32); sadd[0, [0, 127, 128]] = 1e9
    return cover, t3, msk, sadd


def run_C(o, gcmp, gslc, st_win, cmp_w, cmp_pe, kg0, do_prompt=True, do_sample=True):
    nc = build_C(do_prompt, do_sample)
    f = lambda a: np.ascontiguousarray(a, dtype=np.float32)
    cover, t3, msk, sadd = shared_consts()
    ident = np.eye(128, dtype=np.float32)
    pq = o["p_q"].reshape(64, 128, 16, 64); pgt = o["p_gt"].reshape(64, 128, 4, 4, 3)
    pkc = o["p_kvc"].reshape(8192, 2, 4, 64); pks = o["p_kvs"].reshape(8192, 2, 4, 64); pkw = o["p_kvw"].reshape(8192, 2, 4, 64)
    sq = o["s_q"].reshape(32, 16, 64); sgt = o["s_gt"].reshape(32, 4, 4, 3)
    skc = o["s_kvc"].reshape(32, 2, 4, 64); sks = o["s_kvs"].reshape(32, 2, 4, 64); skw = o["s_kvw"].reshape(32, 2, 4, 64)
    stw = st_win.reshape(32, 512, 2, 4, 64)
    maps = []
    for c in range(8):
        h = c % 4; half = c // 4
        tj = 2 * np.arange(NJ) + half
        LM, WM, CM, ADD = core_consts(c)
        m = {"cover": f(cover), "t3": f(t3), "ident": ident, "LM": LM, "WM": WM, "CM": CM, "ADD": ADD, "s_msk": msk, "s_add": sadd,
             "gk0": f(kg0[:, None])}
        q4 = pq[tj][:, :, 4 * h:4 * h + 4]
        qT = q4.transpose(0, 3, 2, 1).reshape(NJ, 64, 512)
        m["qT"] = f(np.concatenate([qT, qT], axis=1))
        m["gts"] = f(pgt[tj][:, :, h].transpose(1, 0, 2, 3).reshape(128, NJ, 12))
        kcv = np.zeros((128, TP), np.float32)
        kcv[0:64, :8192] = pkc[:, 0, h].T; kcv[64:128, :8192] = pkc[:, 1, h].T
        m["kcv"] = kcv
        m["ksw"] = f(np.concatenate([pks[:, 0, h].T, pkw[:, 0, h].T], axis=0))
        m["vs"] = f(pks[:, 1, h].reshape(64, 128, 64).transpose(1, 0, 2))
        m["vw"] = f(pkw[:, 1, h].reshape(64, 128, 64).transpose(1, 0, 2))
        m["wkv"] = f(np.concatenate([cmp_w[0].transpose(1, 0, 2), cmp_w[1].transpose(1, 0, 2)], axis=0))
        m["peT"] = f(np.concatenate([cmp_pe[:, 0, :].T, cmp_pe[:, 1, :].T], axis=0))
        bs = np.arange(16 * half, 16 * half + 16)
        s_kcv = np.zeros((NSEQ, 128, TP), np.float32); s_ks = np.zeros((NSEQ, 64, TPK), np.float32)
        s_vs = np.zeros((NSEQ, TPK, 64), np.float32); s_kw = np.zeros((NSEQ, 64, 640), np.float32); s_vw = np.zeros((NSEQ, 640, 64), np.float32)
        for i, b in enumerate(bs):
            s_kcv[i, 0:64, :8192] = gcmp[b, :, 0, h].T; s_kcv[i, 0:64, 8192] = skc[b, 0, h]
            s_kcv[i, 64:128, :8192] = gcmp[b, :, 1, h].T; s_kcv[i, 64:128, 8192] = skc[b, 1, h]
            s_ks[i, :, :8192] = gslc[b, :, 0, h].T; s_ks[i, :, 8192] = sks[b, 0, h]
            s_vs[i, :8192] = gslc[b, :, 1, h]; s_vs[i, 8192] = sks[b, 1, h]
            s_kw[i, :, :512] = stw[b, :, 0, h].T; s_kw[i, :, 512] = skw[b, 0, h]
            s_vw[i, :512] = stw[b, :, 1, h]; s_vw[i, 512] = skw[b, 1, h]
        m["s_kcv"] = s_kcv; m["s_ks"] = s_ks
        m["s_vs"] = f(s_vs.reshape(NSEQ, 65, 128, 64).transpose(0, 2, 1, 3))
        m["s_kw"] = s_kw; m["s_vw"] = f(s_vw.reshape(NSEQ, 5, 128, 64).transpose(0, 2, 1, 3))
        sqT = sq[bs][:, 4 * h:4 * h + 4].transpose(0, 2, 1)
        m["s_q"] = f(np.concatenate([sqT, sqT], axis=1))
        m["s_gt"] = f(sgt[bs][:, h])
        maps.append(m)
    res = run_bass_kernel_spmd(nc, maps, core_ids=list(range(8)))
    op = np.zeros((64, 128, 16, 64), np.float32); os_ = np.zeros((32, 16, 64), np.float32)
    for c, r in enumerate(res.results):
        h = c % 4; half = c // 4
        tj = 2 * np.arange(NJ) + half
        oc = r["o_p"].reshape(128, NJ, 4, 64).transpose(1, 0, 2, 3)
        op[tj, :, 4 * h:4 * h + 4] = oc
        os_[16 * half:16 * half + 16, 4 * h:4 * h + 4] = r["o_s"]
    return op.reshape(8192, 1024), os_.reshape(32, 1024)


DFF = 5632
NQ = 11
FCH = 44 // NQ


def tiles_to_T(P, src, width, col0, dstT, kc0, gbc, bufs, idb, pst, norm):
    xts, xnb, sq, ssq, rstd = bufs
    nkc = width // 128
    for ti, (r0, rows) in enumerate(TILES):
        xt = xts[ti % 2]; xb = xnb[ti % 2]
        P.dma("sp", xt[:rows, :width], src[r0:r0 + rows, col0:col0 + width], writes=[xt])
        if norm:
            rms_rows(P, xt, rows, width, gbc, xb, sq, ssq, rstd)
        else:
            P.op("dve", lambda e, xt=xt, xb=xb, rows=rows: e.tensor_copy(out=xb[:rows, :width], in_=xt[:rows, :width]),
                 reads=[xt], writes=[xb])
        for b0 in range(0, nkc, 8):
            pt = pst[(b0 // 8) % 2]
            nb = min(8, nkc - b0)
            for j in range(nb):
                kc = b0 + j
                P.op("pe", lambda e, pt=pt, j=j, kc=kc, xb=xb, rows=rows: e.transpose(
                    out=pt[:, j * 128:j * 128 + rows], in_=xb[:rows, kc * 128:(kc + 1) * 128], identity=idb[:rows, :rows]),
                    reads=[xb, idb], writes=[pt])
            src_ap = pt[:, :].rearrange("p (a b) -> p a b", b=128)[:, :nb, :rows]
            dst = dstT[:, kc0 + b0:kc0 + b0 + nb, r0:r0 + rows]
            if (b0 // 8) % 2 == 0:
                P.op("act", lambda e, s=src_ap, d=dst: e.copy(out=d, in_=s), reads=[pt], writes=[dstT])
            else:
                P.op("dve", lambda e, s=src_ap, d=dst: e.tensor_copy(out=d, in_=s), reads=[pt], writes=[dstT])


def build_D():
    nc = bass.Bass("TRN2", target_bir_lowering=False)
    yg = nc.dram_tensor("yg", [NTOK, 1024], F32, kind="ExternalInput").ap()
    oa = nc.dram_tensor("oa", [NTOK, 1024], F32, kind="ExternalInput").ap()
    h = nc.dram_tensor("h", [NTOK, D], F32, kind="ExternalInput").ap()
    gs = nc.dram_tensor("gs", [1024], F32, kind="ExternalInput").ap()
    g2 = nc.dram_tensor("g2", [D], F32, kind="ExternalInput").ap()
    Wo = nc.dram_tensor("Wo", [D, D], F32, kind="ExternalInput").ap()
    Wgu = nc.dram_tensor("Wgu", [D, 2 * DFF], F32, kind="ExternalInput").ap()
    Wd = nc.dram_tensor("Wd", [DFF, D], F32, kind="ExternalInput").ap()
    ident = nc.dram_tensor("ident", [128, 128], F32, kind="ExternalInput").ap()
    hout = nc.dram_tensor("hout", [NTOK, D], F32, kind="ExternalOutput").ap()
    P = Prog(nc)
    gsbc = P.sb([128, 1024], F32, "gsbc")
    g2bc = P.sb([128, D], F32, "g2bc")
    idf = P.sb([128, 128], F32, "idf")
    idb = P.sb([128, 128], BF16, "idb")
    xT = P.sb([128, 16, NTOK], BF16, "xT")
    h1 = P.sb([128, 9, D], F32, "h1")
    xt0 = P.sb([128, D], F32, "xt0"); xts = [xt0, xt0]
    xnb = [P.sb([128, D], BF16, f"xnb{i}") for i in range(2)]
    sq = P.sb([128, D], BF16, "sq")
    ssq = P.sb([128, 1], F32, "ssq")
    rstd = P.sb([128, 1], F32, "rstd")
    bufs = (xts, xnb, sq, ssq, rstd)
    pst = [P.ps([128, 1024], BF16, f"pst{i}") for i in range(2)]
    psm = [P.ps([128, 512], F32, f"psm{i}") for i in range(6)]
    wst = [P.sb([128, 4, 512], F32, f"wst{i}") for i in range(2)]
    wo_bf = P.sb([128, 16, 512], BF16, "wo_bf")
    wg_bf = [P.sb([128, 16, 128], BF16, f"wg_bf{i}") for i in range(2)]
    wv_bf = [P.sb([128, 16, 128], BF16, f"wv_bf{i}") for i in range(2)]
    actT = P.sb([128, FCH, NTOK], BF16, "actT")
    sg = [P.sb([128, 512], F32, f"sg{i}") for i in range(2)]

    P.dma("sp", gsbc[:], gs.partition_broadcast(128), writes=[gsbc])
    P.dma("sp", g2bc[:], g2.partition_broadcast(128), writes=[g2bc])
    P.dma("sp", idf[:], ident[:, :], writes=[idf])
    P.op("dve", lambda e: e.tensor_copy(out=idb[:], in_=idf[:]), reads=[idf], writes=[idb])
    cnt = [0]

    def load_w(dst, src_rows_view, nk, cw):
        for k0 in range(0, nk, 4):
            nkk = min(4, nk - k0)
            ws = wst[cnt[0] % 2]
            P.dma("sp", ws[:, :nkk, :cw], src_rows_view[:, k0:k0 + nkk, :], writes=[ws])
            eng = ("pool", "dve", "pool", "act")[cnt[0] % 4]
            if eng == "act":
                P.op("act", lambda e, ws=ws, k0=k0, nkk=nkk: e.copy(out=dst[:, k0:k0 + nkk, :cw], in_=ws[:, :nkk, :cw]), reads=[ws], writes=[dst])
            else:
                P.op(eng, lambda e, ws=ws, k0=k0, nkk=nkk: e.tensor_copy(out=dst[:, k0:k0 + nkk, :cw], in_=ws[:, :nkk, :cw]), reads=[ws], writes=[dst])
            cnt[0] += 1

    tiles_to_T(P, yg, 1024, 0, xT, 0, gsbc, bufs, idb, pst, True)
    tiles_to_T(P, oa, 1024, 0, xT, 8, None, bufs, idb, pst, False)
    k = 0
    for cg in range(4):
        load_w(wo_bf, Wo[:, cg * 512:(cg + 1) * 512].rearrange("(kc p) c -> p kc c", p=128), 16, 512)
        for ti, (r0, rows) in enumerate(TILES):
            pm = psm[k % 6]; k += 1
            for kc in range(16):
                P.op("pe", lambda e, pm=pm, kc=kc, r0=r0, rows=rows: e.matmul(
                    pm[:rows, :], xT[:, kc, r0:r0 + rows], wo_bf[:, kc, :], start=(kc == 0), stop=(kc == 15)),
                    reads=[xT, wo_bf], writes=[pm])
            xt = xts[k % 2]
            P.dma("sp", xt[:rows, :512], h[r0:r0 + rows, cg * 512:(cg + 1) * 512], writes=[xt])
            P.op("dve", lambda e, pm=pm, xt=xt, ti=ti, rows=rows, cg=cg: e.tensor_tensor(
                out=h1[:rows, ti, cg * 512:(cg + 1) * 512], in0=pm[:rows, :], in1=xt[:rows, :512], op=ALU.add),
                reads=[pm, xt], writes=[h1])
    for ti, (r0, rows) in enumerate(TILES):
        xb = xnb[ti % 2]
        h1t = Buf(h1.t, "h1v")
        P.op("act", lambda e, ti=ti, rows=rows: e.activation(out=sq[:rows, :], in_=h1[:rows, ti, :], func=AF.Square, accum_out=ssq[:rows, :]),
             reads=[h1], writes=[sq, ssq])
        P.op("dve", lambda e, rows=rows: e.tensor_scalar(out=rstd[:rows, :], in0=ssq[:rows, :], scalar1=1.0 / D, scalar2=1e-6, op0=ALU.mult, op1=ALU.add),
             reads=[ssq], writes=[rstd])
        P.op("act", lambda e, rows=rows: e.activation(out=rstd[:rows, :], in_=rstd[:rows, :], func=AF.Sqrt), reads=[rstd], writes=[rstd])
        P.op("dve", lambda e, rows=rows: e.reciprocal(out=rstd[:rows, :], in_=rstd[:rows, :]), reads=[rstd], writes=[rstd])
        P.op("dve", lambda e, ti=ti, rows=rows, xb=xb: e.scalar_tensor_tensor(out=xb[:rows, :], in0=h1[:rows, ti, :], scalar=rstd[:rows, :],
                                                                      in1=g2bc[:rows, :], op0=ALU.mult, op1=ALU.mult),
             reads=[h1, rstd, g2bc], writes=[xb])
        for half in range(2):
            pt = pst[half]
            for j in range(8):
                kc = half * 8 + j
                P.op("pe", lambda e, pt=pt, j=j, kc=kc, xb=xb, rows=rows: e.transpose(
                    out=pt[:, j * 128:j * 128 + rows], in_=xb[:rows, kc * 128:(kc + 1) * 128], identity=idb[:rows, :rows]),
                    reads=[xb, idb], writes=[pt])
            src_ap = pt[:, :].rearrange("p (a b) -> p a b", b=128)[:, :, :rows]
            dst = xT[:, half * 8:half * 8 + 8, r0:r0 + rows]
            if half == 0:
                P.op("act", lambda e, s=src_ap, d=dst: e.copy(out=d, in_=s), reads=[pt], writes=[xT])
            else:
                P.op("dve", lambda e, s=src_ap, d=dst: e.tensor_copy(out=d, in_=s), reads=[pt], writes=[xT])
    TG = [(0, 512), (512, 512), (1024, 4)]
    for qi in range(NQ):
        for fi in range(FCH):
            fc = qi * FCH + fi
            wg = wg_bf[fc % 2]; wv = wv_bf[fc % 2]
            load_w(wg, Wgu[:, fc * 128:(fc + 1) * 128].rearrange("(kc p) c -> p kc c", p=128), 16, 128)
            load_w(wv, Wgu[:, DFF + fc * 128:DFF + (fc + 1) * 128].rearrange("(kc p) c -> p kc c", p=128), 16, 128)
            for gi, (t0, tw) in enumerate(TG):
                pg = psm[(2 * gi) % 6]; pv = psm[(2 * gi + 1) % 6]
                for kc in range(16):
                    P.op("pe", lambda e, pg=pg, kc=kc, wg=wg, t0=t0, tw=tw: e.matmul(
                        pg[:, :tw], wg[:, kc, :], xT[:, kc, t0:t0 + tw], start=(kc == 0), stop=(kc == 15)),
                        reads=[xT, wg], writes=[pg])
                for kc in range(16):
                    P.op("pe", lambda e, pv=pv, kc=kc, wv=wv, t0=t0, tw=tw: e.matmul(
                        pv[:, :tw], wv[:, kc, :], xT[:, kc, t0:t0 + tw], start=(kc == 0), stop=(kc == 15)),
                        reads=[xT, wv], writes=[pv])
                s = sg[gi % 2]
                P.op("act", lambda e, pg=pg, s=s, tw=tw: e.activation(out=s[:, :tw], in_=pg[:, :tw], func=AF.Silu), reads=[pg], writes=[s])
                P.op("dve", lambda e, pv=pv, s=s, fi=fi, t0=t0, tw=tw: e.tensor_tensor(
                    out=actT[:, fi, t0:t0 + tw], in0=s[:, :tw], in1=pv[:, :tw], op=ALU.mult), reads=[s, pv], writes=[actT])
        for cg in range(4):
            load_w(wo_bf, Wd[qi * FCH * 128:(qi + 1) * FCH * 128, cg * 512:(cg + 1) * 512].rearrange("(kc p) c -> p kc c", p=128), FCH, 512)
            for ti, (r0, rows) in enumerate(TILES):
                pm = psm[k % 6]; k += 1
                for fi in range(FCH):
                    P.op("pe", lambda e, pm=pm, fi=fi, r0=r0, rows=rows: e.matmul(
                        pm[:rows, :], actT[:, fi, r0:r0 + rows], wo_bf[:, fi, :], start=(fi == 0), stop=(fi == FCH - 1)),
                        reads=[actT, wo_bf], writes=[pm])
                P.op("dve", lambda e, pm=pm, ti=ti, rows=rows, cg=cg: e.tensor_tensor(
                    out=h1[:rows, ti, cg * 512:(cg + 1) * 512], in0=pm[:rows, :], in1=h1[:rows, ti, cg * 512:(cg + 1) * 512], op=ALU.add),
                    reads=[pm, h1], writes=[h1])
    for ti, (r0, rows) in enumerate(TILES):
        P.dma("sp", hout[r0:r0 + rows, :], h1[:rows, ti, :], reads=[h1])
    P.emit()
    P.close()
    return nc


def run_D(yg_p, yg_s, oa_p, oa_s, h_p, h_s, gs, g2, Wo, Wgu, Wd):
    nc = build_D()
    ident = np.eye(128, dtype=np.float32)
    f = lambda a: np.ascontiguousarray(a, dtype=np.float32)
    maps = []
    for c in range(8):
        cat = lambda p, s: f(np.concatenate([p[c * 1024:(c + 1) * 1024], s[c * 4:(c + 1) * 4]], axis=0))
        maps.append({"yg": cat(yg_p, yg_s), "oa": cat(oa_p, oa_s), "h": cat(h_p, h_s), "gs": f(gs), "g2": f(g2),
                     "Wo": f(Wo), "Wgu": f(Wgu), "Wd": f(Wd), "ident": ident})
    res = run_bass_kernel_spmd(nc, maps, core_ids=list(range(8)))
    hp = np.concatenate([r["hout"][:1024] for r in res.results], axis=0)
    hs = np.concatenate([r["hout"][1024:] for r in res.results], axis=0)
    return hp, hs


def kernel(x_prompt, x_sample, cache_cmp_kv, cache_slc_kv, state_win_kv, state_ssm, state_conv, page_table,
           norm1_g, w_in, conv_w, conv_b, dt_bias, a_log, d_skip, ssm_norm_g, q_norm_g, k_norm_g, cmp_pe, cmp_w,
           w_out, norm2_g, w_gu, w_down):
    A = lambda a: np.asarray(a)
    hp = A(x_prompt)[0].astype(np.float32, copy=False); hs = A(x_sample)[:, 0].astype(np.float32, copy=False)
    pt = A(page_table)
    gath = run_G([A(cache_cmp_kv)[0], A(cache_slc_kv)[0], A(cache_cmp_kv)[1], A(cache_slc_kv)[1]], pt)
    depth = 2
    cmp_p = np.zeros((depth, 1, 8192, 2, 4, 64), np.float32); cmp_s = np.zeros((depth, 32, 1, 2, 4, 64), np.float32)
    slc_p = np.zeros_like(cmp_p); slc_s = np.zeros_like(cmp_s)
    win_p = np.zeros((depth, 1, 512, 2, 4, 64), np.float32); win_s = np.zeros((depth, 32, 512, 2, 4, 64), np.float32)
    ssm_p = np.zeros((depth, 1, 16, 64, 128), np.float32); ssm_s = np.zeros((depth, 32, 16, 64, 128), np.float32)
    conv_p = np.zeros((depth, 1, 3, 2048), np.float32); conv_s = np.zeros((depth, 32, 3, 2048), np.float32)
    for l in range(depth):
        o = run_A(hp, hs, A(norm1_g)[l], A(w_in)[l], A(q_norm_g)[l], A(k_norm_g)[l], A(state_win_kv)[l], A(state_conv)[l])
        ygp, sp, ygs, ss = run_B(o["p_z"], o["p_xbc"], o["p_dt"], o["s_z"], o["s_xbc"], o["s_dt"], A(state_conv)[l], A(state_ssm)[l],
                                 A(conv_w)[l], A(conv_b)[l], A(dt_bias)[l], A(a_log)[l], A(d_skip)[l])
        oap, oas = run_C(o, gath[2 * l], gath[2 * l + 1], A(state_win_kv)[l], A(cmp_w)[l], A(cmp_pe)[l], A(k_norm_g)[l][0])
        cmp_p[l, 0] = o["p_kvc"].reshape(8192, 2, 4, 64); cmp_s[l, :, 0] = o["s_kvc"].reshape(32, 2, 4, 64)
        slc_p[l, 0] = o["p_kvs"].reshape(8192, 2, 4, 64); slc_s[l, :, 0] = o["s_kvs"].reshape(32, 2, 4, 64)
        win_p[l, 0] = o["p_kvw"][-512:].reshape(512, 2, 4, 64)
        win_s[l] = np.concatenate([o["sw_keep"], o["s_kvw"][:, None]], axis=1).reshape(32, 512, 2, 4, 64)
        ssm_p[l, 0] = sp; ssm_s[l] = ss
        conv_p[l, 0] = o["p_xbc"][-3:]
        conv_s[l] = np.concatenate([o["sc_keep"], o["s_xbc"][:, None]], axis=1)
        hp, hs = run_D(ygp, ygs, oap, oas, hp, hs, A(ssm_norm_g)[l], A(norm2_g)[l], A(w_out)[l], A(w_gu)[l], A(w_down)[l])
    return (hp[None].astype(np.float32), hs[:, None].astype(np.float32), cmp_p, cmp_s, slc_p, slc_s, win_p, win_s, ssm_p, ssm_s, conv_p, conv_s)
```

```python
from concourse.bass_utils import run_bass_kernel_spmd
from contextlib import ExitStack
import numpy as np
import concourse.bass as bass
import concourse.mybir as mybir

F32 = mybir.dt.float32
BF16 = mybir.dt.bfloat16
I32 = mybir.dt.int32
U32 = mybir.dt.uint32
AF = mybir.ActivationFunctionType
ALU = mybir.AluOpType
AX = mybir.AxisListType

ENGS = ("pe", "act", "dve", "pool", "sp")


class Buf:
    def __init__(self, t, name=""):
        self.t = t
        self.name = name
        self.w = None
        self.r = {}

    def __getitem__(self, idx):
        return self.t[idx]


class Prog:
    def __init__(self, nc, n_dma_sems=12, self_sync=True):
        self.nc = nc
        self.st = ExitStack()
        self.ops = {e: [] for e in ENGS}
        self.cnt = {e: 0 for e in ENGS}
        self.waited = {e: {} for e in ENGS}
        self.sem = {}
        for e in ENGS:
            if e != "sp":
                self.sem[e] = self.st.enter_context(nc.semaphore("c_" + e))
        self.dq = {}
        for q in ("sp", "act", "pool"):
            sems = [self.st.enter_context(nc.semaphore(f"d_{q}{i}")) for i in range(n_dma_sems)]
            self.dq[q] = {"sems": sems, "m": 0}
        self.self_sync = self_sync
        self.nalloc = 0

    def sb(self, shape, dt, name=None):
        self.nalloc += 1
        name = name or f"sb{self.nalloc}"
        return Buf(self.st.enter_context(self.nc.sbuf_tensor(name, list(shape), dt)), name)

    def ps(self, shape, dt, name=None):
        self.nalloc += 1
        name = name or f"ps{self.nalloc}"
        return Buf(self.st.enter_context(self.nc.psum_tensor(name, list(shape), dt)), name)

    def dram(self, name, shape, dt, kind="Internal"):
        return Buf(self.nc.dram_tensor(name, list(shape), dt, kind=kind).ap(), name)

    def _deps(self, eng, reads, writes):
        waits = []

        def need(ev):
            if ev is None:
                return
            sem, val, key = ev
            if key == eng and (eng == "pe" or not self.self_sync):
                return
            if self.waited[eng].get(key, 0) >= val:
                return
            self.waited[eng][key] = val
            waits.append((sem, val))

        for b in reads:
            need(b.w)
        for b in writes:
            need(b.w)
            for ev in b.r.values():
                need(ev)
        return waits

    def _commit(self, ev, reads, writes):
        for b in reads:
            b.r[ev[2]] = ev
        for b in writes:
            b.w = ev
            b.r = {}

    def op(self, eng, fn, reads=(), writes=()):
        waits = self._deps(eng, reads, writes)
        self.cnt[eng] += 1
        ev = (self.sem[eng], self.cnt[eng], eng)
        self.ops[eng].append((waits, fn, (self.sem[eng], 1)))
        self._commit(ev, reads, writes)
        return ev

    def dma(self, q, out, in_, reads=(), writes=(), indirect=None, **kw):
        D = self.dq[q]
        m = D["m"]
        D["m"] += 1
        ns = len(D["sems"])
        sem = D["sems"][m % ns]
        key = f"d_{q}{m % ns}"
        tgt = 16 * (m // ns + 1)
        D.setdefault("tg", {})[m % ns] = tgt
        waits = self._deps(q, reads, writes)
        if m >= ns and self.waited[q].get(key, 0) < tgt - 16:
            self.waited[q][key] = tgt - 16
            waits.append((sem, tgt - 16))
        if indirect is None:
            fn = lambda e: e.dma_start(out=out, in_=in_, **kw)
        else:
            fn = indirect
        self.ops[q].append((waits, fn, (sem, 16)))
        ev = (sem, tgt, key)
        self._commit(ev, reads, writes)
        return ev

    def barrier(self):
        evs = []
        for e in ENGS:
            if e != "sp" and self.cnt[e] > 0:
                evs.append((self.sem[e], self.cnt[e], e))
        for q, D in self.dq.items():
            for i, tg in D.get("tg", {}).items():
                evs.append((D["sems"][i], tg, f"d_{q}{i}"))
        for e in ENGS:
            waits = []
            for sem, val, key in evs:
                if self.waited[e].get(key, 0) >= val:
                    continue
                self.waited[e][key] = val
                waits.append((sem, val))
            if waits:
                self.ops[e].append((waits, None, None))

    def emit(self):
        nc = self.nc
        self.barrier()
        ops = self.ops

        def run(name, e):
            for waits, fn, inc in ops[name]:
                for sem, val in waits:
                    e.wait_ge(sem, val)
                if fn is not None:
                    ins = fn(e)
                    ins.then_inc(inc[0], inc[1])

        with nc.Block() as block:
            @block.tensor
            def _(e):
                run("pe", e)

            @block.scalar
            def _(e):
                run("act", e)

            @block.vector
            def _(e):
                run("dve", e)

            @block.gpsimd
            def _(e):
                run("pool", e)

            @block.sync
            def _(e):
                run("sp", e)
        self.ops = {e: [] for e in ENGS}

    def close(self):
        self.st.close()


def bcast_ap(ap, dims):
    return bass.AP(ap.tensor, ap.offset, dims)


D = 2048
NTOK = 1028
TILES = [(i * 128, 128) for i in range(8)] + [(1024, 4)]
EPS = 1e-6


def rms_rows(P, xt, rows, width, gbc, out_bf, sq_scr, ssq, rstd):
    P.op("act", lambda e: e.activation(out=sq_scr[:rows, :width], in_=xt[:rows, :width], func=AF.Square,
                                       accum_out=ssq[:rows, :]),
         reads=[xt], writes=[sq_scr, ssq])
    P.op("dve", lambda e: e.tensor_scalar(out=rstd[:rows, :], in0=ssq[:rows, :], scalar1=1.0 / width, scalar2=EPS,
                                          op0=ALU.mult, op1=ALU.add), reads=[ssq], writes=[rstd])
    P.op("act", lambda e: e.activation(out=rstd[:rows, :], in_=rstd[:rows, :], func=AF.Sqrt), reads=[rstd], writes=[rstd])
    P.op("dve", lambda e: e.reciprocal(out=rstd[:rows, :], in_=rstd[:rows, :]), reads=[rstd], writes=[rstd])
    P.op("dve", lambda e: e.scalar_tensor_tensor(out=out_bf[:rows, :width], in0=xt[:rows, :width], scalar=rstd[:rows, :],
                                                 in1=gbc[:rows, :width], op0=ALU.mult, op1=ALU.mult),
         reads=[xt, rstd, gbc], writes=[out_bf])


def build_A(ncols, norm_groups=None):
    norm_groups = norm_groups or {}
    nc = bass.Bass("TRN2", target_bir_lowering=False)
    gains = nc.dram_tensor("gains", [3 * 64], F32, kind="ExternalInput").ap()
    sw_in = nc.dram_tensor("sw_in", [4, 512, 512], F32, kind="ExternalInput").ap()
    sc_in = nc.dram_tensor("sc_in", [4, 3, 2048], F32, kind="ExternalInput").ap()
    sw_out = nc.dram_tensor("sw_out", [4, 511, 512], F32, kind="ExternalOutput").ap()
    sc_out = nc.dram_tensor("sc_out", [4, 2, 2048], F32, kind="ExternalOutput").ap()
    h = nc.dram_tensor("h", [NTOK, D], F32, kind="ExternalInput").ap()
    g = nc.dram_tensor("g", [D], F32, kind="ExternalInput").ap()
    W = nc.dram_tensor("W", [D, ncols], F32, kind="ExternalInput").ap()
    ident = nc.dram_tensor("ident", [128, 128], F32, kind="ExternalInput").ap()
    u = nc.dram_tensor("u", [NTOK, ncols], F32, kind="ExternalOutput").ap()
    P = Prog(nc)
    gbc = P.sb([128, D], F32, "gbc")
    idf = P.sb([128, 128], F32, "idf")
    idb = P.sb([128, 128], BF16, "idb")
    xnT = P.sb([128, 16, NTOK], BF16, "xnT")
    xts = [P.sb([128, D], F32, f"xt{i}") for i in range(2)]
    xnb = [P.sb([128, D], BF16, f"xnb{i}") for i in range(2)]
    sq = P.sb([128, D], BF16, "sq")
    ssq = P.sb([128, 1], F32, "ssq")
    rstd = P.sb([128, 1], F32, "rstd")
    pst = [P.ps([128, 1024], BF16, f"pst{i}") for i in range(2)]
    psm = [P.ps([128, 512], F32, f"psm{i}") for i in range(4)]
    wst = [P.sb([128, 16, 512], F32, f"wst{i}") for i in range(2)]
    wbf = [P.sb([128, 16, 512], BF16, f"wbf{i}") for i in range(2)]
    ot = [P.sb([128, 512], F32, f"ot{i}") for i in range(4)]
    gn = P.sb([128, 3, 64], F32, "gn")
    nsq = P.sb([128, 512], F32, "nsq"); ns8 = P.sb([128, 8], F32, "ns8")
    P.dma("sp", gn[:], gains.partition_broadcast(128), writes=[gn])
    for b in range(4):
        P.dma("act", sw_out[b], sw_in[b, 1:512, :])
        P.dma("act", sc_out[b], sc_in[b, 1:3, :])

    P.dma("sp", gbc[:], g.partition_broadcast(128), writes=[gbc])
    P.dma("sp", idf[:], ident[:, :], writes=[idf])
    P.op("dve", lambda e: e.tensor_copy(out=idb[:], in_=idf[:]), reads=[idf], writes=[idb])
    for ti, (r0, rows) in enumerate(TILES):
        xt = xts[ti % 2]; xb = xnb[ti % 2]
        P.dma("sp", xt[:rows, :], h[r0:r0 + rows, :], writes=[xt])
        rms_rows(P, xt, rows, D, gbc, xb, sq, ssq, rstd)
        for half in range(2):
            pt = pst[half]
            for j in range(8):
                kc = half * 8 + j
                P.op("pe", lambda e, pt=pt, j=j, kc=kc, xb=xb, rows=rows: e.transpose(
                    out=pt[:, j * 128:j * 128 + rows], in_=xb[:rows, kc * 128:(kc + 1) * 128], identity=idb[:rows, :rows]),
                    reads=[xb, idb], writes=[pt])
            eng = "act" if half == 0 else "dve"
            src = pt[:, :].rearrange("p (a b) -> p a b", b=128)[:, :, :rows]
            dst = xnT[:, half * 8:half * 8 + 8, r0:r0 + rows]
            if eng == "act":
                P.op("act", lambda e, src=src, dst=dst: e.copy(out=dst, in_=src), reads=[pt], writes=[xnT])
            else:
                P.op("dve", lambda e, src=src, dst=dst: e.tensor_copy(out=dst, in_=src), reads=[pt], writes=[xnT])
    ngrp = (ncols + 511) // 512
    k = 0
    for gi in range(ngrp):
        c0 = gi * 512
        cw = min(512, ncols - c0)
        ws = wst[gi % 2]; wb = wbf[gi % 2]
        P.dma("sp", ws[:, :, :cw], W[:, c0:c0 + cw].rearrange("(kc p) c -> p kc c", p=128), writes=[ws])
        for q4 in range(4):
            eng = ("pool", "dve", "pool", "act")[q4]
            sl = slice(q4 * 4, q4 * 4 + 4)
            if eng == "act":
                P.op("act", lambda e, ws=ws, wb=wb, sl=sl, cw=cw: e.copy(out=wb[:, sl, :cw], in_=ws[:, sl, :cw]), reads=[ws], writes=[wb])
            else:
                P.op(eng, lambda e, ws=ws, wb=wb, sl=sl, cw=cw: e.tensor_copy(out=wb[:, sl, :cw], in_=ws[:, sl, :cw]), reads=[ws], writes=[wb])
        for ti, (r0, rows) in enumerate(TILES):
            pm = psm[k % 4]; o = ot[k % 4]
            for kc in range(16):
                P.op("pe", lambda e, pm=pm, kc=kc, wb=wb, r0=r0, rows=rows, cw=cw: e.matmul(
                    pm[:rows, :cw], xnT[:, kc, r0:r0 + rows], wb[:, kc, :cw], start=(kc == 0), stop=(kc == 15)),
                    reads=[xnT, wb], writes=[pm])
            if k % 2 == 0:
                P.op("act", lambda e, pm=pm, o=o, rows=rows, cw=cw: e.copy(out=o[:rows, :cw], in_=pm[:rows, :cw]), reads=[pm], writes=[o])
            else:
                P.op("dve", lambda e, pm=pm, o=o, rows=rows, cw=cw: e.tensor_copy(out=o[:rows, :cw], in_=pm[:rows, :cw]), reads=[pm], writes=[o])
            if gi in norm_groups:
                nh, gidx = norm_groups[gi]
                w = nh * 64
                P.op("act", lambda e, o=o, rows=rows, w=w: e.activation(out=nsq[:rows, :w], in_=o[:rows, :w], func=AF.Square), reads=[o], writes=[nsq])
                P.op("dve", lambda e, rows=rows, nh=nh, w=w: e.tensor_reduce(out=ns8[:rows, :nh], in_=nsq[:rows, :w].rearrange("p (h d) -> p h d", d=64),
                                                                           axis=AX.X, op=ALU.add), reads=[nsq], writes=[ns8])
                P.op("dve", lambda e, rows=rows, nh=nh: e.tensor_scalar(out=ns8[:rows, :nh], in0=ns8[:rows, :nh], scalar1=1.0 / 64, scalar2=EPS, op0=ALU.mult, op1=ALU.add),
                     reads=[ns8], writes=[ns8])
                P.op("act", lambda e, rows=rows, nh=nh: e.activation(out=ns8[:rows, :nh], in_=ns8[:rows, :nh], func=AF.Sqrt), reads=[ns8], writes=[ns8])
                P.op("dve", lambda e, rows=rows, nh=nh: e.reciprocal(out=ns8[:rows, :nh], in_=ns8[:rows, :nh]), reads=[ns8], writes=[ns8])
                ov = lambda o=o, rows=rows, w=w: o[:rows, :w].rearrange("p (h d) -> p h d", d=64)
                P.op("dve", lambda e, o=o, rows=rows, nh=nh, w=w: e.tensor_tensor(
                    out=o[:rows, :w].rearrange("p (h d) -> p h d", d=64), in0=o[:rows, :w].rearrange("p (h d) -> p h d", d=64),
                    in1=bcast_ap(ns8[:rows, 0:1], [[8, rows], [1, nh], [0, 64]]), op=ALU.mult), reads=[o, ns8], writes=[o])
                P.op("dve", lambda e, o=o, rows=rows, nh=nh, w=w, gidx=gidx: e.tensor_tensor(
                    out=o[:rows, :w].rearrange("p (h d) -> p h d", d=64), in0=o[:rows, :w].rearrange("p (h d) -> p h d", d=64),
                    in1=bcast_ap(gn[:rows, gidx, 0:1], [[192, rows], [0, nh], [1, 64]]), op=ALU.mult), reads=[o, gn], writes=[o])
            P.dma("sp", u[r0:r0 + rows, c0:c0 + cw], o[:rows, :cw], reads=[o])
            k += 1
    P.emit()
    P.close()
    return nc


PERM = np.concatenate([np.arange(0, 1024), np.arange(1024, 3072), np.arange(3088, 4112), np.arange(4112, 4624),
                       np.arange(4624, 5136), np.arange(5136, 5648), np.arange(3072, 3088), np.arange(5648, 5696)])
NORMG = {6: (8, 0), 7: (8, 0), 9: (4, 1), 10: (4, 2)}


def run_A(h_p, h_s, g, W, qg, kg, st_win, st_conv):
    nc = build_A(5696, NORMG)
    ident = np.eye(128, dtype=np.float32)
    f = lambda a: np.ascontiguousarray(a, dtype=np.float32)
    g = f(g); Wp = f(W[:, PERM])
    gains = f(np.concatenate([qg, kg[1], kg[2]]))
    maps = []
    for c in range(8):
        hc = np.concatenate([h_p[c * 1024:(c + 1) * 1024], h_s[c * 4:(c + 1) * 4]], axis=0)
        maps.append({"h": f(hc), "g": g, "W": Wp, "ident": ident, "gains": gains,
                     "sw_in": f(st_win[4 * c:4 * c + 4].reshape(4, 512, 512)), "sc_in": f(st_conv[4 * c:4 * c + 4])})
    res = run_bass_kernel_spmd(nc, maps, core_ids=list(range(8)))
    up = np.concatenate([r["u"][:1024] for r in res.results], axis=0)
    us = np.concatenate([r["u"][1024:] for r in res.results], axis=0)
    names = (("z", 0, 1024), ("xbc", 1024, 3072), ("q", 3072, 4096), ("kvc", 4096, 4608), ("kvs", 4608, 5120), ("kvw", 5120, 5632),
             ("dt", 5632, 5648), ("gt", 5648, 5696))
    out = {}
    for nm, a, b in names:
        out["p_" + nm] = up[:, a:b]
        out["s_" + nm] = us[:, a:b]
    out["sw_keep"] = np.concatenate([r["sw_out"] for r in res.results], axis=0)
    out["sc_keep"] = np.concatenate([r["sc_out"] for r in res.results], axis=0)
    return out


T = 8192
NCK = 64


def build_B():
    nc = bass.Bass("TRN2", target_bir_lowering=False)
    I = lambda n, s: nc.dram_tensor(n, s, F32, kind="ExternalInput").ap()
    O = lambda n, s: nc.dram_tensor(n, s, F32, kind="ExternalOutput").ap()
    xbcT = I("xbcT", [3, 128, T + 3]); cw = I("cw", [3, 128, 4]); cb = I("cb", [128, 3])
    dtr = I("dtr", [128, NCK, 2]); z = I("z", [128, NCK, 128])
    dtb = I("dtb", [128, 2]); alog = I("alog", [128, 2]); dsk = I("dsk", [128, 2])
    consts = I("consts", [4, 128, 128])
    ident = I("ident", [128, 128])
    sx = I("sx", [128, 32, 4]); sB = I("sB", [4, 128, 32, 128]); sC = I("sC", [4, 128, 32, 128])
    scwx = I("scwx", [128, 4]); scbx = I("scbx", [128, 1])
    scwB = I("scwB", [128, 4, 128]); scbB = I("scbB", [128, 128]); scwC = I("scwC", [128, 4, 128]); scbC = I("scbC", [128, 128])
    sdtr = I("sdtr", [128, 32]); spar = I("spar", [128, 3])
    sz = I("sz", [128, 32]); sH = I("sH", [128, 32, 128])
    yg = O("yg", [T, 128]); hfin = O("hfin", [128, 128]); syg = O("syg", [128, 32]); sHo = O("sHo", [128, 32, 128])

    P = Prog(nc)
    cst = P.sb([128, 4, 128], F32, "cst")
    idf = P.sb([128, 128], F32, "idf"); idb = P.sb([128, 128], BF16, "idb")
    P.dma("sp", cst[:], consts.rearrange("a p f -> p a f"), writes=[cst])
    P.dma("sp", idf[:], ident[:, :], writes=[idf])
    P.op("dve", lambda e: e.tensor_copy(out=idb[:], in_=idf[:]), reads=[idf], writes=[idb])
    US, LL, TRI, ONES = 0, 1, 2, 3

    H = P.sb([128, 32, 128], F32, "sHt")
    tmp = P.sb([128, 32, 128], F32, "stmp")
    inb = P.sb([128, 32, 128], F32, "sinb")
    Bbc = P.sb([128, 32, 128], F32, "sBbc")
    wB = P.sb([128, 4, 128], F32, "swB"); bB = P.sb([128, 128], F32, "sbB")
    sxt = P.sb([128, 32, 4], F32, "sxt"); wx = P.sb([128, 4], F32, "swx"); bx = P.sb([128, 1], F32, "sbx")
    xs = P.sb([128, 32], F32, "sxs"); par = P.sb([128, 3], F32, "spar_t")
    dts = P.sb([128, 32], F32, "sdt"); dec = P.sb([128, 32], F32, "sdec"); dtx = P.sb([128, 32], F32, "sdtx")
    szt = P.sb([128, 32], F32, "szt"); yt = P.sb([128, 32], F32, "syt"); av = P.sb([128, 1], F32, "sav")
    P.dma("sp", H[:], sH[:, :, :], writes=[H])
    P.dma("sp", sxt[:], sx[:, :, :], writes=[sxt])
    P.dma("sp", wx[:], scwx[:, :], writes=[wx]); P.dma("sp", bx[:], scbx[:, :], writes=[bx])
    P.dma("sp", par[:], spar[:, :], writes=[par]); P.dma("sp", dts[:], sdtr[:, :], writes=[dts]); P.dma("sp", szt[:], sz[:, :], writes=[szt])
    P.op("dve", lambda e: e.tensor_scalar(out=xs[:], in0=sxt[:, :, 0], scalar1=wx[:, 0:1], scalar2=None, op0=ALU.mult), reads=[sxt, wx], writes=[xs])
    for k in range(1, 4):
        P.op("dve", lambda e, k=k: e.scalar_tensor_tensor(out=xs[:], in0=sxt[:, :, k], scalar=wx[:, k:k + 1], in1=xs[:], op0=ALU.mult, op1=ALU.add),
             reads=[sxt, wx, xs], writes=[xs])
    P.op("act", lambda e: e.activation(out=xs[:], in_=xs[:], func=AF.Silu, bias=bx[:, 0:1]), reads=[xs, bx], writes=[xs])
    P.op("act", lambda e: e.activation(out=dts[:], in_=dts[:], func=AF.Exp, bias=par[:, 0:1]), reads=[dts, par], writes=[dts])
    P.op("act", lambda e: e.activation(out=dts[:], in_=dts[:], func=AF.Ln, bias=1.0), reads=[dts], writes=[dts])
    P.op("act", lambda e: e.activation(out=av[:], in_=par[:, 1:2], func=AF.Exp), reads=[par], writes=[av])
    P.op("dve", lambda e: e.tensor_scalar(out=av[:], in0=av[:], scalar1=-1.0, scalar2=None, op0=ALU.mult), reads=[av], writes=[av])
    P.op("act", lambda e: e.activation(out=dec[:], in_=dts[:], func=AF.Exp, scale=av[:, 0:1]), reads=[dts, av], writes=[dec])
    P.op("dve", lambda e: e.tensor_tensor(out=dtx[:], in0=dts[:], in1=xs[:], op=ALU.mult), reads=[dts, xs], writes=[dtx])

    def conv_rep(src, wsrc, bsrc, dst):
        P.dma("sp", wB[:], wsrc[:, :, :], writes=[wB]); P.dma("sp", bB[:], bsrc[:, :], writes=[bB])
        for k in range(4):
            P.dma("sp", inb[:], src[k], writes=[inb])
            wk = bcast_ap(wB[:, k, :], [[4 * 128, 128], [0, 32], [1, 128]])
            if k == 0:
                P.op("dve", lambda e, wk=wk: e.tensor_tensor(out=dst[:], in0=inb[:], in1=wk, op=ALU.mult), reads=[inb, wB], writes=[dst])
            else:
                P.op("dve", lambda e, wk=wk: e.tensor_tensor(out=tmp[:], in0=inb[:], in1=wk, op=ALU.mult), reads=[inb, wB], writes=[tmp])
                P.op("pool", lambda e: e.tensor_tensor(out=dst[:], in0=dst[:], in1=tmp[:], op=ALU.add), reads=[dst, tmp], writes=[dst])
        bb = bcast_ap(bB[:, :], [[128, 128], [0, 32], [1, 128]])
        P.op("dve", lambda e: e.tensor_tensor(out=dst[:], in0=dst[:], in1=bb, op=ALU.add), reads=[dst, bB], writes=[dst])
        P.op("act", lambda e: e.activation(out=dst[:], in_=dst[:], func=AF.Silu), reads=[dst], writes=[dst])

    conv_rep(sB, scwB, scbB, Bbc)
    decb = bcast_ap(dec[:, :], [[32, 128], [1, 32], [0, 128]])
    dtxb = bcast_ap(dtx[:, :], [[32, 128], [1, 32], [0, 128]])
    P.op("dve", lambda e: e.tensor_tensor(out=H[:], in0=H[:], in1=decb, op=ALU.mult), reads=[H, dec], writes=[H])
    P.op("dve", lambda e: e.tensor_tensor(out=Bbc[:], in0=Bbc[:], in1=dtxb, op=ALU.mult), reads=[Bbc, dtx], writes=[Bbc])
    P.op("dve", lambda e: e.tensor_tensor(out=H[:], in0=H[:], in1=Bbc[:], op=ALU.add), reads=[H, Bbc], writes=[H])
    P.dma("sp", sHo[:, :, :], H[:], reads=[H])
    conv_rep(sC, scwC, scbC, Bbc)
    P.op("dve", lambda e: e.tensor_tensor(out=Bbc[:], in0=Bbc[:], in1=H[:], op=ALU.mult), reads=[Bbc, H], writes=[Bbc])
    P.op("dve", lambda e: e.tensor_reduce(out=yt[:], in_=Bbc[:], axis=AX.X, op=ALU.add), reads=[Bbc], writes=[yt])
    P.op("dve", lambda e: e.scalar_tensor_tensor(out=yt[:], in0=xs[:], scalar=par[:, 2:3], in1=yt[:], op0=ALU.mult, op1=ALU.add),
         reads=[xs, par, yt], writes=[yt])
    P.op("act", lambda e: e.activation(out=szt[:], in_=szt[:], func=AF.Silu), reads=[szt], writes=[szt])
    P.op("dve", lambda e: e.tensor_tensor(out=yt[:], in0=yt[:], in1=szt[:], op=ALU.mult), reads=[yt, szt], writes=[yt])
    P.dma("sp", syg[:, :], yt[:], reads=[yt])

    act3 = [P.sb([128, T], BF16, f"act3_{i}") for i in range(3)]
    cin = P.sb([128, 2051], F32, "cin"); cacc = P.sb([128, 2048], F32, "cacc")
    cwt = P.sb([128, 3, 4], F32, "cwt"); cbt = P.sb([128, 3], F32, "cbt")
    P.dma("sp", cwt[:], cw.rearrange("a p k -> p a k"), writes=[cwt])
    P.dma("sp", cbt[:], cb[:, :], writes=[cbt])
    for a in range(3):
        for blk in range(4):
            c0 = blk * 2048
            P.dma("sp", cin[:], xbcT[a, :, c0:c0 + 2051], writes=[cin])
            P.op("dve", lambda e, a=a: e.tensor_scalar(out=cacc[:], in0=cin[:, 0:2048], scalar1=cwt[:, a, 0:1], scalar2=None, op0=ALU.mult),
                 reads=[cin, cwt], writes=[cacc])
            for k in range(1, 4):
                P.op("dve", lambda e, a=a, k=k: e.scalar_tensor_tensor(out=cacc[:], in0=cin[:, k:k + 2048], scalar=cwt[:, a, k:k + 1], in1=cacc[:],
                                                                    op0=ALU.mult, op1=ALU.add), reads=[cin, cwt, cacc], writes=[cacc])
            P.op("act", lambda e, a=a, c0=c0: e.activation(out=act3[a][:, c0:c0 + 2048], in_=cacc[:], func=AF.Silu, bias=cbt[:, a:a + 1]),
                 reads=[cacc, cbt], writes=[act3[a]])
    xT, BT, CT = act3
    dt = P.sb([128, NCK, 2], F32, "dt"); dA = P.sb([128, NCK, 2], F32, "dA"); ww = P.sb([128, NCK, 2], F32, "ww")
    ee = P.sb([128, NCK, 2], F32, "ee"); cd = P.sb([128, NCK, 2], F32, "cd")
    p3 = P.sb([128, 3, 2], F32, "p3"); a2 = P.sb([128, 2], F32, "a2")
    P.dma("sp", dt[:], dtr[:, :, :], writes=[dt])
    P.dma("sp", p3[:, 0, :], dtb[:, :], writes=[p3]); P.dma("sp", p3[:, 1, :], alog[:, :], writes=[p3]); P.dma("sp", p3[:, 2, :], dsk[:, :], writes=[p3])
    P.op("dve", lambda e: e.tensor_tensor(out=dt[:], in0=dt[:], in1=bcast_ap(p3[:, 0, :], [[6, 128], [0, NCK], [1, 2]]), op=ALU.add), reads=[dt, p3], writes=[dt])
    P.op("act", lambda e: e.activation(out=dt[:], in_=dt[:], func=AF.Exp), reads=[dt], writes=[dt])
    P.op("act", lambda e: e.activation(out=dt[:], in_=dt[:], func=AF.Ln, bias=1.0), reads=[dt], writes=[dt])
    P.op("act", lambda e: e.activation(out=a2[:], in_=p3[:, 1, :], func=AF.Exp), reads=[p3], writes=[a2])
    P.op("dve", lambda e: e.tensor_scalar(out=a2[:], in0=a2[:], scalar1=-1.0, scalar2=None, op0=ALU.mult), reads=[a2], writes=[a2])
    P.op("dve", lambda e: e.tensor_tensor(out=dA[:], in0=dt[:], in1=bcast_ap(a2[:, :], [[2, 128], [0, NCK], [1, 2]]), op=ALU.mult), reads=[dt, a2], writes=[dA])
    ptx_ = P.ps([128, 1024], BF16, "ptx")
    pbank = [P.ps([128, 512], F32, f"pb{i}") for i in range(6)]
    class _V:
        def __init__(s, b, n): s.b = b; s.n = n
    def view(b, n):
        v = Buf(b.t, b.name); v.__dict__ = b.__dict__; return b
    psA, psB, psC = pbank[0], pbank[1], pbank[2]
    dA2 = dA[:, :, :].rearrange("p a b -> p (a b)")
    P.op("pe", lambda e: e.matmul(psA[:, 0:128], cst[:, LL, :], dA2, start=True, stop=True), reads=[cst, dA], writes=[psA])
    P.op("pe", lambda e: e.matmul(psB[:, 0:128], cst[:, US, :], dA2, start=True, stop=True), reads=[cst, dA], writes=[psB])
    P.op("pe", lambda e: e.matmul(psC[:, 0:128], cst[:, ONES, :], dA2, start=True, stop=True), reads=[cst, dA], writes=[psC])
    f2 = lambda t: t[:, :, :].rearrange("p a b -> p (a b)")
    P.op("act", lambda e: e.activation(out=f2(ee), in_=psA[:, 0:128], func=AF.Exp), reads=[psA], writes=[ee])
    P.op("act", lambda e: e.activation(out=f2(ww), in_=psB[:, 0:128], func=AF.Exp), reads=[psB], writes=[ww])
    P.op("act", lambda e: e.activation(out=f2(cd), in_=psC[:, 0:128], func=AF.Exp), reads=[psC], writes=[cd])
    P.op("dve", lambda e: e.tensor_tensor(out=ww[:], in0=ww[:], in1=dt[:], op=ALU.mult), reads=[ww, dt], writes=[ww])

    Hs = P.sb([128, 128], F32, "Hs"); Hb = P.sb([128, 128], BF16, "Hb")
    P.op("dve", lambda e: e.memset(Hs[:], 0.0), writes=[Hs])
    P.op("dve", lambda e: e.memset(Hb[:], 0.0), writes=[Hb])
    ptx = ptx_
    pG = pbank[0]; pseg = [pbank[1], pbank[2]]; pyd = pbank[3]; pyo = pbank[4]; pS = pbank[5]
    xtok = [P.sb([128, 256], BF16, f"xtok{i}") for i in range(2)]
    Gm = [P.sb([128, 128], F32, f"Gm{i}") for i in range(2)]
    UdA = [P.sb([128, 128], F32, f"UdA{i}") for i in range(2)]
    Eb = [P.sb([128, 128], F32, f"Eb{i}") for i in range(2)]
    MT = [P.sb([128, 128], BF16, f"MT{i}") for i in range(2)]
    xw = [P.sb([128, 128], BF16, f"xw{i}") for i in range(2)]
    yds = [P.sb([128, 128], F32, f"yds{i}") for i in range(2)]
    yo = [P.sb([128, 128], F32, f"yo{i}") for i in range(2)]
    zt = [P.sb([128, 128], F32, f"zt{i}") for i in range(2)]
    for ck in range(NCK):
        s = ck % 2
        cs = slice(ck * 128, (ck + 1) * 128)
        xt_ = xtok[s]
        P.dma("sp", zt[s][:], z[:, ck, :], writes=[zt[s]])
        P.op("act", lambda e, s=s: e.activation(out=zt[s][:], in_=zt[s][:], func=AF.Silu), reads=[zt[s]], writes=[zt[s]])
        P.op("pe", lambda e, cs=cs: e.transpose(out=ptx[:, 0:128], in_=xT[:, cs], identity=idb[:]), reads=[xT, idb], writes=[ptx])
        P.op("pe", lambda e, cs=cs: e.transpose(out=ptx[:, 128:256], in_=BT[:, cs], identity=idb[:]), reads=[BT, idb], writes=[ptx])
        P.op("act", lambda e, xt_=xt_: e.copy(out=xt_[:], in_=ptx[:, 0:256]), reads=[ptx], writes=[xt_])
        P.op("pe", lambda e, cs=cs: e.matmul(pG[:, 0:128], BT[:, cs], CT[:, cs], start=True, stop=True), reads=[BT, CT], writes=[pG])
        P.op("dve", lambda e, s=s: e.tensor_tensor(out=Gm[s][:], in0=pG[:, 0:128], in1=cst[:, TRI, :], op=ALU.mult), reads=[pG, cst], writes=[Gm[s]])
        for hh in range(2):
            hs = slice(hh * 64, hh * 64 + 64)
            P.op("dve", lambda e, hh=hh, ck=ck: e.tensor_scalar(out=UdA[hh][:], in0=cst[:, US, :], scalar1=dA[:, ck, hh:hh + 1], scalar2=None, op0=ALU.mult),
                 reads=[cst, dA], writes=[UdA[hh]])
            P.op("pe", lambda e, hh=hh: e.matmul(pseg[hh][:, 0:128], UdA[hh][:], cst[:, LL, :], start=True, stop=True), reads=[UdA[hh], cst], writes=[pseg[hh]])
            P.op("act", lambda e, hh=hh: e.activation(out=Eb[hh][:], in_=pseg[hh][:, 0:128], func=AF.Exp), reads=[pseg[hh]], writes=[Eb[hh]])
            P.op("dve", lambda e, hh=hh, ck=ck, s=s: e.scalar_tensor_tensor(out=MT[hh][:], in0=Eb[hh][:], scalar=dt[:, ck, hh:hh + 1], in1=Gm[s][:],
                                                                        op0=ALU.mult, op1=ALU.mult), reads=[Eb[hh], dt, Gm[s]], writes=[MT[hh]])
            P.op("pe", lambda e, hh=hh, hs=hs, xt_=xt_: e.matmul(pyd[:, hs], MT[hh][:], xt_[:, hs], start=True, stop=True), reads=[MT[hh], xt_], writes=[pyd])
            P.op("pe", lambda e, hs=hs, cs=cs: e.matmul(pyo[:, hs], CT[:, cs], Hb[:, hs], start=True, stop=True), reads=[CT, Hb], writes=[pyo])
            P.op("dve", lambda e, hh=hh, hs=hs, ck=ck, s=s, xt_=xt_: e.tensor_scalar(out=xw[s][:, hs], in0=xt_[:, hs], scalar1=ww[:, ck, hh:hh + 1], scalar2=None, op0=ALU.mult),
                 reads=[xt_, ww], writes=[xw[s]])
        P.op("pe", lambda e, s=s, xt_=xt_: e.matmul(pS[:, 0:128], xt_[:, 128:256], xw[s][:], start=True, stop=True), reads=[xt_, xw[s]], writes=[pS])
        P.op("act", lambda e, s=s: e.copy(out=yds[s][:], in_=pyd[:, 0:128]), reads=[pyd], writes=[yds[s]])
        for hh in range(2):
            hs = slice(hh * 64, hh * 64 + 64)
            P.op("dve", lambda e, hh=hh, hs=hs, ck=ck, s=s: e.scalar_tensor_tensor(out=yo[s][:, hs], in0=pyo[:, hs], scalar=ee[:, ck, hh:hh + 1], in1=yds[s][:, hs],
                                                                               op0=ALU.mult, op1=ALU.add), reads=[pyo, ee, yds[s]], writes=[yo[s]])
            P.op("dve", lambda e, hh=hh, hs=hs, s=s, xt_=xt_: e.scalar_tensor_tensor(out=yo[s][:, hs], in0=xt_[:, hs], scalar=p3[:, 2, hh:hh + 1], in1=yo[s][:, hs],
                                                                                 op0=ALU.mult, op1=ALU.add), reads=[xt_, p3, yo[s]], writes=[yo[s]])
            P.op("dve", lambda e, hh=hh, hs=hs, ck=ck: e.scalar_tensor_tensor(out=Hs[:, hs], in0=Hs[:, hs], scalar=cd[:, ck, hh:hh + 1], in1=pS[:, hs],
                                                                          op0=ALU.mult, op1=ALU.add), reads=[Hs, cd, pS, pyo], writes=[Hs])
        P.op("act", lambda e: e.copy(out=Hb[:], in_=Hs[:]), reads=[Hs, pyo], writes=[Hb])
        P.op("dve", lambda e, s=s: e.tensor_tensor(out=yo[s][:], in0=yo[s][:], in1=zt[s][:], op=ALU.mult), reads=[yo[s], zt[s]], writes=[yo[s]])
        P.dma("sp", yg[cs, :], yo[s][:], reads=[yo[s]])
    P.dma("sp", hfin[:, :], Hs[:], reads=[Hs])
    P.emit()
    P.close()
    return nc


def run_B(z_p, xbc_p, dt_p, z_s, xbc_s, dt_s, st_conv, st_ssm, conv_w, conv_b, dt_bias, a_log, d_skip):
    nc = build_B()
    f = lambda a: np.ascontiguousarray(a, dtype=np.float32)
    t = np.arange(128)
    consts = np.stack([(t[:, None] > t[None, :]), (t[:, None] <= t[None, :]), (t[:, None] <= t[None, :]), np.ones((128, 128), bool)]).astype(np.float32)
    ident = np.eye(128, dtype=np.float32)
    maps = []
    for c in range(8):
        gi = c // 2
        cols = [np.arange(128 * c, 128 * c + 128), np.arange(1024 + 128 * gi, 1024 + 128 * gi + 128), np.arange(1536 + 128 * gi, 1536 + 128 * gi + 128)]
        xbcT = np.zeros((3, 128, T + 3), np.float32)
        for a in range(3):
            xbcT[a, :, 3:] = xbc_p[:, cols[a]].T
        cw = np.stack([conv_w[:, cols[a]].T for a in range(3)])
        cb = np.stack([conv_b[cols[a]] for a in range(3)], axis=1)
        hsl = slice(2 * c, 2 * c + 2)
        m = {"xbcT": xbcT, "cw": f(cw), "cb": f(cb),
             "dtr": f(dt_p[:, hsl].reshape(64, 128, 2).transpose(1, 0, 2)),
             "z": f(z_p[:, 128 * c:128 * c + 128].reshape(64, 128, 128).transpose(1, 0, 2)),
             "dtb": f(np.broadcast_to(dt_bias[hsl], (128, 2))), "alog": f(np.broadcast_to(a_log[hsl], (128, 2))),
             "dsk": f(np.broadcast_to(d_skip[hsl], (128, 2))), "consts": consts, "ident": ident}
        full = np.concatenate([st_conv, xbc_s[:, None, :]], axis=1)
        m["sx"] = f(full[:, :, cols[0]].transpose(2, 0, 1))
        for nm, a in (("B", 1), ("C", 2)):
            m["s" + nm] = f(np.broadcast_to(full[:, :, cols[a]].transpose(1, 0, 2)[:, None], (4, 128, 32, 128)))
            m["scw" + nm] = f(np.broadcast_to(conv_w[:, cols[a]][None], (128, 4, 128)))
            m["scb" + nm] = f(np.broadcast_to(conv_b[cols[a]][None], (128, 128)))
        m["scwx"] = f(conv_w[:, cols[0]].T); m["scbx"] = f(conv_b[cols[0]][:, None])
        hp = np.repeat(np.arange(2 * c, 2 * c + 2), 64)
        m["sdtr"] = f(dt_s[:, hp].T)
        m["spar"] = f(np.stack([dt_bias[hp], a_log[hp], d_skip[hp]], axis=1))
        m["sz"] = f(z_s[:, 128 * c:128 * c + 128].T)
        m["sH"] = f(st_ssm[:, hsl].reshape(32, 128, 128).transpose(1, 0, 2))
        maps.append(m)
    res = run_bass_kernel_spmd(nc, maps, core_ids=list(range(8)))
    yg_p = np.concatenate([r["yg"] for r in res.results], axis=1)
    ssm_p = np.concatenate([r["hfin"].T.reshape(2, 64, 128) for r in res.results], axis=0)
    yg_s = np.concatenate([r["syg"].T for r in res.results], axis=1)
    ssm_s = np.concatenate([r["sHo"].transpose(1, 0, 2).reshape(32, 2, 64, 128) for r in res.results], axis=1)
    return yg_p, ssm_p, yg_s, ssm_s


NPOOL = 2560


def build_G(ntab):
    nc = bass.Bass("TRN2", target_bir_lowering=False)
    pools = [nc.dram_tensor(f"pool{t}", [NPOOL * 2, 8192], F32, kind="ExternalInput").ap() for t in range(ntab)]
    ptT = nc.dram_tensor("ptT", [64, 16], I32, kind="ExternalInput").ap()
    outs = [nc.dram_tensor(f"g{t}", [16, 8192, 128], F32, kind="ExternalOutput").ap() for t in range(ntab)]
    P = Prog(nc)
    ptt = P.sb([64, 16], I32, "ptt")
    idx = P.sb([64, 2, 16], I32, "idx")
    bufs = [P.sb([64, 8192], F32, f"gb{i}") for i in range(4)]
    P.dma("sp", ptt[:], ptT[:, :], writes=[ptt])
    for hp in range(2):
        P.op("dve", lambda e, hp=hp: e.tensor_scalar(out=idx[:, hp, :], in0=ptt[:], scalar1=2.0, scalar2=float(hp), op0=ALU.mult, op1=ALU.add),
             reads=[ptt], writes=[idx])
    k = 0
    for t in range(ntab):
        for s in range(16):
            dst = outs[t][s].rearrange("(j hp tt) c -> j hp (tt c)", hp=2, tt=64)
            for hp in range(2):
                b = bufs[k % 4]; k += 1
                P.dma("pool", None, None, reads=[idx], writes=[b],
                      indirect=lambda e, b=b, hp=hp, s=s, t=t: e.indirect_dma_start(
                          out=b[:, :], out_offset=None, in_=pools[t][:, :],
                          in_offset=bass.IndirectOffsetOnAxis(ap=idx[:, hp, s:s + 1], axis=0)))
                P.dma("sp" if k % 2 else "act", dst[:, hp, :], b[:, :], reads=[b])
    P.emit()
    P.close()
    return nc


def run_G(pool_list, page_table):
    nt = len(pool_list)
    nc = build_G(nt)
    maps = []
    byhead = [[np.ascontiguousarray(p[:, :, :, h, :], dtype=np.float32).reshape(NPOOL * 2, 8192) for h in range(4)] for p in pool_list]
    for c in range(8):
        h = c % 4; half = c // 4
        m = {"ptT": np.ascontiguousarray(page_table[16 * half:16 * half + 16].T, dtype=np.int32)}
        for t in range(nt):
            m[f"pool{t}"] = byhead[t][h]
        maps.append(m)
    res = run_bass_kernel_spmd(nc, maps, core_ids=list(range(8)))
    outs = []
    for t in range(nt):
        g = np.zeros((32, 8192, 2, 4, 64), np.float32)
        for c, r in enumerate(res.results):
            h = c % 4; half = c // 4
            g[16 * half:16 * half + 16, :, :, h, :] = r[f"g{t}"].reshape(16, 8192, 2, 64)
        outs.append(g)
    return outs


TP = 8256
TPK = 8320
NJ = 32
NSEQ = 16
SCALE = 0.125
NEG = -1.0e30
BIG = 32768.0


def ntmax(j):
    return (16 * j + 14) // 128


CM_IDX = {}
for _j in range(NJ):
    for _nt in range(ntmax(_j) + 1):
        CM_IDX[(_j, _nt)] = len(CM_IDX)
NCM = len(CM_IDX)


def build_C(do_prompt=True, do_sample=True):
    nc = bass.Bass("TRN2", target_bir_lowering=False)
    I = lambda n, s: nc.dram_tensor(n, s, F32, kind="ExternalInput").ap()
    O = lambda n, s: nc.dram_tensor(n, s, F32, kind="ExternalOutput").ap()
    qT = I("qT", [NJ, 128, 512]); gts = I("gts", [128, NJ, 12])
    kcv = I("kcv", [128, TP]); ksw = I("ksw", [128, 8192]); vs = I("vs", [128, 64, 64]); vw = I("vw", [128, 64, 64])
    wkv = I("wkv", [128, 32, 64]); peT = I("peT", [128, 32]); gk0 = I("gk0", [64, 1])
    cover = I("cover", [128, 5, 129]); t3 = I("t3", [128, 64, 128]); ident = I("ident", [128, 128])
    LM = I("LM", [128, 2, 128]); WM = I("WM", [128, 6, 128]); CM = I("CM", [128, NCM, 128]); ADD = I("ADD", [128, NJ, 128])
    s_kcv = I("s_kcv", [NSEQ, 128, TP]); s_ks = I("s_ks", [NSEQ, 64, TPK]); s_vs = I("s_vs", [NSEQ, 128, 65, 64])
    s_kw = I("s_kw", [NSEQ, 64, 640]); s_vw = I("s_vw", [NSEQ, 128, 5, 64])
    s_q = I("s_q", [NSEQ, 128, 4]); s_gt = I("s_gt", [NSEQ, 4, 3])
    s_msk = I("s_msk", [128, 75])
    s_add = I("s_add", [1, 129])
    o_p = O("o_p", [128, NJ, 256]); o_s = O("o_s", [NSEQ, 4, 64])

    P = Prog(nc)
    stg = [P.sb([128, 2048], F32, f"stg{i}") for i in range(2)]
    scnt = [0]

    def load_cast(dst, dst_ap, src_ap, parts, fshape, p0=0):
        if isinstance(fshape, int):
            fshape = (fshape,)
        n = int(np.prod(fshape))
        st = stg[scnt[0] % 2]
        eng = ("dve", "pool")[scnt[0] % 2]
        scnt[0] += 1
        sview = st[p0:p0 + parts, 0:n]
        if len(fshape) == 2:
            sview = sview.rearrange("p (a b) -> p a b", b=fshape[1])
        P.dma("sp", sview, src_ap, writes=[st])
        P.op(eng, lambda e: e.tensor_copy(out=dst_ap, in_=sview), reads=[st], writes=[dst])

    idf = P.sb([128, 128], F32, "idf"); idb = P.sb([128, 128], BF16, "idb")
    P.dma("sp", idf[:], ident[:, :], writes=[idf])
    P.op("dve", lambda e: e.tensor_copy(out=idb[:], in_=idf[:]), reads=[idf], writes=[idb])
    ones64 = P.sb([64, 64], F32, "ones64"); P.op("dve", lambda e: e.memset(ones64[:], 1.0), writes=[ones64])
    one_bf = P.sb([1, 2], BF16, "one_bf"); P.op("dve", lambda e: e.memset(one_bf[:], 1.0), writes=[one_bf])
    wkvb = P.sb([128, 32, 64], BF16, "wkvb")
    load_cast(wkvb, wkvb[:, :, :], wkv[:, :, :], 128, (32, 64))
    pe = P.sb([128, 32], F32, "pe"); P.dma("sp", pe[:], peT[:, :], writes=[pe])
    gk = P.sb([64, 1], F32, "gk"); P.dma("sp", gk[:], gk0[:, :], writes=[gk])
    covb = P.sb([128, 5, 129], BF16, "covb")
    load_cast(covb, covb[:, :, :], cover[:, :, :], 128, (5, 129))
    t3b = P.sb([128, 64, 128], BF16, "t3b")
    for a in range(4):
        load_cast(t3b, t3b[:, a * 16:(a + 1) * 16, :], t3[:, a * 16:(a + 1) * 16, :], 128, (16, 128))

    KVlo = P.sb([128, TP], BF16, "KVlo"); KVhi = P.sb([128, TP], BF16, "KVhi")
    KSW = P.sb([128, TPK], BF16, "KSW")
    vs_a = P.sb([128, 65, 65], BF16, "vs_a"); vw_a = P.sb([128, 64, 65], BF16, "vw_a"); cv_a = P.sb([128, 5, 65], BF16, "cv_a")
    for t_ in (vs_a, vw_a, cv_a):
        P.op("pool", lambda e, t_=t_: e.memset(t_[:], 1.0), writes=[t_])
    ckf = P.sb([64, 516], F32, "ckf"); cks = P.sb([64, 516], F32, "cks"); ckT = P.sb([128, 640], BF16, "ckT")
    P.op("pool", lambda e: e.memset(ckT[:], 0.0), writes=[ckT])
    pS = [P.ps([128, 512], F32, f"pS{i}") for i in range(2)]
    pM = P.ps([128, 512], F32, "pM"); pOc = P.ps([128, 512], F32, "pOc"); pOs = P.ps([128, 512], F32, "pOs")
    pOw = P.ps([128, 512], F32, "pOw"); pU = P.ps([128, 512], F32, "pU"); pT = P.ps([128, 1024], BF16, "pT")
    pOT, pUT, pX = pOc, pOs, pOw

    def compress(src):
        for c0 in range(0, TP, 2048):
            w = min(2048, TP - c0)
            st = stg[scnt[0] % 2]; scnt[0] += 1
            P.dma("sp", st[:, :w], src[:, c0:c0 + w], writes=[st])
            for dst, po, eng in ((KVlo, 0, "dve"), (KVhi, 16, "pool")):
                P.op(eng, lambda e, dst=dst, po=po, st=st, c0=c0, w=w: e.tensor_tensor(
                    out=dst[:, c0:c0 + w].rearrange("p (n l) -> p n l", l=16), in0=st[:, :w].rearrange("p (n l) -> p n l", l=16),
                    in1=bcast_ap(pe[:, po:po + 1], [[32, 128], [0, w // 16], [1, 16]]), op=ALU.add), reads=[st, pe], writes=[dst])
        for (n0, nn, pb) in ((0, 512, pS[0]), (512, 3, pS[1])):
            for l in range(32):
                srcb = KVlo if l < 16 else KVhi
                P.op("pe", lambda e, srcb=srcb, l=l, n0=n0, nn=nn, pb=pb: e.matmul(
                    pb[0:64, 0:nn], wkvb[0:64, l, :], bcast_ap(srcb[0:64, 16 * n0 + l:16 * n0 + l + 1], [[TP, 64], [16, nn]]),
                    start=(l == 0), stop=(l == 31)), reads=[wkvb, srcb], writes=[pb])
        P.op("act", lambda e: e.copy(out=ckf[:, 0:512], in_=pS[0][0:64, 0:512]), reads=[pS[0]], writes=[ckf])
        P.op("act", lambda e: e.copy(out=ckf[:, 512:515], in_=pS[1][0:64, 0:3]), reads=[pS[1]], writes=[ckf])
        P.op("act", lambda e: e.activation(out=cks[:, 0:515], in_=ckf[:, 0:515], func=AF.Square), reads=[ckf], writes=[cks])
        P.op("pe", lambda e: e.matmul(pS[0][0:64, 0:512], ones64[:], cks[:, 0:512], start=True, stop=True), reads=[ones64, cks], writes=[pS[0]])
        P.op("pe", lambda e: e.matmul(pS[1][0:64, 0:3], ones64[:], cks[:, 512:515], start=True, stop=True), reads=[ones64, cks], writes=[pS[1]])
        P.op("dve", lambda e: e.tensor_scalar(out=cks[:, 0:512], in0=pS[0][0:64, 0:512], scalar1=1.0 / 64, scalar2=1e-6, op0=ALU.mult, op1=ALU.add),
             reads=[pS[0]], writes=[cks])
        P.op("dve", lambda e: e.tensor_scalar(out=cks[:, 512:515], in0=pS[1][0:64, 0:3], scalar1=1.0 / 64, scalar2=1e-6, op0=ALU.mult, op1=ALU.add),
             reads=[pS[1]], writes=[cks])
        P.op("act", lambda e: e.activation(out=cks[:, 0:515], in_=cks[:, 0:515], func=AF.Sqrt), reads=[cks], writes=[cks])
        P.op("dve", lambda e: e.reciprocal(out=cks[:, 0:515], in_=cks[:, 0:515]), reads=[cks], writes=[cks])
        P.op("dve", lambda e: e.tensor_tensor(out=ckf[:, 0:515], in0=ckf[:, 0:515], in1=cks[:, 0:515], op=ALU.mult), reads=[ckf, cks], writes=[ckf])
        P.op("dve", lambda e: e.tensor_scalar(out=ckT[0:64, 0:515], in0=ckf[:, 0:515], scalar1=gk[:, 0:1], scalar2=None, op0=ALU.mult),
             reads=[ckf, gk], writes=[ckT])
        for nt in range(5):
            nn = 128 if nt < 4 else 3
            for l in range(32):
                srcb = KVlo if l < 16 else KVhi
                col = 16 * 128 * nt + l
                P.op("pe", lambda e, srcb=srcb, l=l, nn=nn, col=col: e.matmul(
                    pU[0:nn, 0:64], bcast_ap(srcb[64:128, col:col + 1], [[TP, 64], [16, nn]]), wkvb[64:128, l, :],
                    start=(l == 0), stop=(l == 31)), reads=[wkvb, srcb], writes=[pU])
            P.op("act", lambda e, nt=nt, nn=nn: e.copy(out=cv_a[0:nn, nt, 0:64], in_=pU[0:nn, 0:64]), reads=[pU], writes=[cv_a])

    def bc_g(buf, ap1):
        return bcast_ap(ap1, [[ap1.ap[0][0], 128], [0, 4], [1, 128]])

    if do_prompt:
        LMb = P.sb([128, 2, 128], BF16, "LMb"); WMb = P.sb([128, 6, 128], BF16, "WMb"); CMb = P.sb([128, NCM, 128], BF16, "CMb")
        load_cast(LMb, LMb[:, :, :], LM[:, :, :], 128, (2, 128))
        load_cast(WMb, WMb[:, :, :], WM[:, :, :], 128, (6, 128))
        for a in range(0, NCM, 16):
            load_cast(CMb, CMb[:, a:a + 16, :], CM[:, a:a + 16, :], 128, (16, 128))
        addt = P.sb([128, NJ, 128], F32, "addt"); P.dma("sp", addt[:], ADD[:, :, :], writes=[addt])
        gtt = P.sb([128, NJ, 12], F32, "gtt"); P.dma("sp", gtt[:], gts[:, :, :], writes=[gtt])
        P.op("act", lambda e: e.activation(out=gtt[:], in_=gtt[:], func=AF.Sigmoid), reads=[gtt], writes=[gtt])
        compress(kcv)
        for c0 in range(0, 8192, 2048):
            load_cast(KSW, KSW[:, c0:c0 + 2048], ksw[:, c0:c0 + 2048], 128, 2048)
        for a in range(0, 64, 32):
            load_cast(vs_a, vs_a[:, a:a + 32, 0:64], vs[:, a:a + 32, :], 128, (32, 64))
            load_cast(vw_a, vw_a[:, a:a + 32, 0:64], vw[:, a:a + 32, :], 128, (32, 64))
        qsb = [P.sb([128, 512], BF16, f"qsb{i}") for i in range(2)]; qwb = [P.sb([128, 512], BF16, f"qwb{i}") for i in range(2)]
        for t_ in qsb + qwb:
            P.op("pool", lambda e, t_=t_: e.memset(t_[:], 0.0), writes=[t_])
        Pt = [P.sb([128, 512], BF16, f"Pt{i}") for i in range(3)]
        imp = P.sb([128, 128], F32, "imp"); sc2 = P.sb([128, 128], F32, "sc2"); m8 = P.sb([128, 16], F32, "m8")
        selb = P.sb([128, 128], BF16, "selb"); selT = P.sb([128, 512], BF16, "selT")
        i4f = P.sb([128, 512], F32, "i4f"); I4b = P.sb([128, 512], BF16, "I4b")
        for g in range(4):
            P.op("pool", lambda e, g=g: e.tensor_copy(out=I4b[:, g * 128:(g + 1) * 128], in_=idf[:, :]), reads=[idf], writes=[I4b])
        rs = P.sb([128, 12], F32, "rs"); cf = P.sb([128, 12], F32, "cf")
        ot = [P.sb([128, 256], F32, f"ot{i}") for i in range(2)]; otmp = P.sb([128, 256], F32, "otmp")
        pk = [0]

        def qk_exp(lhs_ap, q_ap, reads, extra=()):
            ps_ = pS[pk[0] % 2]; pt_ = Pt[pk[0] % 3]; pk[0] += 1
            import os
            if os.environ.get('NOEXTRA'): extra = ()
            n = len(extra)
            P.op("pe", lambda e: e.matmul(ps_[:, :], lhs_ap, q_ap, start=True, stop=(n == 0)), reads=reads, writes=[ps_])
            for i, (l_ap, r_ap, rd) in enumerate(extra):
                P.op("pe", lambda e, l_ap=l_ap, r_ap=r_ap, i=i: e.matmul(ps_[:, :], l_ap, r_ap, start=False, stop=(i == n - 1)), reads=rd, writes=[ps_])
            P.op("act", lambda e: e.activation(out=pt_[:], in_=ps_[:, :], func=AF.Exp, scale=SCALE), reads=[ps_], writes=[pt_])
            return pt_

        OTs = P.sb([65, 512], F32, "OTs"); UTs = P.sb([128, 512], F32, "UTs")
        Otok = [P.sb([128, 260], F32, f"Otok{i}") for i in range(3)]

        def pv(pt_, vbuf, kt, first, last):
            P.op("pe", lambda e: e.matmul(pOT[0:65, :], vbuf[:, kt, :], pt_[:, :], start=first, stop=last), reads=[pt_, vbuf], writes=[pOT])

        def finish_branch(br):
            P.op("act", lambda e: e.copy(out=OTs[:, :], in_=pOT[0:65, :]), reads=[pOT], writes=[OTs])
            for g in range(4):
                P.op("pe", lambda e, g=g: e.transpose(out=pX[:, g * 65:(g + 1) * 65], in_=OTs[0:65, g * 128:(g + 1) * 128], identity=idf[0:65, 0:65]),
                     reads=[OTs, idf], writes=[pX])
            P.op("act", lambda e: e.copy(out=Otok[br][:, :], in_=pX[:, 0:260]), reads=[pX], writes=[Otok[br]])

        def run_branch(tiles, vbuf, with_u=False):
            n = len(tiles)
            pts = [None] * n
            pts[0] = qk_exp(*tiles[0][:4])
            for i in range(n):
                if i + 1 < n:
                    pts[i + 1] = qk_exp(*tiles[i + 1][:4])
                pv(pts[i], vbuf, tiles[i][4], i == 0, i == n - 1)
                if with_u:
                    P.op("pe", lambda e, i=i, pt_=pts[i], nt=tiles[i][4]: e.matmul(pUT[:, :], covb[:, nt, 0:128], pt_[:, :], start=(i == 0), stop=(i == n - 1)),
                         reads=[pts[i], covb], writes=[pUT])

        for j in range(NJ):
            qs_ = qsb[j % 2]; qw_ = qwb[j % 2]
            st = stg[scnt[0] % 2]; scnt[0] += 1
            P.dma("sp", st[:, 0:512], qT[j], writes=[st])
            P.op("dve", lambda e, st=st, qs_=qs_: e.tensor_copy(out=qs_[0:64, :], in_=st[0:64, 0:512]), reads=[st], writes=[qs_])
            P.op("pool", lambda e, st=st, qw_=qw_: e.tensor_copy(out=qw_[64:128, :], in_=st[64:128, 0:512]), reads=[st], writes=[qw_])
            nts = list(range(ntmax(j) + 1))
            run_branch([(ckT[:, nt * 128:(nt + 1) * 128], qs_[:, :], [ckT, qs_], [(CMb[:, CM_IDX[(j, nt)], :], I4b[:, :], [CMb, I4b])], nt) for nt in nts],
                       cv_a, with_u=True)
            finish_branch(0)
            P.op("act", lambda e: e.copy(out=UTs[:, :], in_=pUT[:, :]), reads=[pUT], writes=[UTs])
            for g in range(4):
                P.op("pe", lambda e, g=g: e.transpose(out=pX[:, g * 128:(g + 1) * 128], in_=UTs[:, g * 128:(g + 1) * 128], identity=idf[:, :]),
                     reads=[UTs, idf], writes=[pX])
            P.op("dve", lambda e: e.tensor_scalar(out=rs[:, 0:4], in0=bcast_ap(Otok[0][:, 64:65], [[260, 128], [65, 4]]), scalar1=1e-30, scalar2=None, op0=ALU.max),
                 reads=[Otok[0]], writes=[rs])
            P.op("dve", lambda e: e.reciprocal(out=rs[:, 0:4], in_=rs[:, 0:4]), reads=[rs], writes=[rs])
            P.op("dve", lambda e: e.tensor_scalar(out=imp[:], in0=pX[:, 0:128], scalar1=rs[:, 0:1], scalar2=None, op0=ALU.mult), reads=[pX, rs], writes=[imp])
            for g in range(1, 4):
                P.op("dve", lambda e, g=g: e.scalar_tensor_tensor(out=imp[:], in0=pX[:, g * 128:(g + 1) * 128], scalar=rs[:, g:g + 1], in1=imp[:],
                                                                op0=ALU.mult, op1=ALU.add), reads=[pX, rs, imp], writes=[imp])
            P.op("dve", lambda e, j=j: e.tensor_tensor(out=imp[:], in0=imp[:], in1=addt[:, j, :], op=ALU.add), reads=[imp, addt], writes=[imp])
            P.op("dve", lambda e: e.max(out=m8[:, 0:8], in_=imp[:]), reads=[imp], writes=[m8])
            P.op("dve", lambda e: e.match_replace(out=sc2[:], in_to_replace=m8[:, 0:8], in_values=imp[:], imm_value=NEG), reads=[imp, m8], writes=[sc2])
            P.op("dve", lambda e: e.max(out=m8[:, 8:16], in_=sc2[:]), reads=[sc2], writes=[m8])
            P.op("dve", lambda e: e.tensor_scalar(out=selb[:], in0=imp[:], scalar1=m8[:, 15:16], scalar2=None, op0=ALU.is_ge), reads=[imp, m8], writes=[selb])
            P.op("dve", lambda e: e.tensor_scalar(out=selb[:], in0=selb[:], scalar1=1.0, scalar2=BIG, op0=ALU.subtract, op1=ALU.mult), reads=[selb], writes=[selb])
            P.op("pe", lambda e: e.transpose(out=pT[:, 0:128], in_=selb[:], identity=idb[:]), reads=[selb, idb], writes=[pT])
            P.op("act", lambda e: e.copy(out=selT[:, :].rearrange("p (g q) -> p g q", g=4),
                                         in_=bcast_ap(pT[:, 0:1], [[pT[:, :].ap[0][0], 128], [0, 4], [1, 128]])), reads=[pT], writes=[selT])
            kts = list(range(2 * j + 2))
            tl = []
            for kt in kts:
                ex = [(t3b[:, kt, :], selT[:, :], [t3b, selT])]
                if kt >= 2 * j:
                    ex.append((LMb[:, kt - 2 * j, :], I4b[:, :], [LMb, I4b]))
                tl.append((KSW[:, kt * 128:(kt + 1) * 128], qs_[:, :], [KSW, qs_], ex, kt))
            run_branch(tl, vs_a)
            finish_branch(1)
            wk = [(i, 2 * j - 4 + i) for i in range(6) if 2 * j - 4 + i >= 0]
            run_branch([(KSW[:, kt * 128:(kt + 1) * 128], qw_[:, :], [KSW, qw_], [(WMb[:, i, :], I4b[:, :], [WMb, I4b])], kt) for i, kt in wk], vw_a)
            finish_branch(2)
            o_ = ot[j % 2]
            for br, po in enumerate(Otok):
                P.op("dve", lambda e, br=br, po=po: e.tensor_scalar(out=rs[:, br * 4:br * 4 + 4], in0=bcast_ap(po[:, 64:65], [[260, 128], [65, 4]]),
                                                                     scalar1=1e-30, scalar2=None, op0=ALU.max), reads=[po], writes=[rs])
                P.op("dve", lambda e, br=br: e.reciprocal(out=rs[:, br * 4:br * 4 + 4], in_=rs[:, br * 4:br * 4 + 4]), reads=[rs], writes=[rs])
                P.op("dve", lambda e, br=br, j=j: e.tensor_tensor(out=cf[:, br * 4:br * 4 + 4], in0=rs[:, br * 4:br * 4 + 4],
                                                                in1=bcast_ap(gtt[:, j, br:br + 1], [[NJ * 12, 128], [3, 4]]), op=ALU.mult), reads=[rs, gtt], writes=[cf])
                dst = o_ if br == 0 else otmp
                P.op("dve", lambda e, br=br, po=po, dst=dst: e.tensor_tensor(
                    out=dst[:, :].rearrange("p (g d) -> p g d", g=4), in0=bcast_ap(po[:, 0:1], [[260, 128], [65, 4], [1, 64]]),
                    in1=bcast_ap(cf[:, br * 4:br * 4 + 1], [[12, 128], [1, 4], [0, 64]]), op=ALU.mult), reads=[po, cf], writes=[dst])
                if br > 0:
                    P.op("pool", lambda e, o_=o_: e.tensor_tensor(out=o_[:], in0=o_[:], in1=otmp[:], op=ALU.add), reads=[o_, otmp], writes=[o_])
            P.dma("sp", o_p[:, j, :], o_[:], reads=[o_])

    if do_sample:
        smk = P.sb([128, 75], F32, "smk"); P.dma("sp", smk[:], s_msk[:, :], writes=[smk])
        sad = P.sb([1, 129], F32, "sad"); P.dma("sp", sad[:], s_add[:, :], writes=[sad])
        sqb = P.sb([128, 4], BF16, "sqb"); sqf = P.sb([128, 4], F32, "sqf")
        Pc = P.sb([128, 20], BF16, "Pc"); Pss = P.sb([128, 260], BF16, "Pss"); Pw = P.sb([128, 20], BF16, "Pw")
        Usb = P.sb([4, 132], F32, "Usb"); srs = P.sb([4, 4], F32, "srs"); simp = P.sb([1, 132], F32, "simp"); ssc2 = P.sb([1, 132], F32, "ssc2")
        sm8 = P.sb([1, 16], F32, "sm8"); ssel = P.sb([1, 132], BF16, "ssel"); selx = P.sb([1, TPK], BF16, "selx")
        P.op("dve", lambda e: e.memset(selx[:], 0.0), writes=[selx])
        mcol = P.sb([128, 65], F32, "mcol"); sgt = P.sb([4, 3], F32, "sgt"); scf = P.sb([4, 4], F32, "scf")
        so = P.sb([4, 64], F32, "so"); so2 = P.sb([4, 64], F32, "so2")
        for b in range(NSEQ):
            compress(s_kcv[b])
            P.dma("sp", sqf[:], s_q[b], writes=[sqf])
            P.op("dve", lambda e: e.tensor_copy(out=sqb[:], in_=sqf[:]), reads=[sqf], writes=[sqb])
            P.dma("sp", sgt[:], s_gt[b], writes=[sgt])
            P.op("act", lambda e: e.activation(out=sgt[:], in_=sgt[:], func=AF.Sigmoid), reads=[sgt], writes=[sgt])
            for nt in range(5):
                P.op("pe", lambda e, nt=nt: e.matmul(pS[0][:, nt * 4:nt * 4 + 4], ckT[0:64, nt * 128:(nt + 1) * 128], sqb[0:64, :], start=True, stop=True),
                     reads=[ckT, sqb], writes=[pS[0]])
            P.op("act", lambda e: e.activation(out=Pc[:], in_=pS[0][:, 0:20], func=AF.Exp, scale=SCALE), reads=[pS[0]], writes=[Pc])
            P.op("dve", lambda e: e.tensor_tensor(out=Pc[:, :].rearrange("p (t g) -> p t g", g=4), in0=Pc[:, :].rearrange("p (t g) -> p t g", g=4),
                                                  in1=bcast_ap(smk[:, 0:1], [[75, 128], [1, 5], [0, 4]]), op=ALU.mult), reads=[Pc, smk], writes=[Pc])
            for nt in range(5):
                P.op("pe", lambda e, nt=nt: e.matmul(pOc[0:4, 0:65], Pc[:, nt * 4:nt * 4 + 4], cv_a[:, nt, :], start=(nt == 0), stop=(nt == 4)),
                     reads=[Pc, cv_a], writes=[pOc])
                P.op("pe", lambda e, nt=nt: e.matmul(pU[0:4, 0:129], Pc[:, nt * 4:nt * 4 + 4], covb[:, nt, :], start=(nt == 0), stop=(nt == 4)),
                     reads=[Pc, covb], writes=[pU])
            P.op("dve", lambda e: e.reciprocal(out=srs[:, 0:1], in_=pOc[0:4, 64:65]), reads=[pOc], writes=[srs])
            P.op("act", lambda e: e.copy(out=Usb[:, 0:129], in_=pU[0:4, 0:129]), reads=[pU], writes=[Usb])
            P.op("pe", lambda e: e.matmul(pM[0:1, 0:129], srs[:, 0:1], Usb[:, 0:129], start=True, stop=True), reads=[srs, Usb], writes=[pM])
            P.op("dve", lambda e: e.tensor_tensor(out=simp[:, 0:129], in0=pM[0:1, 0:129], in1=sad[:, :], op=ALU.add), reads=[pM, sad], writes=[simp])
            P.op("dve", lambda e: e.max(out=sm8[:, 0:8], in_=simp[:, 0:129]), reads=[simp], writes=[sm8])
            P.op("dve", lambda e: e.match_replace(out=ssc2[:, 0:129], in_to_replace=sm8[:, 0:8], in_values=simp[:, 0:129], imm_value=NEG),
                 reads=[simp, sm8], writes=[ssc2])
            P.op("dve", lambda e: e.max(out=sm8[:, 8:16], in_=ssc2[:, 0:129]), reads=[ssc2], writes=[sm8])
            P.op("dve", lambda e: e.tensor_scalar(out=ssel[:, 0:129], in0=simp[:, 0:129], scalar1=sm8[:, 15:16], scalar2=None, op0=ALU.is_ge),
                 reads=[simp, sm8], writes=[ssel])
            P.op("dve", lambda e: e.tensor_copy(out=selx[:, 0:TP].rearrange("p (s k) -> p s k", k=64), in_=bcast_ap(ssel[:, 0:1], [[132, 1], [1, 129], [0, 64]])),
                 reads=[ssel], writes=[selx])
            for kt in range(65):
                P.op("pe", lambda e, kt=kt: e.matmul(pS[1][:, kt:kt + 1], selx[0:1, kt * 128:(kt + 1) * 128], one_bf[0:1, 0:1], start=True, stop=True),
                     reads=[selx, one_bf], writes=[pS[1]])
            P.op("dve", lambda e: e.tensor_tensor(out=mcol[:], in0=pS[1][:, 0:65], in1=smk[:, 5:70], op=ALU.mult), reads=[pS[1], smk], writes=[mcol])
            for c0 in range(0, TPK, 2048):
                w = min(2048, TPK - c0)
                load_cast(KSW, KSW[0:64, c0:c0 + w], s_ks[b, :, c0:c0 + w], 64, w)
            for a, na in ((0, 32), (32, 32), (64, 1)):
                load_cast(vs_a, vs_a[:, a:a + na, 0:64], s_vs[b, :, a:a + na, :], 128, (na, 64))
            for kt in range(65):
                P.op("pe", lambda e, kt=kt: e.matmul(pS[0][:, kt * 4:kt * 4 + 4], KSW[0:64, kt * 128:(kt + 1) * 128], sqb[0:64, :], start=True, stop=True),
                     reads=[KSW, sqb], writes=[pS[0]])
            P.op("act", lambda e: e.activation(out=Pss[:], in_=pS[0][:, 0:260], func=AF.Exp, scale=SCALE), reads=[pS[0]], writes=[Pss])
            P.op("dve", lambda e: e.tensor_tensor(out=Pss[:, :].rearrange("p (t g) -> p t g", g=4), in0=Pss[:, :].rearrange("p (t g) -> p t g", g=4),
                                                  in1=bcast_ap(mcol[:, 0:1], [[65, 128], [1, 65], [0, 4]]), op=ALU.mult), reads=[Pss, mcol], writes=[Pss])
            for kt in range(65):
                P.op("pe", lambda e, kt=kt: e.matmul(pOs[0:4, 0:65], Pss[:, kt * 4:kt * 4 + 4], vs_a[:, kt, :], start=(kt == 0), stop=(kt == 64)),
                     reads=[Pss, vs_a], writes=[pOs])
            load_cast(KSW, KSW[64:128, 0:640], s_kw[b], 64, 640, p0=64)
            load_cast(vw_a, vw_a[:, 0:5, 0:64], s_vw[b], 128, (5, 64))
            for kt in range(5):
                P.op("pe", lambda e, kt=kt: e.matmul(pS[1][:, 128 + kt * 4:128 + kt * 4 + 4], KSW[64:128, kt * 128:(kt + 1) * 128], sqb[64:128, :], start=True, stop=True),
                     reads=[KSW, sqb], writes=[pS[1]])
            P.op("act", lambda e: e.activation(out=Pw[:], in_=pS[1][:, 128:148], func=AF.Exp, scale=SCALE), reads=[pS[1]], writes=[Pw])
            P.op("dve", lambda e: e.tensor_tensor(out=Pw[:, :].rearrange("p (t g) -> p t g", g=4), in0=Pw[:, :].rearrange("p (t g) -> p t g", g=4),
                                                  in1=bcast_ap(smk[:, 70:71], [[75, 128], [1, 5], [0, 4]]), op=ALU.mult), reads=[Pw, smk], writes=[Pw])
            for kt in range(5):
                P.op("pe", lambda e, kt=kt: e.matmul(pOw[0:4, 0:65], Pw[:, kt * 4:kt * 4 + 4], vw_a[:, kt, :], start=(kt == 0), stop=(kt == 4)),
                     reads=[Pw, vw_a], writes=[pOw])
            for br, po in enumerate((pOc, pOs, pOw)):
                P.op("dve", lambda e, br=br, po=po: e.reciprocal(out=srs[:, br + 1:br + 2], in_=po[0:4, 64:65]), reads=[po], writes=[srs])
                P.op("dve", lambda e, br=br: e.tensor_tensor(out=scf[:, br:br + 1], in0=srs[:, br + 1:br + 2], in1=sgt[:, br:br + 1], op=ALU.mult),
                     reads=[srs, sgt], writes=[scf])
                if br == 0:
                    P.op("dve", lambda e, po=po: e.tensor_scalar(out=so[:], in0=po[0:4, 0:64], scalar1=scf[:, 0:1], scalar2=None, op0=ALU.mult),
                         reads=[po, scf], writes=[so])
                else:
                    P.op("dve", lambda e, br=br, po=po: e.scalar_tensor_tensor(out=so[:], in0=po[0:4, 0:64], scalar=scf[:, br:br + 1], in1=so[:],
                                                                             op0=ALU.mult, op1=ALU.add), reads=[po, scf, so], writes=[so])
            P.op("act", lambda e: e.copy(out=so2[:], in_=so[:]), reads=[so], writes=[so2])
            P.dma("sp", o_s[b], so2[:], reads=[so2])
    P.emit()
    P.close()
    return nc


def core_consts(c):
    half = c // 4
    kl = np.arange(128)[:, None]; ql = np.arange(128)[None, :]
    tri = (kl <= ql).astype(np.float32); tri2 = (kl >= ql).astype(np.float32)
    one = np.ones((128, 128), np.float32); zero = np.zeros((128, 128), np.float32)
    LM = [tri, zero] if half == 0 else [one, tri]
    WM = [tri2, one, one, one, tri, zero] if half == 0 else [zero, tri2, one, one, one, tri]
    CM = np.zeros((128, NCM, 128), np.float32)
    ADD = np.zeros((128, NJ, 128), np.float32)
    blk = np.arange(128)[None, :]; qq = np.arange(128)[:, None]
    for j in range(NJ):
        t = 2 * j + half
        for nt in range(ntmax(j) + 1):
            CM[:, CM_IDX[(j, nt)], :] = (16 * (128 * nt + kl) + 31 <= 128 * t + ql)
        a = np.zeros((128, 128), np.float32)
        a = np.where(blk > 2 * t + 1, -1.0, a)
        a = np.where(blk == 2 * t + 1, np.where(qq >= 64, 1e9, -1.0), a)
        a = np.where(blk == 2 * t, 1e9, a)
        a = np.where((blk == 2 * t - 1) & (qq < 64), 1e9, a)
        a = np.where(blk == 0, 1e9, a)
        ADD[:, j, :] = a
    neg = lambda m: (-BIG * (1.0 - m)).astype(np.float32)
    LMn = np.stack([neg(m.T) for m in LM], 1)
    WMn = np.stack([neg(m.T) for m in WM], 1)
    CMn = neg(CM.transpose(2, 1, 0))
    return (LMn, WMn, CMn, ADD)


def shared_consts():
    n = np.arange(640)[:, None]; s_ = np.arange(129)[None, :]
    cov = ((16 * n < 64 * s_ + 64) & (16 * n + 32 > 64 * s_)).astype(np.float32)
    cover = cov.reshape(5, 128, 129).transpose(1, 0, 2)
    b = np.arange(128)[:, None, None]; m = np.arange(64)[None, :, None]; k = np.arange(128)[None, None, :]
    t3 = (b == 2 * m + k // 64).astype(np.float32)
    pl = np.arange(128)[:, None]
    msk = np.concatenate([(128 * np.arange(5)[None, :] + pl <= 510), (128 * np.arange(65)[None, :] + pl <= 8192),
                          (128 * np.arange(5)[None, :] + pl <= 512)], axis=1).astype(np.float32)
    sadd = np.zeros((1, 129), np.float32); sadd[0, [0, 127, 128]] = 1e9
    return cover, t3, msk, sadd


def run_C(o, gcmp, gslc, st_win, cmp_w, cmp_pe, kg0, do_prompt=True, do_sample=True):
    nc = build_C(do_prompt, do_sample)
    f = lambda a: np.ascontiguousarray(a, dtype=np.float32)
    cover, t3, msk, sadd = shared_consts()
    ident = np.eye(128, dtype=np.float32)
    pq = o["p_q"].reshape(64, 128, 16, 64); pgt = o["p_gt"].reshape(64, 128, 4, 4, 3)
    pkc = o["p_kvc"].reshape(8192, 2, 4, 64); pks = o["p_kvs"].reshape(8192, 2, 4, 64); pkw = o["p_kvw"].reshape(8192, 2, 4, 64)
    sq = o["s_q"].reshape(32, 16, 64); sgt = o["s_gt"].reshape(32, 4, 4, 3)
    skc = o["s_kvc"].reshape(32, 2, 4, 64); sks = o["s_kvs"].reshape(32, 2, 4, 64); skw = o["s_kvw"].reshape(32, 2, 4, 64)
    stw = st_win.reshape(32, 512, 2, 4, 64)
    maps = []
    for c in range(8):
        h = c % 4; half = c // 4
        tj = 2 * np.arange(NJ) + half
        LM, WM, CM, ADD = core_consts(c)
        m = {"cover": f(cover), "t3": f(t3), "ident": ident, "LM": LM, "WM": WM, "CM": CM, "ADD": ADD, "s_msk": msk, "s_add": sadd,
             "gk0": f(kg0[:, None])}
        q4 = pq[tj][:, :, 4 * h:4 * h + 4]
        qT = q4.transpose(0, 3, 2, 1).reshape(NJ, 64, 512)
        m["qT"] = f(np.concatenate([qT, qT], axis=1))
        m["gts"] = f(pgt[tj][:, :, h].transpose(1, 0, 2, 3).reshape(128, NJ, 12))
        kcv = np.zeros((128, TP), np.float32)
        kcv[0:64, :8192] = pkc[:, 0, h].T; kcv[64:128, :8192] = pkc[:, 1, h].T
        m["kcv"] = kcv
        m["ksw"] = f(np.concatenate([pks[:, 0, h].T, pkw[:, 0, h].T], axis=0))
        m["vs"] = f(pks[:, 1, h].reshape(64, 128, 64).transpose(1, 0, 2))
        m["vw"] = f(pkw[:, 1, h].reshape(64, 128, 64).transpose(1, 0, 2))
        m["wkv"] = f(np.concatenate([cmp_w[0].transpose(1, 0, 2), cmp_w[1].transpose(1, 0, 2)], axis=0))
        m["peT"] = f(np.concatenate([cmp_pe[:, 0, :].T, cmp_pe[:, 1, :].T], axis=0))
        bs = np.arange(16 * half, 16 * half + 16)
        s_kcv = np.zeros((NSEQ, 128, TP), np.float32); s_ks = np.zeros((NSEQ, 64, TPK), np.float32)
        s_vs = np.zeros((NSEQ, TPK, 64), np.float32); s_kw = np.zeros((NSEQ, 64, 640), np.float32); s_vw = np.zeros((NSEQ, 640, 64), np.float32)
        for i, b in enumerate(bs):
            s_kcv[i, 0:64, :8192] = gcmp[b, :, 0, h].T; s_kcv[i, 0:64, 8192] = skc[b, 0, h]
            s_kcv[i, 64:128, :8192] = gcmp[b, :, 1, h].T; s_kcv[i, 64:128, 8192] = skc[b, 1, h]
            s_ks[i, :, :8192] = gslc[b, :, 0, h].T; s_ks[i, :, 8192] = sks[b, 0, h]
            s_vs[i, :8192] = gslc[b, :, 1, h]; s_vs[i, 8192] = sks[b, 1, h]
            s_kw[i, :, :512] = stw[b, :, 0, h].T; s_kw[i, :, 512] = skw[b, 0, h]
            s_vw[i, :512] = stw[b, :, 1, h]; s_vw[i, 512] = skw[b, 1, h]
        m["s_kcv"] = s_kcv; m["s_ks"] = s_ks
        m["s_vs"] = f(s_vs.reshape(NSEQ, 65, 128, 64).transpose(0, 2, 1, 3))
        m["s_kw"] = s_kw; m["s_vw"] = f(s_vw.reshape(NSEQ, 5, 128, 64).transpose(0, 2, 1, 3))
        sqT = sq[bs][:, 4 * h:4 * h + 4].transpose(0, 2, 1)
        m["s_q"] = f(np.concatenate([sqT, sqT], axis=1))
        m["s_gt"] = f(sgt[bs][:, h])
        maps.append(m)
    res = run_bass_kernel_spmd(nc, maps, core_ids=list(range(8)))
    op = np.zeros((64, 128, 16, 64), np.float32); os_ = np.zeros((32, 16, 64), np.float32)
    for c, r in enumerate(res.results):
        h = c % 4; half = c // 4
        tj = 2 * np.arange(NJ) + half
        oc = r["o_p"].reshape(128, NJ, 4, 64).transpose(1, 0, 2, 3)
        op[tj, :, 4 * h:4 * h + 4] = oc
        os_[16 * half:16 * half + 16, 4 * h:4 * h + 4] = r["o_s"]
    return op.reshape(8192, 1024), os_.reshape(32, 1024)


DFF = 5632
NQ = 11
FCH = 44 // NQ


def tiles_to_T(P, src, width, col0, dstT, kc0, gbc, bufs, idb, pst, norm):
    xts, xnb, sq, ssq, rstd = bufs
    nkc = width // 128
    for ti, (r0, rows) in enumerate(TILES):
        xt = xts[ti % 2]; xb = xnb[ti % 2]
        P.dma("sp", xt[:rows, :width], src[r0:r0 + rows, col0:col0 + width], writes=[xt])
        if norm:
            rms_rows(P, xt, rows, width, gbc, xb, sq, ssq, rstd)
        else:
            P.op("dve", lambda e, xt=xt, xb=xb, rows=rows: e.tensor_copy(out=xb[:rows, :width], in_=xt[:rows, :width]),
                 reads=[xt], writes=[xb])
        for b0 in range(0, nkc, 8):
            pt = pst[(b0 // 8) % 2]
            nb = min(8, nkc - b0)
            for j in range(nb):
                kc = b0 + j
                P.op("pe", lambda e, pt=pt, j=j, kc=kc, xb=xb, rows=rows: e.transpose(
                    out=pt[:, j * 128:j * 128 + rows], in_=xb[:rows, kc * 128:(kc + 1) * 128], identity=idb[:rows, :rows]),
                    reads=[xb, idb], writes=[pt])
            src_ap = pt[:, :].rearrange("p (a b) -> p a b", b=128)[:, :nb, :rows]
            dst = dstT[:, kc0 + b0:kc0 + b0 + nb, r0:r0 + rows]
            if (b0 // 8) % 2 == 0:
                P.op("act", lambda e, s=src_ap, d=dst: e.copy(out=d, in_=s), reads=[pt], writes=[dstT])
            else:
                P.op("dve", lambda e, s=src_ap, d=dst: e.tensor_copy(out=d, in_=s), reads=[pt], writes=[dstT])


def build_D():
    nc = bass.Bass("TRN2", target_bir_lowering=False)
    yg = nc.dram_tensor("yg", [NTOK, 1024], F32, kind="ExternalInput").ap()
    oa = nc.dram_tensor("oa", [NTOK, 1024], F32, kind="ExternalInput").ap()
    h = nc.dram_tensor("h", [NTOK, D], F32, kind="ExternalInput").ap()
    gs = nc.dram_tensor("gs", [1024], F32, kind="ExternalInput").ap()
    g2 = nc.dram_tensor("g2", [D], F32, kind="ExternalInput").ap()
    Wo = nc.dram_tensor("Wo", [D, D], F32, kind="ExternalInput").ap()
    Wgu = nc.dram_tensor("Wgu", [D, 2 * DFF], F32, kind="ExternalInput").ap()
    Wd = nc.dram_tensor("Wd", [DFF, D], F32, kind="ExternalInput").ap()
    ident = nc.dram_tensor("ident", [128, 128], F32, kind="ExternalInput").ap()
    hout = nc.dram_tensor("hout", [NTOK, D], F32, kind="ExternalOutput").ap()
    P = Prog(nc)
    gsbc = P.sb([128, 1024], F32, "gsbc")
    g2bc = P.sb([128, D], F32, "g2bc")
    idf = P.sb([128, 128], F32, "idf")
    idb = P.sb([128, 128], BF16, "idb")
    xT = P.sb([128, 16, NTOK], BF16, "xT")
    h1 = P.sb([128, 9, D], F32, "h1")
    xt0 = P.sb([128, D], F32, "xt0"); xts = [xt0, xt0]
    xnb = [P.sb([128, D], BF16, f"xnb{i}") for i in range(2)]
    sq = P.sb([128, D], BF16, "sq")
    ssq = P.sb([128, 1], F32, "ssq")
    rstd = P.sb([128, 1], F32, "rstd")
    bufs = (xts, xnb, sq, ssq, rstd)
    pst = [P.ps([128, 1024], BF16, f"pst{i}") for i in range(2)]
    psm = [P.ps([128, 512], F32, f"psm{i}") for i in range(6)]
    wst = [P.sb([128, 4, 512], F32, f"wst{i}") for i in range(2)]
    wo_bf = P.sb([128, 16, 512], BF16, "wo_bf")
    wg_bf = [P.sb([128, 16, 128], BF16, f"wg_bf{i}") for i in range(2)]
    wv_bf = [P.sb([128, 16, 128], BF16, f"wv_bf{i}") for i in range(2)]
    actT = P.sb([128, FCH, NTOK], BF16, "actT")
    sg = [P.sb([128, 512], F32, f"sg{i}") for i in range(2)]

    P.dma("sp", gsbc[:], gs.partition_broadcast(128), writes=[gsbc])
    P.dma("sp", g2bc[:], g2.partition_broadcast(128), writes=[g2bc])
    P.dma("sp", idf[:], ident[:, :], writes=[idf])
    P.op("dve", lambda e: e.tensor_copy(out=idb[:], in_=idf[:]), reads=[idf], writes=[idb])
    cnt = [0]

    def load_w(dst, src_rows_view, nk, cw):
        kper = 2048 // cw
        for k0 in range(0, nk, kper):
            nkk = min(kper, nk - k0)
            ws = wst[cnt[0] % 2]
            wv_ = ws[:, :, :].rearrange("p a b -> p (a b)")[:, 0:nkk * cw].rearrange("p (k c) -> p k c", c=cw)
            P.dma("sp", wv_, src_rows_view[:, k0:k0 + nkk, :], writes=[ws])
            eng = ("pool", "dve", "pool", "act")[cnt[0] % 4]
            if eng == "act":
                P.op("act", lambda e, wv_=wv_, k0=k0, nkk=nkk: e.copy(out=dst[:, k0:k0 + nkk, :cw], in_=wv_), reads=[ws], writes=[dst])
            else:
                P.op(eng, lambda e, wv_=wv_, k0=k0, nkk=nkk: e.tensor_copy(out=dst[:, k0:k0 + nkk, :cw], in_=wv_), reads=[ws], writes=[dst])
            cnt[0] += 1

    tiles_to_T(P, yg, 1024, 0, xT, 0, gsbc, bufs, idb, pst, True)
    tiles_to_T(P, oa, 1024, 0, xT, 8, None, bufs, idb, pst, False)
    k = 0
    for cg in range(4):
        load_w(wo_bf, Wo[:, cg * 512:(cg + 1) * 512].rearrange("(kc p) c -> p kc c", p=128), 16, 512)
        for ti, (r0, rows) in enumerate(TILES):
            pm = psm[k % 6]; k += 1
            for kc in range(16):
                P.op("pe", lambda e, pm=pm, kc=kc, r0=r0, rows=rows: e.matmul(
                    pm[:rows, :], xT[:, kc, r0:r0 + rows], wo_bf[:, kc, :], start=(kc == 0), stop=(kc == 15)),
                    reads=[xT, wo_bf], writes=[pm])
            xt = xts[k % 2]
            P.dma("sp", xt[:rows, :512], h[r0:r0 + rows, cg * 512:(cg + 1) * 512], writes=[xt])
            P.op("dve", lambda e, pm=pm, xt=xt, ti=ti, rows=rows, cg=cg: e.tensor_tensor(
                out=h1[:rows, ti, cg * 512:(cg + 1) * 512], in0=pm[:rows, :], in1=xt[:rows, :512], op=ALU.add),
                reads=[pm, xt], writes=[h1])
    for ti, (r0, rows) in enumerate(TILES):
        xb = xnb[ti % 2]
        h1t = Buf(h1.t, "h1v")
        P.op("act", lambda e, ti=ti, rows=rows: e.activation(out=sq[:rows, :], in_=h1[:rows, ti, :], func=AF.Square, accum_out=ssq[:rows, :]),
             reads=[h1], writes=[sq, ssq])
        P.op("dve", lambda e, rows=rows: e.tensor_scalar(out=rstd[:rows, :], in0=ssq[:rows, :], scalar1=1.0 / D, scalar2=1e-6, op0=ALU.mult, op1=ALU.add),
             reads=[ssq], writes=[rstd])
        P.op("act", lambda e, rows=rows: e.activation(out=rstd[:rows, :], in_=rstd[:rows, :], func=AF.Sqrt), reads=[rstd], writes=[rstd])
        P.op("dve", lambda e, rows=rows: e.reciprocal(out=rstd[:rows, :], in_=rstd[:rows, :]), reads=[rstd], writes=[rstd])
        P.op("dve", lambda e, ti=ti, rows=rows, xb=xb: e.scalar_tensor_tensor(out=xb[:rows, :], in0=h1[:rows, ti, :], scalar=rstd[:rows, :],
                                                                      in1=g2bc[:rows, :], op0=ALU.mult, op1=ALU.mult),
             reads=[h1, rstd, g2bc], writes=[xb])
        for half in range(2):
            pt = pst[half]
            for j in range(8):
                kc = half * 8 + j
                P.op("pe", lambda e, pt=pt, j=j, kc=kc, xb=xb, rows=rows: e.transpose(
                    out=pt[:, j * 128:j * 128 + rows], in_=xb[:rows, kc * 128:(kc + 1) * 128], identity=idb[:rows, :rows]),
                    reads=[xb, idb], writes=[pt])
            src_ap = pt[:, :].rearrange("p (a b) -> p a b", b=128)[:, :, :rows]
            dst = xT[:, half * 8:half * 8 + 8, r0:r0 + rows]
            if half == 0:
                P.op("act", lambda e, s=src_ap, d=dst: e.copy(out=d, in_=s), reads=[pt], writes=[xT])
            else:
                P.op("dve", lambda e, s=src_ap, d=dst: e.tensor_copy(out=d, in_=s), reads=[pt], writes=[xT])
    TG = [(0, 512), (512, 512), (1024, 4)]
    for qi in range(NQ):
        for fi in range(FCH):
            fc = qi * FCH + fi
            wg = wg_bf[fc % 2]; wv = wv_bf[fc % 2]
            load_w(wg, Wgu[:, fc * 128:(fc + 1) * 128].rearrange("(kc p) c -> p kc c", p=128), 16, 128)
            load_w(wv, Wgu[:, DFF + fc * 128:DFF + (fc + 1) * 128].rearrange("(kc p) c -> p kc c", p=128), 16, 128)
            for gi, (t0, tw) in enumerate(TG):
                pg = psm[(2 * gi) % 6]; pv = psm[(2 * gi + 1) % 6]
                for kc in range(16):
                    P.op("pe", lambda e, pg=pg, kc=kc, wg=wg, t0=t0, tw=tw: e.matmul(
                        pg[:, :tw], wg[:, kc, :], xT[:, kc, t0:t0 + tw], start=(kc == 0), stop=(kc == 15)),
                        reads=[xT, wg], writes=[pg])
                for kc in range(16):
                    P.op("pe", lambda e, pv=pv, kc=kc, wv=wv, t0=t0, tw=tw: e.matmul(
                        pv[:, :tw], wv[:, kc, :], xT[:, kc, t0:t0 + tw], start=(kc == 0), stop=(kc == 15)),
                        reads=[xT, wv], writes=[pv])
                s = sg[gi % 2]
                P.op("act", lambda e, pg=pg, s=s, tw=tw: e.activation(out=s[:, :tw], in_=pg[:, :tw], func=AF.Silu), reads=[pg], writes=[s])
                P.op("dve", lambda e, pv=pv, s=s, fi=fi, t0=t0, tw=tw: e.tensor_tensor(
                    out=actT[:, fi, t0:t0 + tw], in0=s[:, :tw], in1=pv[:, :tw], op=ALU.mult), reads=[s, pv], writes=[actT])
        for cg in range(4):
            load_w(wo_bf, Wd[qi * FCH * 128:(qi + 1) * FCH * 128, cg * 512:(cg + 1) * 512].rearrange("(kc p) c -> p kc c", p=128), FCH, 512)
            for ti, (r0, rows) in enumerate(TILES):
                pm = psm[k % 6]; k += 1
                for fi in range(FCH):
                    P.op("pe", lambda e, pm=pm, fi=fi, r0=r0, rows=rows: e.matmul(
                        pm[:rows, :], actT[:, fi, r0:r0 + rows], wo_bf[:, fi, :], start=(fi == 0), stop=(fi == FCH - 1)),
                        reads=[actT, wo_bf], writes=[pm])
                P.op("dve", lambda e, pm=pm, ti=ti, rows=rows, cg=cg: e.tensor_tensor(
                    out=h1[:rows, ti, cg * 512:(cg + 1) * 512], in0=pm[:rows, :], in1=h1[:rows, ti, cg * 512:(cg + 1) * 512], op=ALU.add),
                    reads=[pm, h1], writes=[h1])
    for ti, (r0, rows) in enumerate(TILES):
        P.dma("sp", hout[r0:r0 + rows, :], h1[:rows, ti, :], reads=[h1])
    P.emit()
    P.close()
    return nc


def run_D(yg_p, yg_s, oa_p, oa_s, h_p, h_s, gs, g2, Wo, Wgu, Wd):
    nc = build_D()
    ident = np.eye(128, dtype=np.float32)
    f = lambda a: np.ascontiguousarray(a, dtype=np.float32)
    maps = []
    for c in range(8):
        cat = lambda p, s: f(np.concatenate([p[c * 1024:(c + 1) * 1024], s[c * 4:(c + 1) * 4]], axis=0))
        maps.append({"yg": cat(yg_p, yg_s), "oa": cat(oa_p, oa_s), "h": cat(h_p, h_s), "gs": f(gs), "g2": f(g2),
                     "Wo": f(Wo), "Wgu": f(Wgu), "Wd": f(Wd), "ident": ident})
    res = run_bass_kernel_spmd(nc, maps, core_ids=list(range(8)))
    hp = np.concatenate([r["hout"][:1024] for r in res.results], axis=0)
    hs = np.concatenate([r["hout"][1024:] for r in res.results], axis=0)
    return hp, hs


def kernel(x_prompt, x_sample, cache_cmp_kv, cache_slc_kv, state_win_kv, state_ssm, state_conv, page_table,
           norm1_g, w_in, conv_w, conv_b, dt_bias, a_log, d_skip, ssm_norm_g, q_norm_g, k_norm_g, cmp_pe, cmp_w,
           w_out, norm2_g, w_gu, w_down):
    A = lambda a: np.asarray(a)
    hp = A(x_prompt)[0].astype(np.float32, copy=False); hs = A(x_sample)[:, 0].astype(np.float32, copy=False)
    pt = A(page_table)
    gath = run_G([A(cache_cmp_kv)[0], A(cache_slc_kv)[0], A(cache_cmp_kv)[1], A(cache_slc_kv)[1]], pt)
    depth = 2
    cmp_p = np.zeros((depth, 1, 8192, 2, 4, 64), np.float32); cmp_s = np.zeros((depth, 32, 1, 2, 4, 64), np.float32)
    slc_p = np.zeros_like(cmp_p); slc_s = np.zeros_like(cmp_s)
    win_p = np.zeros((depth, 1, 512, 2, 4, 64), np.float32); win_s = np.zeros((depth, 32, 512, 2, 4, 64), np.float32)
    ssm_p = np.zeros((depth, 1, 16, 64, 128), np.float32); ssm_s = np.zeros((depth, 32, 16, 64, 128), np.float32)
    conv_p = np.zeros((depth, 1, 3, 2048), np.float32); conv_s = np.zeros((depth, 32, 3, 2048), np.float32)
    for l in range(depth):
        o = run_A(hp, hs, A(norm1_g)[l], A(w_in)[l], A(q_norm_g)[l], A(k_norm_g)[l], A(state_win_kv)[l], A(state_conv)[l])
        ygp, sp, ygs, ss = run_B(o["p_z"], o["p_xbc"], o["p_dt"], o["s_z"], o["s_xbc"], o["s_dt"], A(state_conv)[l], A(state_ssm)[l],
                                 A(conv_w)[l], A(conv_b)[l], A(dt_bias)[l], A(a_log)[l], A(d_skip)[l])
        oap, oas = run_C(o, gath[2 * l], gath[2 * l + 1], A(state_win_kv)[l], A(cmp_w)[l], A(cmp_pe)[l], A(k_norm_g)[l][0])
        cmp_p[l, 0] = o["p_kvc"].reshape(8192, 2, 4, 64); cmp_s[l, :, 0] = o["s_kvc"].reshape(32, 2, 4, 64)
        slc_p[l, 0] = o["p_kvs"].reshape(8192, 2, 4, 64); slc_s[l, :, 0] = o["s_kvs"].reshape(32, 2, 4, 64)
        win_p[l, 0] = o["p_kvw"][-512:].reshape(512, 2, 4, 64)
        win_s[l] = np.concatenate([o["sw_keep"], o["s_kvw"][:, None]], axis=1).reshape(32, 512, 2, 4, 64)
        ssm_p[l, 0] = sp; ssm_s[l] = ss
        conv_p[l, 0] = o["p_xbc"][-3:]
        conv_s[l] = np.concatenate([o["sc_keep"], o["s_xbc"][:, None]], axis=1)
        hp, hs = run_D(ygp, ygs, oap, oas, hp, hs, A(ssm_norm_g)[l], A(norm2_g)[l], A(w_out)[l], A(w_gu)[l], A(w_down)[l])
    return (hp[None].astype(np.float32), hs[:, None].astype(np.float32), cmp_p, cmp_s, slc_p, slc_s, win_p, win_s, ssm_p, ssm_s, conv_p, conv_s)
```
